# Optimizing a Trainium2 kernel written in Bass

```python
import math
import jax, jax.numpy as jnp
from jax import lax
import numpy as np

D_MODEL = 2048
BATCH = 2
SEQ = 8192
DEPTH = 1

EPS = 1e-6
D_PLE = 256
RET_WIDTH = D_MODEL // 2
RET_HEAD_DIM = 128
RET_HEADS = RET_WIDTH // RET_HEAD_DIM
RET_CHUNK = 128
ROPE_BASE = 10000.0
SSD_WIDTH = D_MODEL - RET_WIDTH
SSD_HEAD_DIM = 64
SSD_HEADS = SSD_WIDTH // SSD_HEAD_DIM
SSD_GROUPS = 2
SSD_HEADS_PER_GROUP = SSD_HEADS // SSD_GROUPS
SSD_STATE = 128
SSD_CONV = 5
SSD_CHUNK = 128
SSD_CONV_DIM = SSD_WIDTH + 2 * SSD_GROUPS * SSD_STATE
D_MIX = RET_WIDTH + SSD_WIDTH
D_FF = (11 * D_MODEL) // 4
FFN_CONV = 3
OFF_Q = 0
OFF_K = OFF_Q + RET_WIDTH
OFF_V = OFF_K + RET_WIDTH
OFF_G = OFF_V + RET_WIDTH
OFF_Z = OFF_G + RET_WIDTH
OFF_XBC = OFF_Z + SSD_WIDTH
OFF_DT = OFF_XBC + SSD_CONV_DIM
N_IN = OFF_DT + 2 * SSD_HEADS

kernel_name = 'hybrid_retention_ssd_encoder_layer'


def rmsnorm(t, w):
    tf = t.astype(jnp.float32)
    y = tf * lax.rsqrt(jnp.mean(tf * tf, axis=-1, keepdims=True) + EPS)
    return (y * w.astype(jnp.float32)).astype(t.dtype)


def rotary(t, positions):
    half = t.shape[-1] // 2
    inv_freq = ROPE_BASE ** (-jnp.arange(half, dtype=jnp.float32) / half)
    ang = positions.astype(jnp.float32)[..., None] * inv_freq
    cos = jnp.cos(ang)[:, :, None, :].astype(t.dtype)
    sin = jnp.sin(ang)[:, :, None, :].astype(t.dtype)
    t1, t2 = t[..., :half], t[..., half:]
    return jnp.concatenate([t1 * cos - t2 * sin, t1 * sin + t2 * cos], axis=-1)


def depthwise_conv(t, w, b):
    pad = w.shape[0] // 2
    y = lax.conv_general_dilated(t, w[:, None, :].astype(t.dtype), window_strides=(1,),
                                 padding=[(pad, pad)], dimension_numbers=('NWC', 'WIO', 'NWC'),
                                 feature_group_count=t.shape[-1])
    return y + b.astype(t.dtype)


def bidir_retention(q, k, v):
    b, L, H, dk = q.shape
    dv = v.shape[-1]
    C = RET_CHUNK
    n = L // C
    dt = q.dtype
    hh = jnp.arange(H, dtype=jnp.float32)
    lf = jnp.log1p(-jnp.exp2(-5.0 - hh))
    lb = jnp.log1p(-jnp.exp2(-5.5 - hh))
    idx = jnp.arange(C, dtype=jnp.float32)
    dist = idx[:, None] - idx[None, :]
    adist = jnp.abs(dist)
    mask = jnp.where(dist >= 0, jnp.exp(lf[:, None, None] * adist),
                     jnp.exp(lb[:, None, None] * adist)).astype(dt)
    q = q.reshape(b, n, C, H, dk)
    k = k.reshape(b, n, C, H, dk)
    v = v.reshape(b, n, C, H, dv)
    s = jnp.einsum('bnihd,bnjhd->bnhij', q, k) * mask
    intra = jnp.einsum('bnhij,bnjhv->bnihv', s, v)
    k_f = jnp.exp(lf[None, :] * (C - 1.0 - idx)[:, None]).astype(dt)
    k_b = jnp.exp(lb[None, :] * idx[:, None]).astype(dt)
    q_f = jnp.exp(lf[None, :] * (idx + 1.0)[:, None]).astype(dt)
    q_b = jnp.exp(lb[None, :] * (C - idx)[:, None]).astype(dt)
    dec_f = jnp.exp(lf * C).astype(dt)[:, None, None]
    dec_b = jnp.exp(lb * C).astype(dt)[:, None, None]
    kv_f = jnp.einsum('bnjhd,jh,bnjhv->nbhdv', k, k_f, v)
    kv_b = jnp.einsum('bnjhd,jh,bnjhv->nbhdv', k, k_b, v)

    def step_f(r, kv):
        return r * dec_f + kv, r

    def step_b(r, kv):
        return r * dec_b + kv, r

    r0 = jnp.zeros((b, H, dk, dv), dt)
    _, r_f = lax.scan(step_f, r0, kv_f)
    _, r_b = lax.scan(step_b, r0, kv_b, reverse=True)
    inter = (jnp.einsum('bnihd,ih,nbhdv->bnihv', q, q_f, r_f)
             + jnp.einsum('bnihd,ih,nbhdv->bnihv', q, q_b, r_b))
    return (intra + inter).reshape(b, L, H, dv)


def retention_group(proj, positions, norm_w):
    b, L, _ = proj.shape
    shp = (b, L, RET_HEADS, RET_HEAD_DIM)
    q = rotary(proj[..., OFF_Q:OFF_K].reshape(shp), positions)
    k = rotary(proj[..., OFF_K:OFF_V].reshape(shp), positions) * (RET_HEAD_DIM ** -0.5)
    v = proj[..., OFF_V:OFF_G].reshape(shp)
    g = proj[..., OFF_G:OFF_Z]
    o = bidir_retention(q, k, v)
    o = rmsnorm(o, norm_w.reshape(RET_HEADS, RET_HEAD_DIM)).reshape(b, L, RET_WIDTH)
    return jax.nn.silu(g) * o


def ssd_scan(x, dt, a, bm, cm):
    b, L, G, E, P = x.shape
    N = bm.shape[-1]
    Q = SSD_CHUNK
    c = L // Q
    x = x.reshape(b, c, Q, G, E, P)
    dt = dt.reshape(b, c, Q, G, E)
    bm = bm.reshape(b, c, Q, G, N)
    cm = cm.reshape(b, c, Q, G, N)
    xdt = x * dt[..., None]
    a_cs = jnp.cumsum(jnp.moveaxis(dt.astype(jnp.float32) * a, 2, -1), axis=-1)
    seg = a_cs[..., :, None] - a_cs[..., None, :]
    tril = jnp.tril(jnp.ones((Q, Q), dtype=bool))
    lmat = jnp.exp(jnp.where(tril, seg, -jnp.inf)).astype(x.dtype)
    cb = jnp.einsum('bclgn,bcsgn->bcgls', cm, bm)
    y_diag = jnp.einsum('bcgls,bcgels,bcsgep->bclgep', cb, lmat, xdt)
    decay_states = jnp.exp(a_cs[..., -1:] - a_cs).astype(x.dtype)
    states = jnp.einsum('bcsgn,bcges,bcsgep->cbgepn', bm, decay_states, xdt)
    chunk_decay = jnp.moveaxis(jnp.exp(a_cs[..., -1]).astype(x.dtype), 1, 0)[..., None, None]

    def step(h, inp):
        st, dec = inp
        return h * dec + st, h

    _, h_in = lax.scan(step, jnp.zeros_like(states[0]), (states, chunk_decay))
    y_off = jnp.einsum('bclgn,cbgepn,bcgel->bclgep', cm, h_in, jnp.exp(a_cs).astype(x.dtype))
    return (y_diag + y_off).astype(x.dtype).reshape(b, L, G, E, P)


def ssd_group(proj, conv_w, conv_b, dt_bias, a_log, d_skip, norm_w):
    b, L, _ = proj.shape
    G, E, P, N = SSD_GROUPS, SSD_HEADS_PER_GROUP, SSD_HEAD_DIM, SSD_STATE
    z = proj[..., OFF_Z:OFF_XBC]
    xbc = jax.nn.silu(depthwise_conv(proj[..., OFF_XBC:OFF_DT], conv_w, conv_b))
    xs = xbc[..., :SSD_WIDTH].reshape(b, L, G, E, P)
    bm = xbc[..., SSD_WIDTH:SSD_WIDTH + G * N].reshape(b, L, G, N)
    cm = xbc[..., SSD_WIDTH + G * N:].reshape(b, L, G, N)
    dt_raw = proj[..., OFF_DT:N_IN]
    dt_f = jax.nn.softplus(dt_raw[..., :SSD_HEADS] + dt_bias[0]).reshape(b, L, G, E)
    dt_b = jax.nn.softplus(dt_raw[..., SSD_HEADS:] + dt_bias[1]).reshape(b, L, G, E)
    a_f = -jnp.exp(a_log[0].astype(jnp.float32)).reshape(G, E)
    a_b = -jnp.exp(a_log[1].astype(jnp.float32)).reshape(G, E)
    y_f = ssd_scan(xs, dt_f, a_f, bm, cm)
    y_b = jnp.flip(ssd_scan(jnp.flip(xs, 1), jnp.flip(dt_b, 1), a_b,
                            jnp.flip(bm, 1), jnp.flip(cm, 1)), 1)
    y_self = jnp.einsum('blgn,blgn->blg', cm, bm)[..., None, None] * dt_b[..., None] * xs
    y = y_f + y_b - y_self + d_skip.reshape(G, E)[..., None] * xs
    y = (y.reshape(b, L, SSD_WIDTH) * jax.nn.silu(z)).reshape(b, L, G, SSD_WIDTH // G)
    return rmsnorm(y, norm_w.reshape(G, SSD_WIDTH // G)).reshape(b, L, SSD_WIDTH)


def conv_glu_ffn(h, w_gate, w_up, conv_w, conv_b, w_down):
    gate = depthwise_conv(h @ w_gate, conv_w, conv_b)
    return (jax.nn.gelu(gate, approximate=True) * (h @ w_up)) @ w_down


def setup_inputs(seed: int = 0) -> dict:
    key = jax.random.key(seed)
    ks = iter(jax.random.split(key, 32))

    def nrm(shape, scale):
        return jax.random.normal(next(ks), shape, jnp.float32) * scale

    def gain(shape):
        return 1.0 + nrm(shape, 0.02)

    x = nrm((BATCH, SEQ, D_MODEL), 1.0)
    p = nrm((DEPTH, BATCH, SEQ, D_PLE), 1.0)
    positions = (jnp.arange(SEQ, dtype=jnp.int32)[None, :]
                 + jax.random.randint(next(ks), (BATCH, 1), 0, 1024, dtype=jnp.int32))
    u = jax.random.uniform(next(ks), (DEPTH, 2, SSD_HEADS), jnp.float32)
    dt0 = jnp.exp(u * (math.log(0.1) - math.log(0.001)) + math.log(0.001))
    ssd_dt_bias = dt0 + jnp.log(-jnp.expm1(-dt0))
    ssd_a_log = jnp.log(jax.random.uniform(next(ks), (DEPTH, 2, SSD_HEADS), jnp.float32, 1.0, 16.0))
    return {
        'x': x,
        'p': p,
        'positions': positions,
        'norm_mix_w': gain((DEPTH, D_MODEL)),
        'w_in': nrm((DEPTH, D_MODEL, N_IN), D_MODEL ** -0.5),
        'ret_norm_w': gain((DEPTH, RET_WIDTH)),
        'ssd_conv_w': nrm((DEPTH, SSD_CONV, SSD_CONV_DIM), SSD_CONV ** -0.5),
        'ssd_conv_b': nrm((DEPTH, SSD_CONV_DIM), 0.02),
        'ssd_dt_bias': ssd_dt_bias,
        'ssd_a_log': ssd_a_log,
        'ssd_d': 1.0 + nrm((DEPTH, SSD_HEADS), 0.1),
        'ssd_norm_w': gain((DEPTH, SSD_WIDTH)),
        'w_out': nrm((DEPTH, D_MIX, D_MODEL), D_MIX ** -0.5),
        'norm_ffn_w': gain((DEPTH, D_MODEL)),
        'ffn_w_gate': nrm((DEPTH, D_MODEL, D_FF), D_MODEL ** -0.5),
        'ffn_w_up': nrm((DEPTH, D_MODEL, D_FF), D_MODEL ** -0.5),
        'ffn_conv_w': nrm((DEPTH, FFN_CONV, D_FF), FFN_CONV ** -0.5),
        'ffn_conv_b': nrm((DEPTH, D_FF), 0.02),
        'ffn_w_down': nrm((DEPTH, D_FF, D_MODEL), D_FF ** -0.5),
        'ple_norm_w': gain((DEPTH, D_MODEL)),
        'ple_w_gate': nrm((DEPTH, D_MODEL, D_MODEL), D_MODEL ** -0.5),
        'ple_b_gate': nrm((DEPTH, D_MODEL), 0.02),
        'ple_w_proj': nrm((DEPTH, D_PLE, D_MODEL), D_PLE ** -0.5),
        'final_norm_w': gain((D_MODEL,)),
    }


def reference(x, p, positions, norm_mix_w, w_in, ret_norm_w, ssd_conv_w, ssd_conv_b,
              ssd_dt_bias, ssd_a_log, ssd_d, ssd_norm_w, w_out, norm_ffn_w, ffn_w_gate,
              ffn_w_up, ffn_conv_w, ffn_conv_b, ffn_w_down, ple_norm_w, ple_w_gate,
              ple_b_gate, ple_w_proj, final_norm_w):
    h = x
    for i in range(DEPTH):
        hn = rmsnorm(h, norm_mix_w[i])
        proj = hn @ w_in[i]
        ret = retention_group(proj, positions, ret_norm_w[i])
        ssd = ssd_group(proj, ssd_conv_w[i], ssd_conv_b[i], ssd_dt_bias[i], ssd_a_log[i],
                        ssd_d[i], ssd_norm_w[i])
        h = h + jnp.concatenate([ret, ssd], axis=-1) @ w_out[i]
        hn = rmsnorm(h, norm_ffn_w[i])
        h = h + conv_glu_ffn(hn, ffn_w_gate[i], ffn_w_up[i], ffn_conv_w[i], ffn_conv_b[i], ffn_w_down[i])
        gate = jax.nn.sigmoid(rmsnorm(h, ple_norm_w[i]) @ ple_w_gate[i] + ple_b_gate[i])
        h = h + gate * (p[i] @ ple_w_proj[i])
    return rmsnorm(h, final_norm_w)
```

```python
import contextlib
import math
import numpy as np
import ml_dtypes
import concourse.bass as bass
import concourse.mybir as mybir
from concourse.bass_utils import run_bass_kernel_spmd

F32 = mybir.dt.float32
BF16 = mybir.dt.bfloat16
I32 = mybir.dt.int32
AF = mybir.ActivationFunctionType
ALU = mybir.AluOpType

D = 2048
L = 8192
NT = 16
TW = 516
EPS = 1e-6
DFF = 5632
NFB = DFF // 128
SEMCH = 20000
NSEM = 100


import os
STOP = float(os.environ.get('KSTOP', '99'))


class StopBuild(Exception):
    pass


def chk(level):
    if STOP <= level:
        raise StopBuild()


ALL_BUFS = []


class Buf:
    def __init__(self, name, psum=False):
        ALL_BUFS.append(self)
        self.name = name
        self.psum = psum
        self.dtok = None
        self.w = None
        self.r = []
        self.dsem = None
        self.dcnt = 0


class Eng:
    def __init__(self, ctx, name, e):
        self.ctx = ctx
        self.name = name
        self.e = e
        self.sems = []
        self.n = 0
        self.seen = {}
        self.last = None

    def _sem(self, idx):
        while len(self.sems) <= idx:
            self.sems.append(self.ctx.new_sem(f"{self.name}_p{len(self.sems)}"))
        return self.sems[idx]

    def mark(self, ins):
        k, v = divmod(self.n, SEMCH)
        ins.then_inc(self._sem(k), 1)
        self.n += 1
        self.last = ("E", self, k, v + 1)
        return self.last

    def wait(self, tok):
        if tok is None:
            return
        if tok[0] == "E":
            _, prod, k, v = tok
            if prod is self and (not self.ctx.same_sync or self.name == "pe"):
                return
            key = ("E", prod.name)
            if self.seen.get(key, (-1, 0)) >= (k, v):
                return
            self.e.wait_ge(prod.sems[k], v)
            self.seen[key] = (k, v)
        else:
            _, sem, name, v = tok
            key = ("D", name)
            if self.seen.get(key, 0) >= v:
                return
            self.e.wait_ge(sem, v)
            self.seen[key] = v


class Ctx:
    def __init__(self, nc, es):
        self.nc = nc
        self.es = es
        self.same_sync = not os.environ.get('KNOSAME')
        self.nsem = 0
        self.PE = Eng(self, "pe", nc.tensor)
        self.ACT = Eng(self, "act", nc.scalar)
        self.DVE = Eng(self, "dve", nc.vector)
        self.POOL = Eng(self, "pool", nc.gpsimd)
        self.SP = Eng(self, "sp", nc.sync)
        self.out_toks = []
        self.sem_pool = [es.enter_context(nc.semaphore(f"sem{i}")) for i in range(NSEM)]

    def new_sem(self, name):
        self.nsem += 1
        return self.sem_pool[self.nsem - 1]

    def sb(self, name, shape, dt):
        return self.es.enter_context(self.nc.sbuf_tensor("s_" + name, list(shape), dt))

    def ps(self, name, shape, dt):
        return self.es.enter_context(self.nc.psum_tensor("p_" + name, list(shape), dt))

    def op(self, E, reads, writes, fn, mark=True, wait=True):
        if wait:
            for b in reads:
                E.wait(b.w)
                if b.psum and not os.environ.get('KNOPS'):
                    for t in b.r:
                        E.wait(t)
            for b in writes:
                E.wait(b.w)
                for t in b.r:
                    E.wait(t)
        ins = fn(E.e)
        if mark:
            tok = E.mark(ins)
            for b in reads:
                b.r.append(tok)
                if len(b.r) > 24:
                    b.r = b.r[-24:]
            for b in writes:
                b.w = tok
                b.r = []
        return ins

    def fence(self, engines=None):
        allE = (self.PE, self.ACT, self.DVE, self.POOL, self.SP)
        lasts = [E.last for E in allE]
        for E in (engines or allE):
            for t in lasts:
                if t is not None and t[1] is not E:
                    E.wait(t)
            for b in ALL_BUFS:
                E.wait(b.dtok)

    def dma(self, Q, dst, src, out_ap, in_ap, is_out=False, slow=False):
        Q.wait(src.w)
        Q.wait(dst.w)
        for t in dst.r:
            Q.wait(t)
        if dst.dsem is None:
            dst.dsem = self.new_sem("d_" + dst.name)
        dst.dcnt += 1
        if slow:
            Q.e.dma_start(out=out_ap, in_=in_ap, allow_slow_non_contiguous=True).then_inc(dst.dsem, 16)
        else:
            Q.e.dma_start(out=out_ap, in_=in_ap).then_inc(dst.dsem, 16)
        tok = ("D", dst.dsem, dst.name, 16 * dst.dcnt)
        dst.w = tok
        dst.dtok = tok
        dst.r = []
        src.r.append(tok)
        if len(src.r) > 24:
            src.r = src.r[-24:]
        if is_out:
            self.out_toks.append(tok)
        return tok


def bc(ap, shape):
    return ap.to_broadcast(list(shape))


def build_nc(debug_half1=False, ntiles=NT, run_half2=True, run_half1=True):
    del ALL_BUFS[:]
    nc = bass.Bass("TRN2", target_bir_lowering=False)
    es = contextlib.ExitStack()
    with es:
        C = Ctx(nc, es)
        _build(nc, C, debug_half1, ntiles, run_half2, run_half1)
    return nc


def _build(nc, C, debug_half1, ntiles, run_half2, run_half1=True):
    PE, ACT, DVE, POOL, SP = C.PE, C.ACT, C.DVE, C.POOL, C.SP
    op, dma, sb, ps = C.op, C.dma, C.sb, C.ps

    def din(name, shape, dt=F32):
        return nc.dram_tensor(name, list(shape), dt, kind="ExternalInput").ap()

    xTp = din("xTp", [128, 16, L + 4])
    pos_d = din("pos", [1, L], I32)
    wfm_d = din("wfm", [128, 16, 1024])
    wtm_d = din("wtm", [128, 16, 776])
    nmw_d = din("nmw", [128, 16])
    cw_d = din("cw", [128, 4, 5])
    cb_d = din("cb", [128, 4])
    dtb_d = din("dtb", [1, 8])
    alog_d = din("alog", [1, 8])
    dsk_d = din("dsk", [1, 4])
    rnw_d = din("rnw", [1, 256])
    hh_d = din("hh", [1, 2])
    NC2 = 2050
    if run_half2:
        xw_d = din("xw", [128, 16, NC2])
        pT_d = din("pT", [128, 2, 2048])
        idx_d = din("gidx", [128, 16], I32)
        wout_d = din("wout", [128, 16, 2048])
        snw_d = din("snw", [128, 16])
        nfw_d = din("nfw", [128, 16])
        wg_d = din("wg", [128, NFB, 2048])
        wu_d = din("wu", [128, NFB, 2048])
        fcw_d = din("fcw", [128, NFB, 3])
        fcb_d = din("fcb", [128, NFB])
        wd_d = din("wd", [128, 16, NFB * 128])
        pnw_d = din("pnw", [128, 16])
        wpg_d = din("wpg", [128, 16, 2048])
        bpg_d = din("bpg", [128, 16])
        wpp_d = din("wpp", [128, 16, 256])
        fnw_d = din("fnw", [128, 16])
        outT = nc.dram_tensor("outT", [128, 16, 2048], F32, kind="ExternalOutput").ap()
    if debug_half1:
        mix_dbg = nc.dram_tensor("mix_dbg", [2048, 2050], BF16, kind="ExternalOutput").ap()

    agin = nc.dram_tensor("agin", [2048, NC2], BF16)
    if run_half1:
        agout = nc.dram_tensor("agout", [8192, NC2], BF16)
    else:
        agout = nc.dram_tensor("agout", [8192, NC2], BF16, kind="ExternalInput")
    rb_scr = nc.dram_tensor("rb_scr", [64, 128, 256], BF16)
    hb_scr = nc.dram_tensor("hb_scr", [64, 128, 256], BF16)
    B_agin = Buf("agin")
    B_agout = Buf("agout")
    B_in = Buf("ext_in")
    B_rbs = [Buf("rbs")] * 64
    B_hbs = [Buf("hbs")] * 64

    h1 = contextlib.ExitStack()
    C.es_outer = C.es
    C.es = h1
    with h1:
      if run_half1:
        wfm = sb("wfm_s", [128, 16, 1024], BF16); B_wfm = Buf("wfm")
        wtm = sb("wtm_s", [128, 16, 776], BF16); B_wtm = Buf("wtm")
        for k in range(0, 16, 4):
            dma(POOL, B_wfm, B_in, wfm[:, k:k + 4, :], wfm_d[:, k:k + 4, :])
            dma(POOL, B_wtm, B_in, wtm[:, k:k + 4, :], wtm_d[:, k:k + 4, :])
        B_c = Buf("consts")
        nmw = sb("nmw", [128, 16], F32); cw = sb("cw", [128, 4, 5], F32); cb = sb("cb", [128, 4], F32)
        dtb = sb("dtb", [128, 8], F32); alog = sb("alog", [128, 8], F32); dsk = sb("dsk", [128, 4], F32)
        rnw = sb("rnw", [128, 256], F32); hh = sb("hh", [128, 2], F32)
        dma(SP, B_c, B_in, nmw[:], nmw_d[:, :])
        dma(SP, B_c, B_in, cw[:], cw_d[:, :, :])
        dma(SP, B_c, B_in, cb[:], cb_d[:, :])
        dma(SP, B_c, B_in, dtb[:], dtb_d[0:1, :].partition_broadcast(128))
        dma(SP, B_c, B_in, alog[:], alog_d[0:1, :].partition_broadcast(128))
        dma(SP, B_c, B_in, dsk[:], dsk_d[0:1, :].partition_broadcast(128))
        dma(SP, B_c, B_in, rnw[:], rnw_d[0:1, :].partition_broadcast(128))
        dma(SP, B_c, B_in, hh[:], hh_d[0:1, :].partition_broadcast(128))

        B_k = Buf("kconst")
        dmat = sb("dmat", [128, 128], F32)
        TI = sb("TI", [128, 128], F32); TS = sb("TS", [128, 128], F32)
        TIs = sb("TIs", [128, 128], F32); TSs = sb("TSs", [128, 128], F32)
        onesf = sb("onesf", [128, 128], F32); onesb = sb("onesb", [128, 128], BF16)
        identf = sb("identf", [128, 128], F32); ident = sb("ident", [128, 128], BF16)
        permf = sb("permf", [128, 128], F32); perm = sb("perm", [128, 128], BF16)
        pcol = sb("pcol", [128, 1], F32); irow = sb("irow", [128, 128], F32)
        ifr = sb("ifr", [128, 1], F32); sgn = sb("sgn", [128, 1], F32)
        tmpc = sb("tmpc", [128, 128], F32); tmpc2 = sb("tmpc2", [128, 128], F32)
        g = POOL
        op(g, [], [B_k], lambda e: e.iota(dmat[:], pattern=[[1, 128]], base=0, channel_multiplier=-1,
                                          allow_small_or_imprecise_dtypes=True))
        op(g, [], [B_k], lambda e: e.tensor_single_scalar(out=TI[:], in_=dmat[:], scalar=0.0, op=ALU.is_ge))
        op(g, [], [B_k], lambda e: e.tensor_single_scalar(out=TS[:], in_=dmat[:], scalar=0.0, op=ALU.is_le))
        op(g, [], [B_k], lambda e: e.tensor_single_scalar(out=TIs[:], in_=dmat[:], scalar=0.0, op=ALU.is_gt))
        op(g, [], [B_k], lambda e: e.tensor_single_scalar(out=TSs[:], in_=dmat[:], scalar=0.0, op=ALU.is_lt))
        op(g, [], [B_k], lambda e: e.tensor_single_scalar(out=identf[:], in_=dmat[:], scalar=0.0, op=ALU.is_equal))
        op(g, [], [B_k], lambda e: e.tensor_copy(out=ident[:], in_=identf[:]))
        op(g, [], [B_k], lambda e: e.memset(onesf[:], 1.0))
        op(g, [], [B_k], lambda e: e.memset(onesb[:], 1.0))
        op(g, [], [B_k], lambda e: e.tensor_scalar(out=tmpc[:], in0=dmat[:], scalar1=64.0, scalar2=0.0,
                                                   op0=ALU.add, op1=ALU.is_equal))
        op(g, [], [B_k], lambda e: e.tensor_scalar(out=tmpc2[:], in0=dmat[:], scalar1=-64.0, scalar2=0.0,
                                                   op0=ALU.add, op1=ALU.is_equal))
        op(g, [], [B_k], lambda e: e.tensor_add(out=permf[:], in0=tmpc[:], in1=tmpc2[:]))
        op(g, [], [B_k], lambda e: e.tensor_copy(out=perm[:], in_=permf[:]))
        op(g, [], [B_k], lambda e: e.iota(pcol[:], pattern=[[0, 1]], base=0, channel_multiplier=1,
                                          allow_small_or_imprecise_dtypes=True))
        op(g, [], [B_k], lambda e: e.iota(irow[:], pattern=[[1, 128]], base=0, channel_multiplier=0,
                                          allow_small_or_imprecise_dtypes=True))
        pm = sb("pm", [128, 1], F32)
        op(DVE, [B_k], [B_k], lambda e: e.tensor_scalar(out=pm[:], in0=pcol[:], scalar1=64.0, scalar2=-64.0, op0=ALU.is_ge, op1=ALU.mult))
        op(DVE, [B_k], [B_k], lambda e: e.tensor_add(out=pm[:], in0=pm[:], in1=pcol[:]))
        op(DVE, [B_k], [B_k], lambda e: e.tensor_scalar(out=sgn[:], in0=pcol[:], scalar1=64.0, scalar2=2.0,
                                                   op0=ALU.is_ge, op1=ALU.mult))
        op(DVE, [B_k], [B_k], lambda e: e.tensor_scalar_add(out=sgn[:], in0=sgn[:], scalar1=-1.0))
        op(ACT, [B_k], [B_k], lambda e: e.activation(out=ifr[:], in_=pm[:], func=AF.Exp,
                                                     scale=-math.log(10000.0) / 64.0))
        aneg = sb("aneg", [128, 8], F32)
        op(ACT, [B_c], [B_k], lambda e: e.activation(out=aneg[:], in_=alog[:], func=AF.Exp))
        op(ACT, [B_k], [B_k], lambda e: e.mul(out=aneg[:], in_=aneg[:], mul=-1.0))
        lf = sb("lf", [128, 2], F32); lb = sb("lb", [128, 2], F32)
        LN2 = math.log(2.0)
        for (dst, off) in ((lf, 5.0), (lb, 5.5)):
            op(ACT, [B_c, B_k], [B_k], lambda e, dst=dst, off=off: e.activation(
                out=dst[:], in_=hh[:], func=AF.Exp, scale=-LN2, bias=-LN2 * off))
            op(ACT, [B_k], [B_k], lambda e, dst=dst: e.activation(
                out=dst[:], in_=dst[:], func=AF.Ln, scale=-1.0, bias=1.0))
        maskT = sb("maskT", [128, 2, 128], F32)
        kfc = sb("kfc", [128, 2], F32); kbc = sb("kbc", [128, 2], F32)
        qfr = sb("qfr", [128, 2, 128], F32); qbr = sb("qbr", [128, 2, 128], F32)
        decf = sb("decf", [128, 2], F32); decb = sb("decb", [128, 2], F32)
        posd = sb("posd", [128, 128], F32); negd = sb("negd", [128, 128], F32)
        op(DVE, [B_k], [B_k], lambda e: e.tensor_scalar_max(out=posd[:], in0=dmat[:], scalar1=0.0))
        op(DVE, [B_k], [B_k], lambda e: e.tensor_sub(out=negd[:], in0=posd[:], in1=dmat[:]))
        jr = sb("jr", [128, 1], F32)
        op(DVE, [B_k], [B_k], lambda e: e.tensor_scalar(out=jr[:], in0=pcol[:], scalar1=-1.0, scalar2=127.0,
                                                        op0=ALU.mult, op1=ALU.add))
        ip1 = sb("ip1", [128, 128], F32); cmi = sb("cmi", [128, 128], F32)
        op(DVE, [B_k], [B_k], lambda e: e.tensor_scalar_add(out=ip1[:], in0=irow[:], scalar1=1.0))
        op(DVE, [B_k], [B_k], lambda e: e.tensor_scalar(out=cmi[:], in0=irow[:], scalar1=-1.0, scalar2=128.0,
                                                        op0=ALU.mult, op1=ALU.add))
        for h in range(2):
            op(DVE, [B_k], [B_k], lambda e, h=h: e.tensor_scalar_mul(out=tmpc[:], in0=posd[:], scalar1=lf[:, h:h + 1]))
            op(DVE, [B_k], [B_k], lambda e, h=h: e.scalar_tensor_tensor(
                out=tmpc[:], in0=negd[:], scalar=lb[:, h:h + 1], in1=tmpc[:], op0=ALU.mult, op1=ALU.add))
            op(ACT, [B_k], [B_k], lambda e, h=h: e.activation(out=maskT[:, h, :], in_=tmpc[:], func=AF.Exp))
            op(ACT, [B_k], [B_k], lambda e, h=h: e.activation(out=kfc[:, h:h + 1], in_=jr[:], func=AF.Exp,
                                                               scale=lf[:, h:h + 1]))
            op(ACT, [B_k], [B_k], lambda e, h=h: e.activation(out=kbc[:, h:h + 1], in_=pcol[:], func=AF.Exp,
                                                               scale=lb[:, h:h + 1]))
            op(ACT, [B_k], [B_k], lambda e, h=h: e.activation(out=qfr[:, h, :], in_=ip1[:], func=AF.Exp,
                                                               scale=lf[:, h:h + 1]))
            op(ACT, [B_k], [B_k], lambda e, h=h: e.activation(out=qbr[:, h, :], in_=cmi[:], func=AF.Exp,
                                                               scale=lb[:, h:h + 1]))
        op(ACT, [B_k], [B_k], lambda e: e.activation(out=decf[:], in_=lf[:], func=AF.Exp, scale=128.0))
        op(ACT, [B_k], [B_k], lambda e: e.activation(out=decb[:], in_=lb[:], func=AF.Exp, scale=128.0))

        epsc = sb("epsc", [128, 1], F32)
        op(POOL, [], [B_k], lambda e: e.memset(epsc[:], EPS))
        xs = sb("xs", [128, 16, TW], F32); B_xs = Buf("xs")
        sq = sb("sq", [128, 16, TW], BF16); B_sq = Buf("sq")
        hn = sq; B_hn = B_sq
        rstd = sb("rstd", [128, TW], F32); B_rstd = Buf("rstd")
        posi = sb("posi", [128, 512], I32); B_posi = Buf("posi")
        ang = sb("ang", [128, 512], F32); ang2 = sb("ang2", [128, 512], F32)
        cosT = sb("cosT", [128, 512], F32); sinT = sb("sinT", [128, 512], F32); B_cs = Buf("cossin")
        rawb = sb("rawb", [128, 512], BF16); B_rawb = Buf("rawb")
        rt1 = sb("rt1", [128, 512], F32); B_rt1 = Buf("rt1")
        qT = sb("qT", [128, 2, 512], BF16); kT = sb("kT", [128, 2, 512], BF16)
        qfT = sb("qfT", [128, 2, 512], BF16); qbT = sb("qbT", [128, 2, 512], BF16)
        B_qT = Buf("qT"); B_kT = Buf("kT"); B_qfb = Buf("qfb")
        rawc = sb("rawc", [128, TW], F32); B_rawc = Buf("rawc")
        cacc = sb("cacc", [128, 512], F32); B_cacc = Buf("cacc")
        xbcT = sb("xbcT", [128, 4, 512], BF16); B_xbcT = [Buf(f"xbcT{i}") for i in range(4)]
        vtm = sb("vtm", [128, 4, 256], BF16); gtm = sb("gtm", [128, 4, 256], BF16); ztm = sb("ztm", [128, 4, 256], BF16)
        B_vtm = Buf("vtm"); B_gtm = Buf("gtm"); B_ztm = Buf("ztm")
        dtr = sb("dtr", [128, 4, 8], F32); dtv = sb("dtv", [128, 4, 8], F32); adt = sb("adt", [128, 4, 8], F32)
        B_dt = Buf("dt")
        ktm = sb("ktm", [128, 4, 2, 128], BF16); B_ktm = Buf("ktm")
        xtm = sb("xtm", [128, 4, 256], BF16); B_xtm = Buf("xtm")
        btm = sb("btm", [128, 4, 128], BF16); B_btm = Buf("btm")
        nfc = sb("nfc", [128, 8], F32); dE = sb("dE", [128, 8], F32); dI = sb("dI", [128, 8], F32)
        decs = sb("decs", [128, 8], F32); wE = sb("wE", [128, 8], F32); B_sm = Buf("ssd_small")
        Lm = sb("Lm", [128, 8, 128], F32); B_Lm = Buf("Lm")
        cbm = sb("cbm", [128, 2, 128], F32); B_cbm = Buf("cbm")
        MT = sb("MT", [128, 8, 128], BF16); B_MT = Buf("MT")
        xdt = sb("xdt", [128, 8, 64], BF16); xE = sb("xE", [128, 8, 64], BF16); B_xdt = Buf("xdt"); B_xE = Buf("xE")
        Hf = sb("Hf", [128, 256], F32); Hfb = sb("Hfb", [128, 256], BF16); B_Hf = Buf("Hf"); B_Hfb = Buf("Hfb")
        Hb = sb("Hb", [128, 256], F32); Hbb = sb("Hbb", [128, 256], BF16); B_Hb = Buf("Hb"); B_Hbb = Buf("Hbb")
        Htmp = sb("Htmp", [128, 256], F32); B_Htmp = Buf("Htmp")
        rf = sb("rf", [128, 2, 128], F32); rfb = sb("rfb", [128, 2, 128], BF16); B_rf = Buf("rf"); B_rfb = Buf("rfb")
        rbk = sb("rbk", [128, 2, 128], F32); rbb = sb("rbb", [128, 2, 128], BF16); B_rb = Buf("rb"); B_rbb = Buf("rbb")
        SmT = sb("SmT", [128, 2, 128], BF16); B_SmT = Buf("SmT")
        ssq = sb("ssq", [128, 2], F32); rs2 = sb("rs2", [128, 2], F32); B_ssq = Buf("ssq")
        junk = sb("junk", [128, 256], F32); B_junk = Buf("junk")
        rtmp = sb("rtmp", [128, 256], F32); B_rtmp = Buf("rtmp")
        yt = sb("yt", [128, 256], F32); yt2 = sb("yt2", [128, 256], F32); B_yt = Buf("yt")
        mixtm = sb("mixtm", [128, 4, 512], BF16); B_mixtm = Buf("mixtm")
        mixT = sb("mixT", [128, 4, 512], BF16); B_mixT = Buf("mixT")

        P0 = ps("P0", [128, 512], F32); P1 = ps("P1", [128, 512], F32); P2 = ps("P2", [128, 512], F32)
        P3 = ps("P3", [128, 512], F32); P4 = ps("P4", [128, 512], F32); P56 = ps("P56", [128, 1024], F32)
        P7 = ps("P7", [128, 512], F32)
        B_P0 = Buf("P0", True); B_P1 = Buf("P1", True); B_P2 = Buf("P2", True); B_P3 = Buf("P3", True); B_P4 = Buf("P4", True)
        B_P56 = Buf("P56", True); B_P7 = Buf("P7", True)
        P4b = P4[:].bitcast(BF16)

        zpad = sb("zpad", [128, 4, 1], BF16); B_zpad = Buf("zpad")
        op(POOL, [], [B_zpad], lambda e: e.memset(zpad[:], 0.0))
        op(POOL, [], [B_Hf], lambda e: e.memset(Hf[:], 0.0))
        op(POOL, [], [B_Hfb], lambda e: e.memset(Hfb[:], 0.0))
        op(POOL, [], [B_Hb], lambda e: e.memset(Hb[:], 0.0))
        op(POOL, [], [B_Hbb], lambda e: e.memset(Hbb[:], 0.0))
        op(POOL, [], [B_rf], lambda e: e.memset(rf[:], 0.0))
        op(POOL, [], [B_rfb], lambda e: e.memset(rfb[:], 0.0))
        op(POOL, [], [B_rb], lambda e: e.memset(rbk[:], 0.0))
        op(POOL, [], [B_rbb], lambda e: e.memset(rbb[:], 0.0))

        def mm_group(out_buf, out_ap, pairs, reads, first=True, last=True):
            n = len(pairs)
            for i, (l_, r_) in enumerate(pairs):
                st = first and i == 0
                sp_ = last and i == n - 1
                op(PE, reads, [out_buf], lambda e, l_=l_, r_=r_, st=st, sp_=sp_: e.matmul(
                    out_ap, l_, r_, start=st, stop=sp_), mark=(i == n - 1), wait=(i == 0))

        def tile_body(t, pas):
            c0 = 512 * t
            for k4 in range(0, 16, 4):
                dma(SP, B_xs, B_in, xs[:, k4:k4 + 4, :], xTp[:, k4:k4 + 4, c0:c0 + TW])
            chk(1)
            for k in range(16):
                op(ACT, [B_xs], [B_sq], lambda e, k=k: e.activation(out=sq[:, k, :], in_=xs[:, k, :], func=AF.Square))
            chk(2)
            mm_group(B_P4, P4[:, 0:512], [(onesb[:], sq[:, k, 0:512]) for k in range(16)], [B_sq, B_k])
            mm_group(B_P1, P1[:, 0:32], [(onesb[:], sq[:, k, TW - 32:TW]) for k in range(16)], [B_sq, B_k])
            chk(2.1)
            epsb = EPS
            op(ACT, [B_P4], [B_rstd], lambda e: e.activation(out=(ang[:, 0:512] if os.environ.get('KALT') else rstd[:, 0:512]), in_=P4[:, 0:512], func=(AF.Copy if os.environ.get('KALT2') else AF.Ln), scale=1.0 / D, bias=(0.0 if os.environ.get('KALT2') else epsc[:, 0:1])))
            if not os.environ.get('KSKIP'):
                op(ACT, [B_P1], [B_rstd], lambda e: e.activation(out=rstd[:, TW - 32:TW], in_=P1[:, 0:32], func=AF.Ln, scale=1.0 / D, bias=epsc[:, 0:1]))
            chk(2.2)
            op(ACT, [B_rstd], [B_rstd], lambda e: e.activation(out=rstd[:], in_=rstd[:], func=AF.Exp, scale=-0.5))
            chk(2.3)
            for k in range(16):
                if k % 2 == 0:
                    op(DVE, [B_xs, B_rstd, B_c], [B_hn], lambda e, k=k: e.scalar_tensor_tensor(
                        out=hn[:, k, :], in0=xs[:, k, :], scalar=nmw[:, k:k + 1], in1=rstd[:], op0=ALU.mult, op1=ALU.mult))
                else:
                    op(POOL, [B_rstd], [B_xs], lambda e, k=k: e.tensor_tensor(out=xs[:, k, :], in0=xs[:, k, :], in1=rstd[:], op=ALU.mult))
                    op(POOL, [B_xs, B_c], [B_hn], lambda e, k=k: e.tensor_scalar_mul(out=hn[:, k, :], in0=xs[:, k, :], scalar1=nmw[:, k:k + 1]))
            chk(3)
            dma(SP, B_posi, B_in, posi[:], pos_d[0:1, c0:c0 + 512].partition_broadcast(128))
            TWO_PI = 2 * math.pi
            op(DVE, [B_posi], [B_cs], lambda e: e.tensor_copy(out=ang[:], in_=posi[:]))
            op(DVE, [B_cs, B_k], [B_cs], lambda e: e.tensor_scalar_mul(out=ang[:], in0=ang[:], scalar1=ifr[:, 0:1]))
            op(DVE, [B_cs], [B_cs], lambda e: e.tensor_scalar_mul(out=ang2[:], in0=ang[:], scalar1=1.0 / TWO_PI))
            op(DVE, [B_cs], [B_posi], lambda e: e.tensor_copy(out=posi[:], in_=ang2[:]))
            op(DVE, [B_posi], [B_cs], lambda e: e.tensor_copy(out=ang2[:], in_=posi[:]))
            op(DVE, [B_cs], [B_cs], lambda e: e.scalar_tensor_tensor(out=ang[:], in0=ang2[:], scalar=-TWO_PI, in1=ang[:], op0=ALU.mult, op1=ALU.add))
            op(DVE, [B_cs], [B_cs], lambda e: e.tensor_scalar(out=ang2[:], in0=ang[:], scalar1=math.pi, scalar2=-TWO_PI, op0=ALU.is_gt, op1=ALU.mult))
            op(DVE, [B_cs], [B_cs], lambda e: e.tensor_add(out=ang[:], in0=ang[:], in1=ang2[:]))
            op(DVE, [B_cs], [B_cs], lambda e: e.tensor_scalar(out=ang2[:], in0=ang[:], scalar1=-math.pi, scalar2=TWO_PI, op0=ALU.is_lt, op1=ALU.mult))
            op(DVE, [B_cs], [B_cs], lambda e: e.tensor_add(out=ang[:], in0=ang[:], in1=ang2[:]))
            op(DVE, [B_cs], [B_cs], lambda e: e.tensor_scalar_add(out=ang2[:], in0=ang[:], scalar1=math.pi / 2))
            op(DVE, [B_cs], [B_cs], lambda e: e.tensor_scalar(out=cosT[:], in0=ang2[:], scalar1=math.pi, scalar2=-TWO_PI, op0=ALU.is_gt, op1=ALU.mult))
            op(DVE, [B_cs], [B_cs], lambda e: e.tensor_add(out=ang2[:], in0=ang2[:], in1=cosT[:]))
            op(ACT, [B_cs], [B_cs], lambda e: e.activation(out=sinT[:], in_=ang[:], func=AF.Sin))
            op(ACT, [B_cs], [B_cs], lambda e: e.activation(out=cosT[:], in_=ang2[:], func=AF.Sin))
            op(DVE, [B_cs, B_k], [B_cs], lambda e: e.tensor_scalar_mul(out=sinT[:], in0=sinT[:], scalar1=sgn[:, 0:1]))

            chk(4)
            def fm_block(bi, conv):
                if conv:
                    mm_group(B_P0, P0[:, 0:512], [(wfm[:, k, bi * 128:(bi + 1) * 128], hn[:, k, 0:512]) for k in range(16)],
                             [B_hn, B_wfm])
                    mm_group(B_P1, P1[:, 32:64], [(wfm[:, k, bi * 128:(bi + 1) * 128], hn[:, k, TW - 32:TW]) for k in range(16)],
                             [B_hn, B_wfm])
                else:
                    mm_group(B_P0, P0[:, 0:512], [(wfm[:, k, bi * 128:(bi + 1) * 128], hn[:, k, 2:514]) for k in range(16)],
                             [B_hn, B_wfm])

            def rotary(bi, dst, dbuf, hidx, scale):
                second = (hidx == 1)
                fm_block(bi, False)
                chk(4.45 if second else 4.1)
                op(ACT, [B_P0], [B_rawb], lambda e: e.activation(out=rawb[:], in_=P0[:, 0:512], func=AF.Copy, scale=scale))
                op(DVE, [B_P0, B_cs], [B_rt1], lambda e: e.scalar_tensor_tensor(
                    out=rt1[:], in0=P0[:, 0:512], scalar=scale, in1=cosT[:], op0=ALU.mult, op1=ALU.mult))
                chk(4.46 if second else 4.2)
                mm_group(B_P7, P7[:, 0:512], [(perm[:], rawb[:])], [B_rawb, B_k])
                chk(4.47 if second else 4.25)
                op(DVE, [B_P7, B_cs], [B_cacc], lambda e: e.tensor_tensor(out=cacc[:], in0=P7[:, 0:512], in1=sinT[:], op=ALU.mult))
                chk(4.48 if second else 4.3)
                op(DVE, [B_rt1, B_cacc], [dbuf], lambda e: e.tensor_add(out=dst[:, hidx, :], in0=rt1[:], in1=cacc[:]))
                chk(4.49 if second else 4.4)

            if pas == 2:
                for h in range(2):
                    rotary(h, qT, B_qT, h, 1.0)
                    for c in range(4):
                        op(POOL, [B_qT, B_k], [B_qfb], lambda e, h=h, c=c: e.tensor_tensor(
                            out=qfT[:, h, c * 128:(c + 1) * 128], in0=qT[:, h, c * 128:(c + 1) * 128], in1=qfr[:, h, :], op=ALU.mult))
                        op(POOL, [B_qT, B_k], [B_qfb], lambda e, h=h, c=c: e.tensor_tensor(
                            out=qbT[:, h, c * 128:(c + 1) * 128], in0=qT[:, h, c * 128:(c + 1) * 128], in1=qbr[:, h, :], op=ALU.mult))
            for h in range(2):
                rotary(2 + h, kT, B_kT, h, 128.0 ** -0.5)
            chk(4.5)
            conv_blocks = [0, 1, 2] if pas == 1 else [0, 1, 2, 3]
            for ci in conv_blocks:
                fm_block(4 + ci, True)
                chk(4.6)
                op(ACT, [B_P0], [B_rawc], lambda e: e.activation(out=rawc[:, 0:512], in_=P0[:, 0:512], func=AF.Copy))
                op(ACT, [B_P1], [B_rawc], lambda e: e.activation(out=rawc[:, 512:516], in_=P1[:, 60:64], func=AF.Copy))
                chk(4.7)
                op(DVE, [B_rawc, B_c], [B_cacc], lambda e, ci=ci: e.tensor_scalar_mul(
                    out=cacc[:], in0=rawc[:, 0:512], scalar1=cw[:, ci, 0:1]))
                for j in range(1, 5):
                    op(DVE, [B_rawc, B_c], [B_cacc], lambda e, ci=ci, j=j: e.scalar_tensor_tensor(
                        out=cacc[:], in0=rawc[:, j:j + 512], scalar=cw[:, ci, j:j + 1], in1=cacc[:], op0=ALU.mult, op1=ALU.add))
                chk(4.8)
                op(ACT, [B_cacc, B_c], [B_xbcT[ci]], lambda e, ci=ci: e.activation(
                    out=xbcT[:, ci, :], in_=cacc[:], func=AF.Silu, bias=cb[:, ci:ci + 1], scale=1.0))

            chk(5)
            for c in range(4):
                lo = 2 + 128 * c
                if pas == 1:
                    mm_group(B_P2, P2[:, 0:256], [(hn[:, k, lo:lo + 128], wtm[:, k, 0:256]) for k in range(16)], [B_hn, B_wtm])
                    mm_group(B_P3, P3[:, 0:8], [(hn[:, k, lo:lo + 128], wtm[:, k, 512:520]) for k in range(16)], [B_hn, B_wtm])
                else:
                    mm_group(B_P2, P2[:, 0:512], [(hn[:, k, lo:lo + 128], wtm[:, k, 0:512]) for k in range(16)], [B_hn, B_wtm])
                    mm_group(B_P3, P3[:, 0:264], [(hn[:, k, lo:lo + 128], wtm[:, k, 512:776]) for k in range(16)], [B_hn, B_wtm])
                op(ACT, [B_P2], [B_vtm], lambda e, c=c: e.activation(out=vtm[:, c, :], in_=P2[:, 0:256], func=AF.Copy))
                op(DVE, [B_P3], [B_dt], lambda e, c=c: e.tensor_copy(out=dtr[:, c, :], in_=P3[:, 0:8]))
                if pas == 2:
                    op(ACT, [B_P2], [B_gtm], lambda e, c=c: e.activation(out=gtm[:, c, :], in_=P2[:, 256:512], func=AF.Silu))
                    op(ACT, [B_P3], [B_ztm], lambda e, c=c: e.activation(out=ztm[:, c, :], in_=P3[:, 8:264], func=AF.Silu))
            op(DVE, [B_dt, B_c], [B_dt], lambda e: e.tensor_tensor(out=dtr[:], in0=dtr[:], in1=bc(dtb[:].unsqueeze(1), [128, 4, 8]), op=ALU.add))
            op(ACT, [B_dt], [B_dt], lambda e: e.activation(out=dtr[:], in_=dtr[:], func=AF.Exp))
            op(ACT, [B_dt], [B_dt], lambda e: e.activation(out=dtv[:], in_=dtr[:], func=AF.Ln, bias=1.0, scale=1.0))
            op(DVE, [B_dt, B_k], [B_dt], lambda e: e.tensor_tensor(out=adt[:], in0=dtv[:], in1=bc(aneg[:].unsqueeze(1), [128, 4, 8]), op=ALU.mult))

            chk(6)
            kw = kbc if pas == 1 else kfc
            for c in range(4):
                for h in range(2):
                    op(PE, [B_kT, B_k], [B_P4], lambda e, c=c, h=h: e.transpose(
                        P4b[:, (c * 2 + h) * 128:(c * 2 + h + 1) * 128], kT[:, h, c * 128:(c + 1) * 128], ident[:]))
            for c in range(4):
                for h in range(2):
                    op(DVE, [B_P4, B_k], [B_ktm], lambda e, c=c, h=h: e.tensor_scalar_mul(
                        out=ktm[:, c, h, :], in0=P4b[:, (c * 2 + h) * 128:(c * 2 + h + 1) * 128], scalar1=kw[:, h:h + 1]))
            for c in range(4):
                for bl in range(2):
                    op(PE, [B_xbcT[bl], B_k], [B_P4], lambda e, c=c, bl=bl: e.transpose(
                        P4b[:, (c * 2 + bl) * 128:(c * 2 + bl + 1) * 128], xbcT[:, bl, c * 128:(c + 1) * 128], ident[:]))
            op(ACT, [B_P4], [B_xtm], lambda e: e.activation(out=xtm[:].rearrange("p c f -> p (c f)"), in_=P4b[:, 0:1024], func=AF.Copy))
            for c in range(4):
                op(PE, [B_xbcT[2], B_k], [B_P4], lambda e, c=c: e.transpose(
                    P4b[:, c * 128:(c + 1) * 128], xbcT[:, 2, c * 128:(c + 1) * 128], ident[:]))
            op(ACT, [B_P4], [B_btm], lambda e: e.activation(out=btm[:].rearrange("p c f -> p (c f)"), in_=P4b[:, 0:512], func=AF.Copy))

            chk(7)
            chunks = [3, 2, 1, 0] if pas == 1 else [0, 1, 2, 3]
            for c in chunks:
                gc = 4 * t + c
                cs = slice(c * 128, (c + 1) * 128)
                mm_group(B_P1, P1[:, 16:20], [(TI[:], adt[:, c, 0:4])], [B_dt, B_k])
                mm_group(B_P1, P1[:, 20:24], [(TS[:], adt[:, c, 4:8])], [B_dt, B_k])
                mm_group(B_P1, P1[:, 24:28], [(TSs[:], adt[:, c, 0:4])], [B_dt, B_k])
                mm_group(B_P1, P1[:, 28:32], [(TIs[:], adt[:, c, 4:8])], [B_dt, B_k])
                mm_group(B_P1, P1[:, 32:40], [(onesf[:], adt[:, c, :])], [B_dt, B_k])
                op(DVE, [B_P1], [B_sm], lambda e: e.tensor_scalar_mul(out=nfc[:], in0=P1[:, 16:24], scalar1=-1.0))
                op(ACT, [B_P1], [B_sm], lambda e: e.activation(out=dI[:], in_=P1[:, 16:24], func=AF.Exp))
                op(ACT, [B_P1], [B_sm], lambda e: e.activation(out=dE[:], in_=P1[:, 24:32], func=AF.Exp))
                op(ACT, [B_P1], [B_sm], lambda e: e.activation(out=decs[:], in_=P1[:, 32:40], func=AF.Exp))
                op(DVE, [B_sm, B_dt], [B_sm], lambda e, c=c: e.tensor_mul(out=wE[:], in0=dE[:], in1=dtv[:, c, :]))
                dsel = slice(4, 8) if pas == 1 else slice(0, 4)
                op(DVE, [B_xtm, B_sm], [B_xE], lambda e, c=c: e.tensor_tensor(
                    out=xE[:, 0:4, :], in0=xtm[:, c, :].rearrange("p (h f) -> p h f", h=4),
                    in1=bc(wE[:, dsel].unsqueeze(2), [128, 4, 64]), op=ALU.mult))
                if pas == 1:
                    dma(SP, B_hbs[gc], B_Hbb, hb_scr.ap()[gc, :, :], Hbb[:])
                    dma(SP, B_rbs[gc], B_rbb, rb_scr.ap()[gc, :, :], rbb[:].rearrange("p h f -> p (h f)"))
                    mm_group(B_P2, P2[:, 0:256], [(btm[:, c, :], xE[:, 0:4, :].rearrange("p h f -> p (h f)"))], [B_btm, B_xE])
                    op(DVE, [B_Hb, B_sm], [B_Htmp], lambda e: e.tensor_tensor(
                        out=Htmp[:].rearrange("p (h f) -> p h f", h=4), in0=Hb[:].rearrange("p (h f) -> p h f", h=4),
                        in1=bc(decs[:, 4:8].unsqueeze(2), [128, 4, 64]), op=ALU.mult))
                    op(DVE, [B_Htmp, B_P2], [B_Hb], lambda e: e.tensor_add(out=Hb[:], in0=Htmp[:], in1=P2[:, 0:256]))
                    op(ACT, [B_Hb], [B_Hbb], lambda e: e.activation(out=Hbb[:], in_=Hb[:], func=AF.Copy))
                    for h in range(2):
                        mm_group(B_P0, P0[:, h * 128:(h + 1) * 128], [(ktm[:, c, h, :], vtm[:, c, h * 128:(h + 1) * 128])], [B_ktm, B_vtm])
                    for h in range(2):
                        op(DVE, [B_rb, B_P0, B_k], [B_rb], lambda e, h=h: e.scalar_tensor_tensor(
                            out=rbk[:, h, :], in0=rbk[:, h, :], scalar=decb[:, h:h + 1], in1=P0[:, h * 128:(h + 1) * 128],
                            op0=ALU.mult, op1=ALU.add))
                    op(ACT, [B_rb], [B_rbb], lambda e: e.activation(out=rbb[:], in_=rbk[:], func=AF.Copy))
                    continue
                dma(SP, B_Hbb, B_hbs[gc], Hbb[:], hb_scr.ap()[gc, :, :])
                dma(SP, B_rbb, B_rbs[gc], rbb[:].rearrange("p h f -> p (h f)"), rb_scr.ap()[gc, :, :])
                for hd in range(8):
                    tri = TI if hd < 4 else TS
                    mm_group(B_P56, P56[:, hd * 128:(hd + 1) * 128], [(bc(adt[:, c, hd:hd + 1], [128, 128]), tri[:])], [B_dt, B_k])
                for hd in range(8):
                    op(DVE, [B_P56, B_sm], [B_Lm], lambda e, hd=hd: e.tensor_scalar(
                        out=Lm[:, hd, :], in0=P56[:, hd * 128:(hd + 1) * 128], scalar1=nfc[:, hd:hd + 1], scalar2=0.0,
                        op0=ALU.add, op1=ALU.min))
                op(ACT, [B_Lm], [B_Lm], lambda e: e.activation(out=Lm[:], in_=Lm[:], func=AF.Exp))
                mm_group(B_P7, P7[:, 0:128], [(xbcT[:, 2, cs], xbcT[:, 3, cs])], [B_xbcT[2], B_xbcT[3]])
                op(DVE, [B_P7, B_k], [B_cbm], lambda e: e.tensor_tensor(out=cbm[:, 0, :], in0=P7[:, 0:128], in1=TI[:], op=ALU.mult))
                op(DVE, [B_P7, B_k], [B_cbm], lambda e: e.tensor_tensor(out=cbm[:, 1, :], in0=P7[:, 0:128], in1=TSs[:], op=ALU.mult))
                for d_ in range(2):
                    op(DVE, [B_Lm, B_cbm], [B_MT], lambda e, d_=d_: e.scalar_tensor_tensor(
                        out=MT[:, 4 * d_:4 * d_ + 4, :], in0=Lm[:, 4 * d_:4 * d_ + 4, :], scalar=1.0,
                        in1=bc(cbm[:, d_:d_ + 1, :], [128, 4, 128]), op0=ALU.mult, op1=ALU.mult))
                    op(POOL, [B_xtm, B_dt], [B_xdt], lambda e, d_=d_, c=c: e.tensor_tensor(
                        out=xdt[:, 4 * d_:4 * d_ + 4, :], in0=xtm[:, c, :].rearrange("p (h f) -> p h f", h=4),
                        in1=bc(dtv[:, c, 4 * d_:4 * d_ + 4].unsqueeze(2), [128, 4, 64]), op=ALU.mult))
                for h in range(4):
                    mm_group(B_P2, P2[:, h * 64:(h + 1) * 64], [(MT[:, h, :], xdt[:, h, :]), (MT[:, 4 + h, :], xdt[:, 4 + h, :])],
                             [B_MT, B_xdt])
                mm_group(B_P3, P3[:, 0:256], [(xbcT[:, 3, cs], Hfb[:])], [B_xbcT[3], B_Hfb])
                mm_group(B_P3, P3[:, 256:512], [(xbcT[:, 3, cs], Hbb[:])], [B_xbcT[3], B_Hbb])
                mm_group(B_P2, P2[:, 256:512], [(btm[:, c, :], xE[:, 0:4, :].rearrange("p h f -> p (h f)"))], [B_btm, B_xE])
                op(DVE, [B_Hf, B_sm], [B_Htmp], lambda e: e.tensor_tensor(
                    out=Htmp[:].rearrange("p (h f) -> p h f", h=4), in0=Hf[:].rearrange("p (h f) -> p h f", h=4),
                    in1=bc(decs[:, 0:4].unsqueeze(2), [128, 4, 64]), op=ALU.mult))
                op(DVE, [B_Htmp, B_P2], [B_Hf], lambda e: e.tensor_add(out=Hf[:], in0=Htmp[:], in1=P2[:, 256:512]))
                op(ACT, [B_Hf], [B_Hfb], lambda e: e.activation(out=Hfb[:], in_=Hf[:], func=AF.Copy))
                v3 = lambda a: a.rearrange("p (h f) -> p h f", h=4)
                op(DVE, [B_P3, B_sm], [B_yt], lambda e: e.tensor_tensor(
                    out=v3(yt[:]), in0=v3(P3[:, 0:256]), in1=bc(dI[:, 0:4].unsqueeze(2), [128, 4, 64]), op=ALU.mult))
                op(DVE, [B_P3, B_sm], [B_yt], lambda e: e.tensor_tensor(
                    out=v3(yt2[:]), in0=v3(P3[:, 256:512]), in1=bc(dI[:, 4:8].unsqueeze(2), [128, 4, 64]), op=ALU.mult))
                op(DVE, [B_yt], [B_yt], lambda e: e.tensor_add(out=yt[:], in0=yt[:], in1=yt2[:]))
                op(DVE, [B_xtm, B_c], [B_yt], lambda e, c=c: e.tensor_tensor(
                    out=v3(yt2[:]), in0=v3(xtm[:, c, :]), in1=bc(dsk[:].unsqueeze(2), [128, 4, 64]), op=ALU.mult))
                op(DVE, [B_yt], [B_yt], lambda e: e.tensor_add(out=yt[:], in0=yt[:], in1=yt2[:]))
                op(DVE, [B_yt, B_P2], [B_yt], lambda e: e.tensor_add(out=yt[:], in0=yt[:], in1=P2[:, 0:256]))
                op(DVE, [B_yt, B_ztm], [B_mixtm], lambda e, c=c: e.tensor_mul(out=mixtm[:, c, 256:512], in0=yt[:], in1=ztm[:, c, :]))
                for h in range(2):
                    mm_group(B_P7, P7[:, 128 + h * 128:256 + h * 128], [(kT[:, h, cs], qT[:, h, cs])], [B_kT, B_qT])
                for h in range(2):
                    op(DVE, [B_P7, B_k], [B_SmT], lambda e, h=h: e.tensor_tensor(
                        out=SmT[:, h, :], in0=P7[:, 128 + h * 128:256 + h * 128], in1=maskT[:, h, :], op=ALU.mult))
                for h in range(2):
                    hs = slice(h * 128, (h + 1) * 128)
                    mm_group(B_P0, P0[:, hs], [(SmT[:, h, :], vtm[:, c, hs]), (qfT[:, h, cs], rfb[:, h, :]), (qbT[:, h, cs], rbb[:, h, :])],
                             [B_SmT, B_vtm, B_qfb, B_rfb, B_rbb])
                    mm_group(B_P0, P0[:, 256 + h * 128:384 + h * 128], [(ktm[:, c, h, :], vtm[:, c, hs])], [B_ktm, B_vtm])
                for h in range(2):
                    op(DVE, [B_rf, B_P0, B_k], [B_rf], lambda e, h=h: e.scalar_tensor_tensor(
                        out=rf[:, h, :], in0=rf[:, h, :], scalar=decf[:, h:h + 1], in1=P0[:, 256 + h * 128:384 + h * 128],
                        op0=ALU.mult, op1=ALU.add))
                op(ACT, [B_rf], [B_rfb], lambda e: e.activation(out=rfb[:], in_=rf[:], func=AF.Copy))
                op(ACT, [B_P0], [B_junk], lambda e: e.activation(out=junk[:], in_=P0[:, 0:256], func=AF.Square))
                op(DVE, [B_junk], [B_ssq], lambda e: e.reduce_sum(out=ssq[:], in_=junk[:].rearrange("p (h f) -> p h f", h=2), axis=mybir.AxisListType.X))
                op(DVE, [B_ssq], [B_ssq], lambda e: e.tensor_scalar(out=rs2[:], in0=ssq[:], scalar1=1.0 / 128.0, scalar2=EPS,
                                                                    op0=ALU.mult, op1=ALU.add))
                op(ACT, [B_ssq], [B_ssq], lambda e: e.activation(out=rs2[:], in_=rs2[:], func=AF.Ln))
                op(ACT, [B_ssq], [B_ssq], lambda e: e.activation(out=rs2[:], in_=rs2[:], func=AF.Exp, scale=-0.5))
                for h in range(2):
                    hs = slice(h * 128, (h + 1) * 128)
                    op(DVE, [B_P0, B_ssq, B_c], [B_rtmp], lambda e, h=h, hs=hs: e.scalar_tensor_tensor(
                        out=rtmp[:, hs], in0=P0[:, hs], scalar=rs2[:, h:h + 1], in1=rnw[:, hs], op0=ALU.mult, op1=ALU.mult))
                op(DVE, [B_rtmp, B_gtm], [B_mixtm], lambda e, c=c: e.tensor_mul(out=mixtm[:, c, 0:256], in0=rtmp[:], in1=gtm[:, c, :]))
            if pas == 2:
                for half in range(2):
                    for c in range(4):
                        for fb in range(2):
                            f = half * 2 + fb
                            op(PE, [B_mixtm, B_k], [B_P4], lambda e, c=c, f=f, fb=fb: e.transpose(
                                P4b[:, fb * 512 + c * 128:fb * 512 + (c + 1) * 128], mixtm[:, c, f * 128:(f + 1) * 128], ident[:]))
                    op(ACT, [B_P4], [B_mixT], lambda e, half=half: e.activation(
                        out=mixT[:, 2 * half:2 * half + 2, :].rearrange("p a t -> p (a t)"), in_=P4b[:, 0:1024], func=AF.Copy))
                seg, tq = t // 4, t % 4
                def arows(sg, lo, n):
                    return agin.ap()[sg * 512:(sg + 1) * 512, lo:lo + n].rearrange("(b p) c -> p b c", p=128)
                dma(SP, B_agin, B_mixT, arows(seg, 1 + 512 * tq, 512), mixT[:, :, :])
                if tq == 0:
                    if seg > 0:
                        dma(SP, B_agin, B_mixT, arows(seg - 1, 2049, 1), mixT[:, :, 0:1], slow=True)
                    else:
                        dma(SP, B_agin, B_zpad, arows(0, 0, 1), zpad[:, :, :], slow=True)
                if tq == 3:
                    if seg < 3:
                        dma(SP, B_agin, B_mixT, arows(seg + 1, 0, 1), mixT[:, :, 511:512], slow=True)
                    else:
                        dma(SP, B_agin, B_zpad, arows(3, 2049, 1), zpad[:, :, :], slow=True)

        try:
            chk(0)
            for t in reversed(range(ntiles)):
                tile_body(t, 1)
            chk(10)
            for t in range(ntiles):
                tile_body(t, 2)
        except StopBuild:
            pass

        if debug_half1:
            dma(SP, Buf("dbgout"), B_agin, mix_dbg[:, :], agin.ap()[:, :], is_out=True)
        C.fence()
    C.es = C.es_outer

    if run_half2:
        if run_half1:
            cc_sem = C.new_sem("cc")
            for k in range(16):
                POOL.e.collective_compute("AllGather", ALU.bypass, replica_groups=[[0, 1, 2, 3], [4, 5, 6, 7]],
                                          ins=[agin.ap()[k * 128:(k + 1) * 128, :].opt()],
                                          outs=[agout.ap()[k * 512:(k + 1) * 512, :].opt()]).then_inc(cc_sem)
            tok = ("D", cc_sem, "cc", 16)
            B_agout.w = tok
            B_agout.dtok = tok
        h2 = contextlib.ExitStack()
        C.es = h2
        with h2:
            _half2(nc, C, dict(locals()))
        C.es = C.es_outer

    for tok in C.out_toks:
        SP.wait(tok)


def _half2(nc, C, V):
    PE, ACT, DVE, POOL, SP = C.PE, C.ACT, C.DVE, C.POOL, C.SP
    op, dma, sb, ps = C.op, C.dma, C.sb, C.ps
    B_in, B_agout, agout, outT = V["B_in"], V["B_agout"], V["agout"], V["outT"]
    NC2 = 2050
    CT = [(0, 512), (512, 512), (1024, 512), (1536, 512), (2048, 2)]
    h1_scr = nc.dram_tensor("h1_scr", [16, 128, NC2], F32)
    h2_scr = nc.dram_tensor("h2_scr", [16, 128, 2048], F32)
    B_h1s = Buf("h1s"); B_h2s = Buf("h2s"); B_out = Buf("outb")

    def mm_group(out_buf, out_ap, pairs, reads):
        n = len(pairs)
        for i, (l_, r_) in enumerate(pairs):
            op(PE, reads, [out_buf], lambda e, l_=l_, r_=r_, st=(i == 0), sp_=(i == n - 1): e.matmul(
                out_ap, l_, r_, start=st, stop=sp_), mark=(i == n - 1), wait=(i == 0))

    B_c = Buf("c2")
    def small(name, src, shape, dt=F32):
        t = sb(name, shape, dt)
        dma(SP, B_c, B_in, t[:], src)
        return t
    gidx = small("gidx", V["idx_d"][:, :], [128, 16], I32)
    snw = small("snw", V["snw_d"][:, :], [128, 16]); nfw = small("nfw", V["nfw_d"][:, :], [128, 16])
    pnw = small("pnw", V["pnw_d"][:, :], [128, 16]); fnw = small("fnw", V["fnw_d"][:, :], [128, 16])
    bpg = small("bpg", V["bpg_d"][:, :], [128, 16]); fcb = small("fcb", V["fcb_d"][:, :], [128, NFB])
    fcw = small("fcw", V["fcw_d"][:, :, :], [128, NFB, 3])
    B_k = Buf("k2")
    onesf = sb("onesf2", [128, 128], F32); onesb = sb("onesb2", [128, 128], BF16); epsc = sb("epsc2", [128, 1], F32)
    op(POOL, [], [B_k], lambda e: e.memset(onesf[:], 1.0))
    op(POOL, [], [B_k], lambda e: e.memset(onesb[:], 1.0))
    op(POOL, [], [B_k], lambda e: e.memset(epsc[:], EPS))
    pTb = sb("pTb", [128, 2, 2048], BF16); B_pT = Buf("pT")
    dma(POOL, B_pT, B_in, pTb[:], V["pT_d"][:, :, :])

    Q = [ps(f"Q{i}", [128, 512], F32) for i in range(8)]
    B_Q = [Buf(f"Q{i}", True) for i in range(8)]

    sacc = sb("sacc", [128, NC2], F32); B_sacc = Buf("sacc")
    rstd2 = sb("rstd2", [128, NC2], F32); B_rstd2 = Buf("rstd2")
    sqt = sb("sqt", [128, 512], F32); B_sqt = Buf("sqt")

    def rsqrt_cols(src_buf, src_ap, dst_buf, dst_ap, n, scale):
        mm_group(B_Q[6], Q[6][:, 0:n], [(onesf[:], src_ap)], [src_buf, B_k])
        op(ACT, [B_Q[6], B_k], [dst_buf], lambda e: e.activation(out=dst_ap, in_=Q[6][:, 0:n], func=AF.Ln, scale=scale, bias=epsc[:, 0:1]))
        op(ACT, [dst_buf], [dst_buf], lambda e: e.activation(out=dst_ap, in_=dst_ap, func=AF.Exp, scale=-0.5))

    pc = contextlib.ExitStack()
    C.es = pc
    with pc:
        mixg = sb("mixg", [128, 16, NC2], BF16); B_mixg = Buf("mixg")
        for kk in range(16):
            POOL.wait(B_agout.w)
            POOL.wait(B_c.w)
            if B_mixg.dsem is None:
                B_mixg.dsem = C.new_sem("d_mixg")
            B_mixg.dcnt += 1
            POOL.e.indirect_dma_start(out=mixg[:, kk, :], out_offset=None, in_=agout.ap()[:, :],
                                      in_offset=bass.IndirectOffsetOnAxis(ap=gidx[:, kk:kk + 1], axis=0)).then_inc(B_mixg.dsem, 16)
            tok = ("D", B_mixg.dsem, B_mixg.name, 16 * B_mixg.dcnt)
            B_mixg.w = tok; B_mixg.dtok = tok
        sqb = sb("sqb", [128, 4, 512], F32); B_sqb = Buf("sqb")
        rg = sb("rg", [128, 512], F32); B_rg = Buf("rg")
        for g_ in range(2):
            blks = [8 * g_ + 2, 8 * g_ + 3, 8 * g_ + 6, 8 * g_ + 7]
            for (c0, n) in CT:
                for bi, kk in enumerate(blks):
                    op(ACT, [B_mixg], [B_sqb], lambda e, bi=bi, kk=kk: e.activation(out=sqb[:, bi, 0:n], in_=mixg[:, kk, c0:c0 + n], func=AF.Square))
                mm_group(B_Q[6], Q[6][:, 0:n], [(onesf[:], sqb[:, bi, 0:n]) for bi in range(4)], [B_sqb, B_k])
                op(ACT, [B_Q[6], B_k], [B_rg], lambda e: e.activation(out=rg[:, 0:n], in_=Q[6][:, 0:n], func=AF.Ln, scale=1.0 / 512.0, bias=epsc[:, 0:1]))
                op(ACT, [B_rg], [B_rg], lambda e: e.activation(out=rg[:, 0:n], in_=rg[:, 0:n], func=AF.Exp, scale=-0.5))
                for kk in blks:
                    op(DVE, [B_rg, B_c], [B_mixg], lambda e, kk=kk: e.scalar_tensor_tensor(
                        out=mixg[:, kk, c0:c0 + n], in0=mixg[:, kk, c0:c0 + n], scalar=snw[:, kk:kk + 1], in1=rg[:, 0:n], op0=ALU.mult, op1=ALU.mult))
        wo = [sb(f"wo{i}", [128, 16, 128], BF16) for i in range(2)]; B_wo = [Buf(f"wo{i}") for i in range(2)]
        xrow = [sb(f"xrow{i}", [128, NC2], F32) for i in range(2)]; B_xrow = [Buf(f"xrow{i}") for i in range(2)]
        op(POOL, [], [B_sacc], lambda e: e.memset(sacc[:], 0.0))
        for m in range(16):
            s_ = m % 2
            dma(POOL, B_wo[s_], B_in, wo[s_][:].rearrange("p k n -> p (k n)"), V["wout_d"][:, m, :])
            dma(SP, B_xrow[s_], B_in, xrow[s_][:], V["xw_d"][:, m, :])
            for ti, (c0, n) in enumerate(CT):
                qb = 4 + (ti % 2)
                mm_group(B_Q[qb], Q[qb][:, 0:n], [(wo[s_][:, k, :], mixg[:, k, c0:c0 + n]) for k in range(16)], [B_wo[s_], B_mixg])
                op(DVE, [B_Q[qb]], [B_xrow[s_]], lambda e, qb=qb, c0=c0, n=n: e.tensor_add(out=xrow[s_][:, c0:c0 + n], in0=xrow[s_][:, c0:c0 + n], in1=Q[qb][:, 0:n]))
                op(POOL, [B_xrow[s_]], [B_sqt], lambda e, c0=c0, n=n: e.tensor_tensor(out=sqt[:, 0:n], in0=xrow[s_][:, c0:c0 + n], in1=xrow[s_][:, c0:c0 + n], op=ALU.mult))
                op(POOL, [B_sqt], [B_sacc], lambda e, c0=c0, n=n: e.tensor_add(out=sacc[:, c0:c0 + n], in0=sacc[:, c0:c0 + n], in1=sqt[:, 0:n]))
            dma(SP, B_h1s, B_xrow[s_], h1_scr.ap()[m, :, :], xrow[s_][:])
        for (c0, n) in CT:
            rsqrt_cols(B_sacc, sacc[:, c0:c0 + n], B_rstd2, rstd2[:, c0:c0 + n], n, 1.0 / D)
        C.fence()
    C.es = h2_es = V["h2"]

    hst = [sb(f"hst{i}", [128, 4, 514], F32) for i in range(2)]; B_hst = [Buf(f"hst{i}") for i in range(2)]
    hnq = sb("hnq", [128, 16, 514], BF16); B_hnq = Buf("hnq")
    act = sb("act", [128, NFB, 512], BF16); B_act = Buf("act")
    wgb = [sb(f"wgb{i}", [128, 16, 128], BF16) for i in range(2)]; B_wg = [Buf(f"wg{i}") for i in range(2)]
    wub = [sb(f"wub{i}", [128, 16, 128], BF16) for i in range(2)]; B_wu = [Buf(f"wu{i}") for i in range(2)]
    wdb = [sb(f"wdb{i}", [128, NFB, 128], BF16) for i in range(2)]; B_wd = [Buf(f"wd{i}") for i in range(2)]
    wpb = [sb(f"wpb{i}", [128, 2, 128], BF16) for i in range(2)]; B_wp = [Buf(f"wp{i}") for i in range(2)]
    graw = sb("graw", [128, 514], F32); B_graw = Buf("graw")
    gacc = sb("gacc", [128, 512], F32); B_gacc = Buf("gacc")
    hrow = [sb(f"hrow{i}", [128, 512], F32) for i in range(2)]; B_hrow = [Buf(f"hrow{i}") for i in range(2)]
    sacc2 = sb("sacc2", [128, 512], F32); B_sacc2 = Buf("sacc2")
    rstdq = sb("rstdq", [128, 512], F32); B_rstdq = Buf("rstdq")
    h3q = sb("h3q", [128, 16, 512], F32); B_h3q = Buf("h3q")
    sgt = sb("sgt", [128, 512], F32); B_sgt = Buf("sgt")

    for q in range(4):
        w0 = 512 * q
        for s4 in range(4):
            b_ = s4 % 2
            dma(SP, B_hst[b_], B_h1s, hst[b_][:], h1_scr.ap()[4 * s4:4 * s4 + 4, :, w0:w0 + 514].rearrange("k p c -> p k c"))
            for kk in range(4):
                k = 4 * s4 + kk
                op(DVE, [B_hst[b_], B_rstd2, B_c], [B_hnq], lambda e, b_=b_, kk=kk, k=k: e.scalar_tensor_tensor(
                    out=hnq[:, k, :], in0=hst[b_][:, kk, :], scalar=nfw[:, k:k + 1], in1=rstd2[:, w0:w0 + 514], op0=ALU.mult, op1=ALU.mult))
        for f in range(NFB):
            s_ = f % 2
            dma(POOL, B_wg[s_], B_in, wgb[s_][:].rearrange("p k n -> p (k n)"), V["wg_d"][:, f, :])
            dma(POOL, B_wu[s_], B_in, wub[s_][:].rearrange("p k n -> p (k n)"), V["wu_d"][:, f, :])
            mm_group(B_Q[s_], Q[s_][:, 0:512], [(wgb[s_][:, k, :], hnq[:, k, 0:512]) for k in range(16)], [B_wg[s_], B_hnq])
            mm_group(B_Q[6], Q[6][:, 0:32], [(wgb[s_][:, k, :], hnq[:, k, 482:514]) for k in range(16)], [B_wg[s_], B_hnq])
            mm_group(B_Q[2 + s_], Q[2 + s_][:, 0:512], [(wub[s_][:, k, :], hnq[:, k, 1:513]) for k in range(16)], [B_wu[s_], B_hnq])
            op(ACT, [B_Q[s_]], [B_graw], lambda e, s_=s_: e.activation(out=graw[:, 0:512], in_=Q[s_][:, 0:512], func=AF.Copy))
            op(ACT, [B_Q[6]], [B_graw], lambda e: e.activation(out=graw[:, 512:514], in_=Q[6][:, 30:32], func=AF.Copy))
            op(DVE, [B_graw, B_c], [B_gacc], lambda e, f=f: e.tensor_scalar_mul(out=gacc[:], in0=graw[:, 0:512], scalar1=fcw[:, f, 0:1]))
            for j in (1, 2):
                op(DVE, [B_graw, B_c], [B_gacc], lambda e, f=f, j=j: e.scalar_tensor_tensor(
                    out=gacc[:], in0=graw[:, j:j + 512], scalar=fcw[:, f, j:j + 1], in1=gacc[:], op0=ALU.mult, op1=ALU.add))
            op(ACT, [B_gacc, B_c], [B_gacc], lambda e, f=f: e.activation(out=gacc[:], in_=gacc[:], func=AF.Gelu_apprx_tanh, bias=fcb[:, f:f + 1], scale=1.0))
            op(DVE, [B_gacc, B_Q[2 + s_]], [B_act], lambda e, f=f, s_=s_: e.tensor_tensor(out=act[:, f, :], in0=gacc[:], in1=Q[2 + s_][:, 0:512], op=ALU.mult))
        op(POOL, [], [B_sacc2], lambda e: e.memset(sacc2[:], 0.0))
        for m in range(16):
            s_ = m % 2
            dma(POOL, B_wd[s_], B_in, wdb[s_][:].rearrange("p k n -> p (k n)"), V["wd_d"][:, m, :])
            dma(SP, B_hrow[s_], B_h1s, hrow[s_][:], h1_scr.ap()[m, :, w0 + 1:w0 + 513])
            mm_group(B_Q[4 + s_], Q[4 + s_][:, 0:512], [(wdb[s_][:, f, :], act[:, f, :]) for f in range(NFB)], [B_wd[s_], B_act])
            op(DVE, [B_Q[4 + s_]], [B_hrow[s_]], lambda e, s_=s_: e.tensor_add(out=hrow[s_][:], in0=hrow[s_][:], in1=Q[4 + s_][:, 0:512]))
            op(POOL, [B_hrow[s_]], [B_sqt], lambda e, s_=s_: e.tensor_tensor(out=sqt[:], in0=hrow[s_][:], in1=hrow[s_][:], op=ALU.mult))
            op(POOL, [B_sqt], [B_sacc2], lambda e: e.tensor_add(out=sacc2[:], in0=sacc2[:], in1=sqt[:]))
            dma(SP, B_h2s, B_hrow[s_], h2_scr.ap()[m, :, w0:w0 + 512], hrow[s_][:])
        rsqrt_cols(B_sacc2, sacc2[:], B_rstdq, rstdq[:], 512, 1.0 / D)
        for s4 in range(4):
            b_ = s4 % 2
            dma(SP, B_hst[b_], B_h2s, hst[b_][:, :, 0:512], h2_scr.ap()[4 * s4:4 * s4 + 4, :, w0:w0 + 512].rearrange("k p c -> p k c"))
            for kk in range(4):
                k = 4 * s4 + kk
                op(DVE, [B_hst[b_], B_rstdq, B_c], [B_hnq], lambda e, b_=b_, kk=kk, k=k: e.scalar_tensor_tensor(
                    out=hnq[:, k, 0:512], in0=hst[b_][:, kk, 0:512], scalar=pnw[:, k:k + 1], in1=rstdq[:], op0=ALU.mult, op1=ALU.mult))
                op(ACT, [B_hst[b_]], [B_h3q], lambda e, b_=b_, kk=kk, k=k: e.activation(out=h3q[:, k, :], in_=hst[b_][:, kk, 0:512], func=AF.Copy))
        op(POOL, [], [B_sacc2], lambda e: e.memset(sacc2[:], 0.0))
        for m in range(16):
            s_ = m % 2
            dma(POOL, B_wg[s_], B_in, wgb[s_][:].rearrange("p k n -> p (k n)"), V["wpg_d"][:, m, :])
            dma(POOL, B_wp[s_], B_in, wpb[s_][:].rearrange("p k n -> p (k n)"), V["wpp_d"][:, m, :])
            mm_group(B_Q[s_], Q[s_][:, 0:512], [(wgb[s_][:, k, :], hnq[:, k, 0:512]) for k in range(16)], [B_wg[s_], B_hnq])
            mm_group(B_Q[2 + s_], Q[2 + s_][:, 0:512], [(wpb[s_][:, kk, :], pTb[:, kk, w0:w0 + 512]) for kk in range(2)], [B_wp[s_], B_pT])
            op(ACT, [B_Q[s_], B_c], [B_sgt], lambda e, s_=s_, m=m: e.activation(out=sgt[:], in_=Q[s_][:, 0:512], func=AF.Sigmoid, bias=bpg[:, m:m + 1], scale=1.0))
            op(DVE, [B_sgt, B_Q[2 + s_]], [B_sgt], lambda e, s_=s_: e.tensor_tensor(out=sgt[:], in0=sgt[:], in1=Q[2 + s_][:, 0:512], op=ALU.mult))
            op(DVE, [B_sgt], [B_h3q], lambda e, m=m: e.tensor_add(out=h3q[:, m, :], in0=h3q[:, m, :], in1=sgt[:]))
            op(POOL, [B_h3q], [B_sqt], lambda e, m=m: e.tensor_tensor(out=sqt[:], in0=h3q[:, m, :], in1=h3q[:, m, :], op=ALU.mult))
            op(POOL, [B_sqt], [B_sacc2], lambda e: e.tensor_add(out=sacc2[:], in0=sacc2[:], in1=sqt[:]))
        rsqrt_cols(B_sacc2, sacc2[:], B_rstdq, rstdq[:], 512, 1.0 / D)
        for m in range(16):
            op(DVE, [B_rstdq, B_c], [B_h3q], lambda e, m=m: e.scalar_tensor_tensor(
                out=h3q[:, m, :], in0=h3q[:, m, :], scalar=fnw[:, m:m + 1], in1=rstdq[:], op0=ALU.mult, op1=ALU.mult))
        for m4 in range(0, 16, 4):
            dma(SP, B_out, B_h3q, outT[:, m4:m4 + 4, w0:w0 + 512], h3q[:, m4:m4 + 4, :], is_out=True)


def _pk(a):
    kp, n = a.shape
    return np.ascontiguousarray(a.reshape(kp // 128, 128, n).transpose(1, 0, 2))


def _col(v):
    return np.ascontiguousarray(v.reshape(-1, 128).T)


def prep_half1(inp, b, j):
    f32 = np.float32
    x = np.asarray(inp["x"], f32)
    xT = np.zeros((D, L + 4), f32)
    xT[:, 2:L + 2] = x[b].T
    w_in = np.asarray(inp["w_in"], f32)[0]
    OFF_Q, OFF_K, OFF_V, OFF_G, OFF_Z, OFF_XBC, OFF_DT = 0, 1024, 2048, 3072, 4096, 5120, 6656
    g = j // 2
    rh = [2 * j, 2 * j + 1]
    sh = [4 * j + i for i in range(4)]
    def hc(off, h, w=128):
        return list(range(off + h * w, off + (h + 1) * w))
    fm_cols = hc(OFF_Q, rh[0]) + hc(OFF_Q, rh[1]) + hc(OFF_K, rh[0]) + hc(OFF_K, rh[1])
    xcols = []
    for h in sh:
        xcols += hc(OFF_XBC, h, 64)
    bcols = list(range(OFF_XBC + 1024 + g * 128, OFF_XBC + 1024 + (g + 1) * 128))
    ccols = list(range(OFF_XBC + 1024 + 256 + g * 128, OFF_XBC + 1024 + 256 + (g + 1) * 128))
    fm_cols += xcols + bcols + ccols
    zcols = []
    for h in sh:
        zcols += hc(OFF_Z, h, 64)
    dtcols = [OFF_DT + h for h in sh] + [OFF_DT + 16 + h for h in sh]
    tm_cols = hc(OFF_V, rh[0]) + hc(OFF_V, rh[1]) + hc(OFF_G, rh[0]) + hc(OFF_G, rh[1]) + dtcols + zcols
    convc = np.array(xcols + bcols + ccols) - OFF_XBC
    cw = np.asarray(inp["ssd_conv_w"], f32)[0][:, convc]
    cbv = np.asarray(inp["ssd_conv_b"], f32)[0][convc]
    m = {
        "xTp": _pk(xT),
        "pos": np.ascontiguousarray(np.asarray(inp["positions"]).astype(np.int32)[b][None, :]),
        "wfm": _pk(w_in[:, fm_cols]),
        "wtm": _pk(w_in[:, tm_cols]),
        "nmw": _col(np.asarray(inp["norm_mix_w"], f32)[0]),
        "cw": np.ascontiguousarray(cw.T.reshape(4, 128, 5).transpose(1, 0, 2)),
        "cb": np.ascontiguousarray(cbv.reshape(4, 128).T),
        "dtb": np.concatenate([np.asarray(inp["ssd_dt_bias"], f32)[0, 0, sh], np.asarray(inp["ssd_dt_bias"], f32)[0, 1, sh]])[None, :],
        "alog": np.concatenate([np.asarray(inp["ssd_a_log"], f32)[0, 0, sh], np.asarray(inp["ssd_a_log"], f32)[0, 1, sh]])[None, :],
        "dsk": np.asarray(inp["ssd_d"], f32)[0, sh][None, :],
        "rnw": np.asarray(inp["ret_norm_w"], f32)[0, rh[0] * 128:(rh[1] + 1) * 128][None, :],
        "hh": np.array([rh], f32),
    }
    return {k: np.ascontiguousarray(v) for k, v in m.items()}


def _blk(a, nb):
    K, N = a.shape
    kk = K // 128
    return np.ascontiguousarray(a.reshape(kk, 128, nb, N // nb).transpose(1, 2, 0, 3).reshape(128, nb, kk * (N // nb)))


def gathered_perm():
    perm = []
    for i in range(4):
        perm += list(range(256 * i, 256 * i + 256))
        perm += list(range(1024 + 256 * i, 1024 + 256 * i + 256))
    return np.array(perm)


def prep_half2(inp, b, j):
    f32 = np.float32
    x = np.asarray(inp["x"], f32)
    xT = np.zeros((D, L + 4), f32)
    xT[:, 2:L + 2] = x[b].T
    perm = gathered_perm()
    snw_full = np.zeros(2048, f32)
    snw_full[1024:] = np.asarray(inp["ssd_norm_w"], f32)[0]
    idx = np.zeros((128, 16), np.int32)
    for i in range(4):
        for bl in range(4):
            idx[:, i * 4 + bl] = ((j * 4 + bl) * 4 + i) * 128 + np.arange(128)
    m = {
        "xw": _pk(xT[:, 2048 * j + 1:2048 * j + 1 + 2050]),
        "pT": _pk(np.asarray(inp["p"], f32)[0, b, 2048 * j:2048 * j + 2048, :].T),
        "gidx": idx,
        "wout": _blk(np.asarray(inp["w_out"], f32)[0][perm, :], 16),
        "snw": _col(snw_full[perm]),
        "nfw": _col(np.asarray(inp["norm_ffn_w"], f32)[0]),
        "wg": _blk(np.asarray(inp["ffn_w_gate"], f32)[0], NFB),
        "wu": _blk(np.asarray(inp["ffn_w_up"], f32)[0], NFB),
        "fcw": np.ascontiguousarray(np.asarray(inp["ffn_conv_w"], f32)[0].T.reshape(NFB, 128, 3).transpose(1, 0, 2)),
        "fcb": _col(np.asarray(inp["ffn_conv_b"], f32)[0]),
        "wd": _blk(np.asarray(inp["ffn_w_down"], f32)[0], 16),
        "pnw": _col(np.asarray(inp["ple_norm_w"], f32)[0]),
        "wpg": _blk(np.asarray(inp["ple_w_gate"], f32)[0], 16),
        "bpg": _col(np.asarray(inp["ple_b_gate"], f32)[0]),
        "wpp": _blk(np.asarray(inp["ple_w_proj"], f32)[0], 16),
        "fnw": _col(np.asarray(inp["final_norm_w"], f32)),
    }
    return {k: np.ascontiguousarray(v) for k, v in m.items()}


_NC_CACHE = {}


def kernel(**inputs):
    if "nc" not in _NC_CACHE:
        _NC_CACHE["nc"] = build_nc()
    nc = _NC_CACHE["nc"]
    in_maps = []
    for c in range(8):
        b, j = c // 4, c % 4
        m = prep_half1(inputs, b, j)
        m.update(prep_half2(inputs, b, j))
        in_maps.append(m)
    res = run_bass_kernel_spmd(nc, in_maps, core_ids=list(range(8)))
    out = np.zeros((2, L, D), np.float32)
    for c in range(8):
        b, j = c // 4, c % 4
        oT = np.asarray(res.results[c]["outT"])
        out[b, 2048 * j:2048 * j + 2048, :] = oT.transpose(2, 1, 0).reshape(2048, D)
    return out
```

```python
import contextlib
import math
import numpy as np
import ml_dtypes
import concourse.bass as bass
import concourse.mybir as mybir
from concourse.bass_utils import run_bass_kernel_spmd

F32 = mybir.dt.float32
BF16 = mybir.dt.bfloat16
I32 = mybir.dt.int32
AF = mybir.ActivationFunctionType
ALU = mybir.AluOpType

D = 2048
L = 8192
NT = 16
TW = 516
EPS = 1e-6
DFF = 5632
NFB = DFF // 128
SEMCH = 20000
NSEM = 100


import os
STOP = float(os.environ.get('KSTOP', '99'))


class StopBuild(Exception):
    pass


def chk(level):
    if STOP <= level:
        raise StopBuild()


ALL_BUFS = []


class Buf:
    def __init__(self, name, psum=False):
        ALL_BUFS.append(self)
        self.name = name
        self.psum = psum
        self.dtok = None
        self.w = None
        self.r = []
        self.dsem = None
        self.dcnt = 0


class Eng:
    def __init__(self, ctx, name, e):
        self.ctx = ctx
        self.name = name
        self.e = e
        self.sems = []
        self.n = 0
        self.seen = {}
        self.last = None

    def _sem(self, idx):
        while len(self.sems) <= idx:
            self.sems.append(self.ctx.new_sem(f"{self.name}_p{len(self.sems)}"))
        return self.sems[idx]

    def mark(self, ins):
        k, v = divmod(self.n, SEMCH)
        ins.then_inc(self._sem(k), 1)
        self.n += 1
        self.last = ("E", self, k, v + 1)
        return self.last

    def wait(self, tok):
        if tok is None:
            return
        if tok[0] == "E":
            _, prod, k, v = tok
            if prod is self and (not self.ctx.same_sync or self.name == "pe"):
                return
            key = ("E", prod.name)
            if self.seen.get(key, (-1, 0)) >= (k, v):
                return
            self.e.wait_ge(prod.sems[k], v)
            self.seen[key] = (k, v)
        else:
            _, sem, name, v = tok
            key = ("D", name)
            if self.seen.get(key, 0) >= v:
                return
            self.e.wait_ge(sem, v)
            self.seen[key] = v


class Ctx:
    def __init__(self, nc, es):
        self.nc = nc
        self.es = es
        self.same_sync = not os.environ.get('KNOSAME')
        self.nsem = 0
        self.PE = Eng(self, "pe", nc.tensor)
        self.ACT = Eng(self, "act", nc.scalar)
        self.DVE = Eng(self, "dve", nc.vector)
        self.POOL = Eng(self, "pool", nc.gpsimd)
        self.SP = Eng(self, "sp", nc.sync)
        self.out_toks = []
        self.sem_pool = [es.enter_context(nc.semaphore(f"sem{i}")) for i in range(NSEM)]

    def new_sem(self, name):
        self.nsem += 1
        return self.sem_pool[self.nsem - 1]

    def sb(self, name, shape, dt):
        return self.es.enter_context(self.nc.sbuf_tensor("s_" + name, list(shape), dt))

    def ps(self, name, shape, dt):
        return self.es.enter_context(self.nc.psum_tensor("p_" + name, list(shape), dt))

    def op(self, E, reads, writes, fn, mark=True, wait=True):
        if wait:
            for b in reads:
                E.wait(b.w)
                if b.psum and not os.environ.get('KNOPS'):
                    for t in b.r:
                        E.wait(t)
            for b in writes:
                E.wait(b.w)
                for t in b.r:
                    E.wait(t)
        ins = fn(E.e)
        if mark:
            tok = E.mark(ins)
            for b in reads:
                b.r.append(tok)
                if len(b.r) > 24:
                    b.r = b.r[-24:]
            for b in writes:
                b.w = tok
                b.r = []
        return ins

    def fence(self, engines=None):
        allE = (self.PE, self.ACT, self.DVE, self.POOL, self.SP)
        lasts = [E.last for E in allE]
        for E in (engines or allE):
            for t in lasts:
                if t is not None and t[1] is not E:
                    E.wait(t)
            for b in ALL_BUFS:
                E.wait(b.dtok)
                for tk in getattr(b, "dtoks", {}).values():
                    E.wait(tk)

    def dma(self, Q, dst, src, out_ap, in_ap, is_out=False, slow=False):
        Q.wait(src.w)
        Q.wait(dst.w)
        for t in dst.r:
            Q.wait(t)
        kind = "sw" if Q is self.POOL else "hw"
        if not hasattr(dst, "dsems"):
            dst.dsems = {}
        if kind not in dst.dsems:
            dst.dsems[kind] = [self.new_sem("d_" + dst.name + kind), 0]
        ent = dst.dsems[kind]
        ent[1] += 1
        dst.dsem = ent[0]
        dst.dcnt = ent[1]
        if slow:
            Q.e.dma_start(out=out_ap, in_=in_ap, allow_slow_non_contiguous=True).then_inc(dst.dsem, 16)
        else:
            Q.e.dma_start(out=out_ap, in_=in_ap).then_inc(dst.dsem, 16)
        tok = ("D", dst.dsem, dst.name + kind, 16 * dst.dcnt)
        dst.w = tok
        dst.dtok = tok
        if not hasattr(dst, "dtoks"):
            dst.dtoks = {}
        dst.dtoks[kind] = tok
        dst.r = []
        src.r.append(tok)
        if len(src.r) > 24:
            src.r = src.r[-24:]
        if is_out:
            self.out_toks.append(tok)
        return tok


def bc(ap, shape):
    return ap.to_broadcast(list(shape))


def build_nc(debug_half1=False, ntiles=NT, run_half2=True, run_half1=True):
    del ALL_BUFS[:]
    nc = bass.Bass("TRN2", target_bir_lowering=False)
    es = contextlib.ExitStack()
    with es:
        C = Ctx(nc, es)
        _build(nc, C, debug_half1, ntiles, run_half2, run_half1)
    return nc


def _build(nc, C, debug_half1, ntiles, run_half2, run_half1=True):
    PE, ACT, DVE, POOL, SP = C.PE, C.ACT, C.DVE, C.POOL, C.SP
    op, dma, sb, ps = C.op, C.dma, C.sb, C.ps

    def din(name, shape, dt=F32):
        return nc.dram_tensor(name, list(shape), dt, kind="ExternalInput").ap()

    xTp = din("xTp", [128, 16, L + 4])
    pos_d = din("pos", [1, L], I32)
    wfm_d = din("wfm", [128, 16, 1024])
    wtm_d = din("wtm", [128, 16, 776])
    nmw_d = din("nmw", [128, 16])
    cw_d = din("cw", [128, 4, 5])
    cb_d = din("cb", [128, 4])
    dtb_d = din("dtb", [1, 8])
    alog_d = din("alog", [1, 8])
    dsk_d = din("dsk", [1, 4])
    rnw_d = din("rnw", [1, 256])
    hh_d = din("hh", [1, 2])
    NC2 = 2050
    if run_half2:
        xw_d = din("xw", [128, 16, NC2])
        pT_d = din("pT", [128, 2, 2048])
        idx_d = din("gidx", [128, 16], I32)
        wout_d = din("wout", [128, 16, 2048])
        snw_d = din("snw", [128, 16])
        nfw_d = din("nfw", [128, 16])
        wg_d = din("wg", [128, NFB, 2048])
        wu_d = din("wu", [128, NFB, 2048])
        fcw_d = din("fcw", [128, NFB, 3])
        fcb_d = din("fcb", [128, NFB])
        wd_d = din("wd", [128, 16, NFB * 128])
        pnw_d = din("pnw", [128, 16])
        wpg_d = din("wpg", [128, 16, 2048])
        bpg_d = din("bpg", [128, 16])
        wpp_d = din("wpp", [128, 16, 256])
        fnw_d = din("fnw", [128, 16])
        outT = nc.dram_tensor("outT", [128, 16, 2048], F32, kind="ExternalOutput").ap()
    if debug_half1:
        mix_dbg = nc.dram_tensor("mix_dbg", [2048, 2050], BF16, kind="ExternalOutput").ap()

    agin = nc.dram_tensor("agin", [2048, NC2], BF16)
    if run_half1:
        agout = nc.dram_tensor("agout", [8192, NC2], BF16)
    else:
        agout = nc.dram_tensor("agout", [8192, NC2], BF16, kind="ExternalInput")
    rb_scr = nc.dram_tensor("rb_scr", [64, 128, 256], BF16)
    hb_scr = nc.dram_tensor("hb_scr", [64, 128, 256], BF16)
    B_agin = Buf("agin")
    B_agout = Buf("agout")
    B_in = Buf("ext_in")
    B_rbs = [Buf("rbs")] * 64
    B_hbs = [Buf("hbs")] * 64

    h1 = contextlib.ExitStack()
    C.es_outer = C.es
    C.es = h1
    with h1:
      if run_half1:
        wfm = sb("wfm_s", [128, 16, 1024], BF16); B_wfm = Buf("wfm")
        wtm = sb("wtm_s", [128, 16, 776], BF16); B_wtm = Buf("wtm")
        for k in range(0, 16, 4):
            dma(POOL, B_wfm, B_in, wfm[:, k:k + 4, :], wfm_d[:, k:k + 4, :])
            dma(POOL, B_wtm, B_in, wtm[:, k:k + 4, :], wtm_d[:, k:k + 4, :])
        B_c = Buf("consts")
        nmw = sb("nmw", [128, 16], F32); cw = sb("cw", [128, 4, 5], F32); cb = sb("cb", [128, 4], F32)
        dtb = sb("dtb", [128, 8], F32); alog = sb("alog", [128, 8], F32); dsk = sb("dsk", [128, 4], F32)
        rnw = sb("rnw", [128, 256], F32); hh = sb("hh", [128, 2], F32)
        dma(SP, B_c, B_in, nmw[:], nmw_d[:, :])
        dma(SP, B_c, B_in, cw[:], cw_d[:, :, :])
        dma(SP, B_c, B_in, cb[:], cb_d[:, :])
        dma(SP, B_c, B_in, dtb[:], dtb_d[0:1, :].partition_broadcast(128))
        dma(SP, B_c, B_in, alog[:], alog_d[0:1, :].partition_broadcast(128))
        dma(SP, B_c, B_in, dsk[:], dsk_d[0:1, :].partition_broadcast(128))
        dma(SP, B_c, B_in, rnw[:], rnw_d[0:1, :].partition_broadcast(128))
        dma(SP, B_c, B_in, hh[:], hh_d[0:1, :].partition_broadcast(128))

        B_k = Buf("kconst")
        dmat = sb("dmat", [128, 128], F32)
        TI = sb("TI", [128, 128], F32); TS = sb("TS", [128, 128], F32)
        TIs = sb("TIs", [128, 128], F32); TSs = sb("TSs", [128, 128], F32)
        onesf = sb("onesf", [128, 128], F32); onesb = sb("onesb", [128, 128], BF16)
        identf = sb("identf", [128, 128], F32); ident = sb("ident", [128, 128], BF16)
        permf = sb("permf", [128, 128], F32); perm = sb("perm", [128, 128], BF16)
        pcol = sb("pcol", [128, 1], F32); irow = sb("irow", [128, 128], F32)
        ifr = sb("ifr", [128, 1], F32); sgn = sb("sgn", [128, 1], F32)
        tmpc = sb("tmpc", [128, 128], F32); tmpc2 = sb("tmpc2", [128, 128], F32)
        g = POOL
        op(g, [], [B_k], lambda e: e.iota(dmat[:], pattern=[[1, 128]], base=0, channel_multiplier=-1,
                                          allow_small_or_imprecise_dtypes=True))
        op(g, [], [B_k], lambda e: e.tensor_single_scalar(out=TI[:], in_=dmat[:], scalar=0.0, op=ALU.is_ge))
        op(g, [], [B_k], lambda e: e.tensor_single_scalar(out=TS[:], in_=dmat[:], scalar=0.0, op=ALU.is_le))
        op(g, [], [B_k], lambda e: e.tensor_single_scalar(out=TIs[:], in_=dmat[:], scalar=0.0, op=ALU.is_gt))
        op(g, [], [B_k], lambda e: e.tensor_single_scalar(out=TSs[:], in_=dmat[:], scalar=0.0, op=ALU.is_lt))
        op(g, [], [B_k], lambda e: e.tensor_single_scalar(out=identf[:], in_=dmat[:], scalar=0.0, op=ALU.is_equal))
        op(g, [], [B_k], lambda e: e.tensor_copy(out=ident[:], in_=identf[:]))
        op(g, [], [B_k], lambda e: e.memset(onesf[:], 1.0))
        op(g, [], [B_k], lambda e: e.memset(onesb[:], 1.0))
        op(g, [], [B_k], lambda e: e.tensor_scalar(out=tmpc[:], in0=dmat[:], scalar1=64.0, scalar2=0.0,
                                                   op0=ALU.add, op1=ALU.is_equal))
        op(g, [], [B_k], lambda e: e.tensor_scalar(out=tmpc2[:], in0=dmat[:], scalar1=-64.0, scalar2=0.0,
                                                   op0=ALU.add, op1=ALU.is_equal))
        op(g, [], [B_k], lambda e: e.tensor_add(out=permf[:], in0=tmpc[:], in1=tmpc2[:]))
        op(g, [], [B_k], lambda e: e.tensor_copy(out=perm[:], in_=permf[:]))
        op(g, [], [B_k], lambda e: e.iota(pcol[:], pattern=[[0, 1]], base=0, channel_multiplier=1,
                                          allow_small_or_imprecise_dtypes=True))
        op(g, [], [B_k], lambda e: e.iota(irow[:], pattern=[[1, 128]], base=0, channel_multiplier=0,
                                          allow_small_or_imprecise_dtypes=True))
        pm = sb("pm", [128, 1], F32)
        op(DVE, [B_k], [B_k], lambda e: e.tensor_scalar(out=pm[:], in0=pcol[:], scalar1=64.0, scalar2=-64.0, op0=ALU.is_ge, op1=ALU.mult))
        op(DVE, [B_k], [B_k], lambda e: e.tensor_add(out=pm[:], in0=pm[:], in1=pcol[:]))
        op(DVE, [B_k], [B_k], lambda e: e.tensor_scalar(out=sgn[:], in0=pcol[:], scalar1=64.0, scalar2=2.0,
                                                   op0=ALU.is_ge, op1=ALU.mult))
        op(DVE, [B_k], [B_k], lambda e: e.tensor_scalar_add(out=sgn[:], in0=sgn[:], scalar1=-1.0))
        op(ACT, [B_k], [B_k], lambda e: e.activation(out=ifr[:], in_=pm[:], func=AF.Exp,
                                                     scale=-math.log(10000.0) / 64.0))
        aneg = sb("aneg", [128, 8], F32)
        op(ACT, [B_c], [B_k], lambda e: e.activation(out=aneg[:], in_=alog[:], func=AF.Exp))
        op(ACT, [B_k], [B_k], lambda e: e.mul(out=aneg[:], in_=aneg[:], mul=-1.0))
        lf = sb("lf", [128, 2], F32); lb = sb("lb", [128, 2], F32)
        LN2 = math.log(2.0)
        for (dst, off) in ((lf, 5.0), (lb, 5.5)):
            op(ACT, [B_c, B_k], [B_k], lambda e, dst=dst, off=off: e.activation(
                out=dst[:], in_=hh[:], func=AF.Exp, scale=-LN2, bias=-LN2 * off))
            op(ACT, [B_k], [B_k], lambda e, dst=dst: e.activation(
                out=dst[:], in_=dst[:], func=AF.Ln, scale=-1.0, bias=1.0))
        maskT = sb("maskT", [128, 2, 128], F32)
        kfc = sb("kfc", [128, 2], F32); kbc = sb("kbc", [128, 2], F32)
        qfr = sb("qfr", [128, 2, 128], F32); qbr = sb("qbr", [128, 2, 128], F32)
        decf = sb("decf", [128, 2], F32); decb = sb("decb", [128, 2], F32)
        posd = sb("posd", [128, 128], F32); negd = sb("negd", [128, 128], F32)
        op(DVE, [B_k], [B_k], lambda e: e.tensor_scalar_max(out=posd[:], in0=dmat[:], scalar1=0.0))
        op(DVE, [B_k], [B_k], lambda e: e.tensor_sub(out=negd[:], in0=posd[:], in1=dmat[:]))
        jr = sb("jr", [128, 1], F32)
        op(DVE, [B_k], [B_k], lambda e: e.tensor_scalar(out=jr[:], in0=pcol[:], scalar1=-1.0, scalar2=127.0,
                                                        op0=ALU.mult, op1=ALU.add))
        ip1 = sb("ip1", [128, 128], F32); cmi = sb("cmi", [128, 128], F32)
        op(DVE, [B_k], [B_k], lambda e: e.tensor_scalar_add(out=ip1[:], in0=irow[:], scalar1=1.0))
        op(DVE, [B_k], [B_k], lambda e: e.tensor_scalar(out=cmi[:], in0=irow[:], scalar1=-1.0, scalar2=128.0,
                                                        op0=ALU.mult, op1=ALU.add))
        for h in range(2):
            op(DVE, [B_k], [B_k], lambda e, h=h: e.tensor_scalar_mul(out=tmpc[:], in0=posd[:], scalar1=lf[:, h:h + 1]))
            op(DVE, [B_k], [B_k], lambda e, h=h: e.scalar_tensor_tensor(
                out=tmpc[:], in0=negd[:], scalar=lb[:, h:h + 1], in1=tmpc[:], op0=ALU.mult, op1=ALU.add))
            op(ACT, [B_k], [B_k], lambda e, h=h: e.activation(out=maskT[:, h, :], in_=tmpc[:], func=AF.Exp))
            op(ACT, [B_k], [B_k], lambda e, h=h: e.activation(out=kfc[:, h:h + 1], in_=jr[:], func=AF.Exp,
                                                               scale=lf[:, h:h + 1]))
            op(ACT, [B_k], [B_k], lambda e, h=h: e.activation(out=kbc[:, h:h + 1], in_=pcol[:], func=AF.Exp,
                                                               scale=lb[:, h:h + 1]))
            op(ACT, [B_k], [B_k], lambda e, h=h: e.activation(out=qfr[:, h, :], in_=ip1[:], func=AF.Exp,
                                                               scale=lf[:, h:h + 1]))
            op(ACT, [B_k], [B_k], lambda e, h=h: e.activation(out=qbr[:, h, :], in_=cmi[:], func=AF.Exp,
                                                               scale=lb[:, h:h + 1]))
        op(ACT, [B_k], [B_k], lambda e: e.activation(out=decf[:], in_=lf[:], func=AF.Exp, scale=128.0))
        op(ACT, [B_k], [B_k], lambda e: e.activation(out=decb[:], in_=lb[:], func=AF.Exp, scale=128.0))

        epsc = sb("epsc", [128, 1], F32)
        op(POOL, [], [B_k], lambda e: e.memset(epsc[:], EPS))
        xs = sb("xs", [128, 16, TW], F32); B_xs = Buf("xs")
        sq = sb("sq", [128, 16, TW], BF16); B_sq = Buf("sq")
        hn = sq; B_hn = B_sq
        rstd = sb("rstd", [128, TW], F32); B_rstd = Buf("rstd")
        posi = sb("posi", [128, 512], I32); B_posi = Buf("posi")
        ang = sb("ang", [128, 512], F32); ang2 = sb("ang2", [128, 512], F32)
        cosT = sb("cosT", [128, 512], F32); sinT = sb("sinT", [128, 512], F32); B_cs = Buf("cossin")
        rawb = sb("rawb", [128, 512], BF16); B_rawb = Buf("rawb")
        rt1 = sb("rt1", [128, 512], F32); B_rt1 = Buf("rt1")
        qT = sb("qT", [128, 2, 512], BF16); kT = sb("kT", [128, 2, 512], BF16)
        qfT = sb("qfT", [128, 2, 512], BF16); qbT = sb("qbT", [128, 2, 512], BF16)
        B_qT = Buf("qT"); B_kT = Buf("kT"); B_qfb = Buf("qfb")
        rawc = sb("rawc", [128, TW], F32); B_rawc = Buf("rawc")
        cacc = sb("cacc", [128, 512], F32); B_cacc = Buf("cacc")
        xbcT = sb("xbcT", [128, 4, 512], BF16); B_xbcT = [Buf(f"xbcT{i}") for i in range(4)]
        vtm = sb("vtm", [128, 4, 256], BF16); gtm = sb("gtm", [128, 4, 256], BF16); ztm = sb("ztm", [128, 4, 256], BF16)
        B_vtm = Buf("vtm"); B_gtm = Buf("gtm"); B_ztm = Buf("ztm")
        dtr = sb("dtr", [128, 4, 8], F32); dtv = sb("dtv", [128, 4, 8], F32); adt = sb("adt", [128, 4, 8], F32)
        B_dt = Buf("dt")
        ktm = sb("ktm", [128, 4, 2, 128], BF16); B_ktm = Buf("ktm")
        xtm = sb("xtm", [128, 4, 256], BF16); B_xtm = Buf("xtm")
        btm = sb("btm", [128, 4, 128], BF16); B_btm = Buf("btm")
        nfc = sb("nfc", [128, 8], F32); dE = sb("dE", [128, 8], F32); dI = sb("dI", [128, 8], F32)
        decs = sb("decs", [128, 8], F32); wE = sb("wE", [128, 8], F32); B_sm = Buf("ssd_small")
        Lm = sb("Lm", [128, 8, 128], F32); B_Lm = Buf("Lm")
        cbm = sb("cbm", [128, 2, 128], F32); B_cbm = Buf("cbm")
        MT = sb("MT", [128, 8, 128], BF16); B_MT = Buf("MT")
        xdt = sb("xdt", [128, 8, 64], BF16); xE = sb("xE", [128, 8, 64], BF16); B_xdt = Buf("xdt"); B_xE = Buf("xE")
        Hf = sb("Hf", [128, 256], F32); Hfb = sb("Hfb", [128, 256], BF16); B_Hf = Buf("Hf"); B_Hfb = Buf("Hfb")
        Hb = sb("Hb", [128, 256], F32); Hbb = sb("Hbb", [128, 256], BF16); B_Hb = Buf("Hb"); B_Hbb = Buf("Hbb")
        Htmp = sb("Htmp", [128, 256], F32); B_Htmp = Buf("Htmp")
        rf = sb("rf", [128, 2, 128], F32); rfb = sb("rfb", [128, 2, 128], BF16); B_rf = Buf("rf"); B_rfb = Buf("rfb")
        rbk = sb("rbk", [128, 2, 128], F32); rbb = sb("rbb", [128, 2, 128], BF16); B_rb = Buf("rb"); B_rbb = Buf("rbb")
        SmT = sb("SmT", [128, 2, 128], BF16); B_SmT = Buf("SmT")
        ssq = sb("ssq", [128, 2], F32); rs2 = sb("rs2", [128, 2], F32); B_ssq = Buf("ssq")
        junk = sb("junk", [128, 256], F32); B_junk = Buf("junk")
        rtmp = sb("rtmp", [128, 256], F32); B_rtmp = Buf("rtmp")
        yt = sb("yt", [128, 256], F32); yt2 = sb("yt2", [128, 256], F32); B_yt = Buf("yt")
        mixtm = sb("mixtm", [128, 4, 512], BF16); B_mixtm = Buf("mixtm")
        mixT = sb("mixT", [128, 4, 512], BF16); B_mixT = Buf("mixT")

        P0 = ps("P0", [128, 512], F32); P1 = ps("P1", [128, 512], F32); P2 = ps("P2", [128, 512], F32)
        P3 = ps("P3", [128, 512], F32); P4 = ps("P4", [128, 512], F32); P56 = ps("P56", [128, 1024], F32)
        P7 = ps("P7", [128, 512], F32)
        B_P0 = Buf("P0", True); B_P1 = Buf("P1", True); B_P2 = Buf("P2", True); B_P3 = Buf("P3", True); B_P4 = Buf("P4", True)
        B_P56 = Buf("P56", True); B_P7 = Buf("P7", True)
        P4b = P4[:].bitcast(BF16)

        zpad = sb("zpad", [128, 4, 1], BF16); B_zpad = Buf("zpad")
        op(POOL, [], [B_zpad], lambda e: e.memset(zpad[:], 0.0))
        op(POOL, [], [B_Hf], lambda e: e.memset(Hf[:], 0.0))
        op(POOL, [], [B_Hfb], lambda e: e.memset(Hfb[:], 0.0))
        op(POOL, [], [B_Hb], lambda e: e.memset(Hb[:], 0.0))
        op(POOL, [], [B_Hbb], lambda e: e.memset(Hbb[:], 0.0))
        op(POOL, [], [B_rf], lambda e: e.memset(rf[:], 0.0))
        op(POOL, [], [B_rfb], lambda e: e.memset(rfb[:], 0.0))
        op(POOL, [], [B_rb], lambda e: e.memset(rbk[:], 0.0))
        op(POOL, [], [B_rbb], lambda e: e.memset(rbb[:], 0.0))

        def mm_group(out_buf, out_ap, pairs, reads, first=True, last=True):
            n = len(pairs)
            for i, (l_, r_) in enumerate(pairs):
                st = first and i == 0
                sp_ = last and i == n - 1
                op(PE, reads, [out_buf], lambda e, l_=l_, r_=r_, st=st, sp_=sp_: e.matmul(
                    out_ap, l_, r_, start=st, stop=sp_), mark=(i == n - 1), wait=(i == 0))

        def tile_body(t, pas):
            c0 = 512 * t
            for k4 in range(0, 16, 4):
                dma(SP, B_xs, B_in, xs[:, k4:k4 + 4, :], xTp[:, k4:k4 + 4, c0:c0 + TW])
            chk(1)
            for k in range(16):
                op(ACT, [B_xs], [B_sq], lambda e, k=k: e.activation(out=sq[:, k, :], in_=xs[:, k, :], func=AF.Square))
            chk(2)
            mm_group(B_P4, P4[:, 0:512], [(onesb[:], sq[:, k, 0:512]) for k in range(16)], [B_sq, B_k])
            mm_group(B_P1, P1[:, 0:32], [(onesb[:], sq[:, k, TW - 32:TW]) for k in range(16)], [B_sq, B_k])
            chk(2.1)
            epsb = EPS
            op(ACT, [B_P4], [B_rstd], lambda e: e.activation(out=(ang[:, 0:512] if os.environ.get('KALT') else rstd[:, 0:512]), in_=P4[:, 0:512], func=(AF.Copy if os.environ.get('KALT2') else AF.Ln), scale=1.0 / D, bias=(0.0 if os.environ.get('KALT2') else epsc[:, 0:1])))
            if not os.environ.get('KSKIP'):
                op(ACT, [B_P1], [B_rstd], lambda e: e.activation(out=rstd[:, TW - 32:TW], in_=P1[:, 0:32], func=AF.Ln, scale=1.0 / D, bias=epsc[:, 0:1]))
            chk(2.2)
            op(ACT, [B_rstd], [B_rstd], lambda e: e.activation(out=rstd[:], in_=rstd[:], func=AF.Exp, scale=-0.5))
            chk(2.3)
            for k in range(16):
                if k % 2 == 0:
                    op(DVE, [B_xs, B_rstd, B_c], [B_hn], lambda e, k=k: e.scalar_tensor_tensor(
                        out=hn[:, k, :], in0=xs[:, k, :], scalar=nmw[:, k:k + 1], in1=rstd[:], op0=ALU.mult, op1=ALU.mult))
                else:
                    op(POOL, [B_rstd], [B_xs], lambda e, k=k: e.tensor_tensor(out=xs[:, k, :], in0=xs[:, k, :], in1=rstd[:], op=ALU.mult))
                    op(POOL, [B_xs, B_c], [B_hn], lambda e, k=k: e.tensor_scalar_mul(out=hn[:, k, :], in0=xs[:, k, :], scalar1=nmw[:, k:k + 1]))
            chk(3)
            dma(SP, B_posi, B_in, posi[:], pos_d[0:1, c0:c0 + 512].partition_broadcast(128))
            TWO_PI = 2 * math.pi
            op(DVE, [B_posi], [B_cs], lambda e: e.tensor_copy(out=ang[:], in_=posi[:]))
            op(DVE, [B_cs, B_k], [B_cs], lambda e: e.tensor_scalar_mul(out=ang[:], in0=ang[:], scalar1=ifr[:, 0:1]))
            op(DVE, [B_cs], [B_cs], lambda e: e.tensor_scalar_mul(out=ang2[:], in0=ang[:], scalar1=1.0 / TWO_PI))
            op(DVE, [B_cs], [B_posi], lambda e: e.tensor_copy(out=posi[:], in_=ang2[:]))
            op(DVE, [B_posi], [B_cs], lambda e: e.tensor_copy(out=ang2[:], in_=posi[:]))
            op(DVE, [B_cs], [B_cs], lambda e: e.scalar_tensor_tensor(out=ang[:], in0=ang2[:], scalar=-TWO_PI, in1=ang[:], op0=ALU.mult, op1=ALU.add))
            op(DVE, [B_cs], [B_cs], lambda e: e.tensor_scalar(out=ang2[:], in0=ang[:], scalar1=math.pi, scalar2=-TWO_PI, op0=ALU.is_gt, op1=ALU.mult))
            op(DVE, [B_cs], [B_cs], lambda e: e.tensor_add(out=ang[:], in0=ang[:], in1=ang2[:]))
            op(DVE, [B_cs], [B_cs], lambda e: e.tensor_scalar(out=ang2[:], in0=ang[:], scalar1=-math.pi, scalar2=TWO_PI, op0=ALU.is_lt, op1=ALU.mult))
            op(DVE, [B_cs], [B_cs], lambda e: e.tensor_add(out=ang[:], in0=ang[:], in1=ang2[:]))
            op(DVE, [B_cs], [B_cs], lambda e: e.tensor_scalar_add(out=ang2[:], in0=ang[:], scalar1=math.pi / 2))
            op(DVE, [B_cs], [B_cs], lambda e: e.tensor_scalar(out=cosT[:], in0=ang2[:], scalar1=math.pi, scalar2=-TWO_PI, op0=ALU.is_gt, op1=ALU.mult))
            op(DVE, [B_cs], [B_cs], lambda e: e.tensor_add(out=ang2[:], in0=ang2[:], in1=cosT[:]))
            op(ACT, [B_cs], [B_cs], lambda e: e.activation(out=sinT[:], in_=ang[:], func=AF.Sin))
            op(ACT, [B_cs], [B_cs], lambda e: e.activation(out=cosT[:], in_=ang2[:], func=AF.Sin))
            op(DVE, [B_cs, B_k], [B_cs], lambda e: e.tensor_scalar_mul(out=sinT[:], in0=sinT[:], scalar1=sgn[:, 0:1]))

            chk(4)
            def fm_block(bi, conv):
                if conv:
                    mm_group(B_P0, P0[:, 0:512], [(wfm[:, k, bi * 128:(bi + 1) * 128], hn[:, k, 0:512]) for k in range(16)],
                             [B_hn, B_wfm])
                    mm_group(B_P1, P1[:, 32:64], [(wfm[:, k, bi * 128:(bi + 1) * 128], hn[:, k, TW - 32:TW]) for k in range(16)],
                             [B_hn, B_wfm])
                else:
                    mm_group(B_P0, P0[:, 0:512], [(wfm[:, k, bi * 128:(bi + 1) * 128], hn[:, k, 2:514]) for k in range(16)],
                             [B_hn, B_wfm])

            def rotary(bi, dst, dbuf, hidx, scale):
                second = (hidx == 1)
                fm_block(bi, False)
                chk(4.45 if second else 4.1)
                op(ACT, [B_P0], [B_rawb], lambda e: e.activation(out=rawb[:], in_=P0[:, 0:512], func=AF.Copy, scale=scale))
                op(DVE, [B_P0, B_cs], [B_rt1], lambda e: e.scalar_tensor_tensor(
                    out=rt1[:], in0=P0[:, 0:512], scalar=scale, in1=cosT[:], op0=ALU.mult, op1=ALU.mult))
                chk(4.46 if second else 4.2)
                mm_group(B_P7, P7[:, 0:512], [(perm[:], rawb[:])], [B_rawb, B_k])
                chk(4.47 if second else 4.25)
                op(DVE, [B_P7, B_cs], [B_cacc], lambda e: e.tensor_tensor(out=cacc[:], in0=P7[:, 0:512], in1=sinT[:], op=ALU.mult))
                chk(4.48 if second else 4.3)
                op(DVE, [B_rt1, B_cacc], [dbuf], lambda e: e.tensor_add(out=dst[:, hidx, :], in0=rt1[:], in1=cacc[:]))
                chk(4.49 if second else 4.4)

            if pas == 2:
                for h in range(2):
                    rotary(h, qT, B_qT, h, 1.0)
                    for c in range(4):
                        op(POOL, [B_qT, B_k], [B_qfb], lambda e, h=h, c=c: e.tensor_tensor(
                            out=qfT[:, h, c * 128:(c + 1) * 128], in0=qT[:, h, c * 128:(c + 1) * 128], in1=qfr[:, h, :], op=ALU.mult))
                        op(POOL, [B_qT, B_k], [B_qfb], lambda e, h=h, c=c: e.tensor_tensor(
                            out=qbT[:, h, c * 128:(c + 1) * 128], in0=qT[:, h, c * 128:(c + 1) * 128], in1=qbr[:, h, :], op=ALU.mult))
            for h in range(2):
                rotary(2 + h, kT, B_kT, h, 128.0 ** -0.5)
            chk(4.5)
            conv_blocks = [0, 1, 2] if pas == 1 else [0, 1, 2, 3]
            for ci in conv_blocks:
                fm_block(4 + ci, True)
                chk(4.6)
                op(ACT, [B_P0], [B_rawc], lambda e: e.activation(out=rawc[:, 0:512], in_=P0[:, 0:512], func=AF.Copy))
                op(ACT, [B_P1], [B_rawc], lambda e: e.activation(out=rawc[:, 512:516], in_=P1[:, 60:64], func=AF.Copy))
                chk(4.7)
                op(DVE, [B_rawc, B_c], [B_cacc], lambda e, ci=ci: e.tensor_scalar_mul(
                    out=cacc[:], in0=rawc[:, 0:512], scalar1=cw[:, ci, 0:1]))
                for j in range(1, 5):
                    op(DVE, [B_rawc, B_c], [B_cacc], lambda e, ci=ci, j=j: e.scalar_tensor_tensor(
                        out=cacc[:], in0=rawc[:, j:j + 512], scalar=cw[:, ci, j:j + 1], in1=cacc[:], op0=ALU.mult, op1=ALU.add))
                chk(4.8)
                op(ACT, [B_cacc, B_c], [B_xbcT[ci]], lambda e, ci=ci: e.activation(
                    out=xbcT[:, ci, :], in_=cacc[:], func=AF.Silu, bias=cb[:, ci:ci + 1], scale=1.0))

            chk(5)
            for c in range(4):
                lo = 2 + 128 * c
                if pas == 1:
                    mm_group(B_P2, P2[:, 0:256], [(hn[:, k, lo:lo + 128], wtm[:, k, 0:256]) for k in range(16)], [B_hn, B_wtm])
                    mm_group(B_P3, P3[:, 0:8], [(hn[:, k, lo:lo + 128], wtm[:, k, 512:520]) for k in range(16)], [B_hn, B_wtm])
                else:
                    mm_group(B_P2, P2[:, 0:512], [(hn[:, k, lo:lo + 128], wtm[:, k, 0:512]) for k in range(16)], [B_hn, B_wtm])
                    mm_group(B_P3, P3[:, 0:264], [(hn[:, k, lo:lo + 128], wtm[:, k, 512:776]) for k in range(16)], [B_hn, B_wtm])
                op(ACT, [B_P2], [B_vtm], lambda e, c=c: e.activation(out=vtm[:, c, :], in_=P2[:, 0:256], func=AF.Copy))
                op(DVE, [B_P3], [B_dt], lambda e, c=c: e.tensor_copy(out=dtr[:, c, :], in_=P3[:, 0:8]))
                if pas == 2:
                    op(ACT, [B_P2], [B_gtm], lambda e, c=c: e.activation(out=gtm[:, c, :], in_=P2[:, 256:512], func=AF.Silu))
                    op(ACT, [B_P3], [B_ztm], lambda e, c=c: e.activation(out=ztm[:, c, :], in_=P3[:, 8:264], func=AF.Silu))
            op(DVE, [B_dt, B_c], [B_dt], lambda e: e.tensor_tensor(out=dtr[:], in0=dtr[:], in1=bc(dtb[:].unsqueeze(1), [128, 4, 8]), op=ALU.add))
            op(ACT, [B_dt], [B_dt], lambda e: e.activation(out=dtr[:], in_=dtr[:], func=AF.Exp))
            op(ACT, [B_dt], [B_dt], lambda e: e.activation(out=dtv[:], in_=dtr[:], func=AF.Ln, bias=1.0, scale=1.0))
            op(DVE, [B_dt, B_k], [B_dt], lambda e: e.tensor_tensor(out=adt[:], in0=dtv[:], in1=bc(aneg[:].unsqueeze(1), [128, 4, 8]), op=ALU.mult))

            chk(6)
            kw = kbc if pas == 1 else kfc
            for c in range(4):
                for h in range(2):
                    op(PE, [B_kT, B_k], [B_P4], lambda e, c=c, h=h: e.transpose(
                        P4b[:, (c * 2 + h) * 128:(c * 2 + h + 1) * 128], kT[:, h, c * 128:(c + 1) * 128], ident[:]))
            for c in range(4):
                for h in range(2):
                    op(DVE, [B_P4, B_k], [B_ktm], lambda e, c=c, h=h: e.tensor_scalar_mul(
                        out=ktm[:, c, h, :], in0=P4b[:, (c * 2 + h) * 128:(c * 2 + h + 1) * 128], scalar1=kw[:, h:h + 1]))
            for c in range(4):
                for bl in range(2):
                    op(PE, [B_xbcT[bl], B_k], [B_P4], lambda e, c=c, bl=bl: e.transpose(
                        P4b[:, (c * 2 + bl) * 128:(c * 2 + bl + 1) * 128], xbcT[:, bl, c * 128:(c + 1) * 128], ident[:]))
            op(ACT, [B_P4], [B_xtm], lambda e: e.activation(out=xtm[:].rearrange("p c f -> p (c f)"), in_=P4b[:, 0:1024], func=AF.Copy))
            for c in range(4):
                op(PE, [B_xbcT[2], B_k], [B_P4], lambda e, c=c: e.transpose(
                    P4b[:, c * 128:(c + 1) * 128], xbcT[:, 2, c * 128:(c + 1) * 128], ident[:]))
            op(ACT, [B_P4], [B_btm], lambda e: e.activation(out=btm[:].rearrange("p c f -> p (c f)"), in_=P4b[:, 0:512], func=AF.Copy))

            chk(7)
            chunks = [3, 2, 1, 0] if pas == 1 else [0, 1, 2, 3]
            for c in chunks:
                gc = 4 * t + c
                cs = slice(c * 128, (c + 1) * 128)
                mm_group(B_P1, P1[:, 16:20], [(TI[:], adt[:, c, 0:4])], [B_dt, B_k])
                mm_group(B_P1, P1[:, 20:24], [(TS[:], adt[:, c, 4:8])], [B_dt, B_k])
                mm_group(B_P1, P1[:, 24:28], [(TSs[:], adt[:, c, 0:4])], [B_dt, B_k])
                mm_group(B_P1, P1[:, 28:32], [(TIs[:], adt[:, c, 4:8])], [B_dt, B_k])
                mm_group(B_P1, P1[:, 32:40], [(onesf[:], adt[:, c, :])], [B_dt, B_k])
                op(DVE, [B_P1], [B_sm], lambda e: e.tensor_scalar_mul(out=nfc[:], in0=P1[:, 16:24], scalar1=-1.0))
                op(ACT, [B_P1], [B_sm], lambda e: e.activation(out=dI[:], in_=P1[:, 16:24], func=AF.Exp))
                op(ACT, [B_P1], [B_sm], lambda e: e.activation(out=dE[:], in_=P1[:, 24:32], func=AF.Exp))
                op(ACT, [B_P1], [B_sm], lambda e: e.activation(out=decs[:], in_=P1[:, 32:40], func=AF.Exp))
                op(DVE, [B_sm, B_dt], [B_sm], lambda e, c=c: e.tensor_mul(out=wE[:], in0=dE[:], in1=dtv[:, c, :]))
                dsel = slice(4, 8) if pas == 1 else slice(0, 4)
                op(DVE, [B_xtm, B_sm], [B_xE], lambda e, c=c: e.tensor_tensor(
                    out=xE[:, 0:4, :], in0=xtm[:, c, :].rearrange("p (h f) -> p h f", h=4),
                    in1=bc(wE[:, dsel].unsqueeze(2), [128, 4, 64]), op=ALU.mult))
                if pas == 1:
                    dma(SP, B_hbs[gc], B_Hbb, hb_scr.ap()[gc, :, :], Hbb[:])
                    dma(SP, B_rbs[gc], B_rbb, rb_scr.ap()[gc, :, :], rbb[:].rearrange("p h f -> p (h f)"))
                    mm_group(B_P2, P2[:, 0:256], [(btm[:, c, :], xE[:, 0:4, :].rearrange("p h f -> p (h f)"))], [B_btm, B_xE])
                    op(DVE, [B_Hb, B_sm], [B_Htmp], lambda e: e.tensor_tensor(
                        out=Htmp[:].rearrange("p (h f) -> p h f", h=4), in0=Hb[:].rearrange("p (h f) -> p h f", h=4),
                        in1=bc(decs[:, 4:8].unsqueeze(2), [128, 4, 64]), op=ALU.mult))
                    op(DVE, [B_Htmp, B_P2], [B_Hb], lambda e: e.tensor_add(out=Hb[:], in0=Htmp[:], in1=P2[:, 0:256]))
                    op(ACT, [B_Hb], [B_Hbb], lambda e: e.activation(out=Hbb[:], in_=Hb[:], func=AF.Copy))
                    for h in range(2):
                        mm_group(B_P0, P0[:, h * 128:(h + 1) * 128], [(ktm[:, c, h, :], vtm[:, c, h * 128:(h + 1) * 128])], [B_ktm, B_vtm])
                    for h in range(2):
                        op(DVE, [B_rb, B_P0, B_k], [B_rb], lambda e, h=h: e.scalar_tensor_tensor(
                            out=rbk[:, h, :], in0=rbk[:, h, :], scalar=decb[:, h:h + 1], in1=P0[:, h * 128:(h + 1) * 128],
                            op0=ALU.mult, op1=ALU.add))
                    op(ACT, [B_rb], [B_rbb], lambda e: e.activation(out=rbb[:], in_=rbk[:], func=AF.Copy))
                    continue
                dma(SP, B_Hbb, B_hbs[gc], Hbb[:], hb_scr.ap()[gc, :, :])
                dma(SP, B_rbb, B_rbs[gc], rbb[:].rearrange("p h f -> p (h f)"), rb_scr.ap()[gc, :, :])
                for hd in range(8):
                    tri = TI if hd < 4 else TS
                    mm_group(B_P56, P56[:, hd * 128:(hd + 1) * 128], [(bc(adt[:, c, hd:hd + 1], [128, 128]), tri[:])], [B_dt, B_k])
                for hd in range(8):
                    op(DVE, [B_P56, B_sm], [B_Lm], lambda e, hd=hd: e.tensor_scalar(
                        out=Lm[:, hd, :], in0=P56[:, hd * 128:(hd + 1) * 128], scalar1=nfc[:, hd:hd + 1], scalar2=0.0,
                        op0=ALU.add, op1=ALU.min))
                op(ACT, [B_Lm], [B_Lm], lambda e: e.activation(out=Lm[:], in_=Lm[:], func=AF.Exp))
                mm_group(B_P7, P7[:, 0:128], [(xbcT[:, 2, cs], xbcT[:, 3, cs])], [B_xbcT[2], B_xbcT[3]])
                op(DVE, [B_P7, B_k], [B_cbm], lambda e: e.tensor_tensor(out=cbm[:, 0, :], in0=P7[:, 0:128], in1=TI[:], op=ALU.mult))
                op(DVE, [B_P7, B_k], [B_cbm], lambda e: e.tensor_tensor(out=cbm[:, 1, :], in0=P7[:, 0:128], in1=TSs[:], op=ALU.mult))
                for d_ in range(2):
                    op(DVE, [B_Lm, B_cbm], [B_MT], lambda e, d_=d_: e.scalar_tensor_tensor(
                        out=MT[:, 4 * d_:4 * d_ + 4, :], in0=Lm[:, 4 * d_:4 * d_ + 4, :], scalar=1.0,
                        in1=bc(cbm[:, d_:d_ + 1, :], [128, 4, 128]), op0=ALU.mult, op1=ALU.mult))
                    op(POOL, [B_xtm, B_dt], [B_xdt], lambda e, d_=d_, c=c: e.tensor_tensor(
                        out=xdt[:, 4 * d_:4 * d_ + 4, :], in0=xtm[:, c, :].rearrange("p (h f) -> p h f", h=4),
                        in1=bc(dtv[:, c, 4 * d_:4 * d_ + 4].unsqueeze(2), [128, 4, 64]), op=ALU.mult))
                for h in range(4):
                    mm_group(B_P2, P2[:, h * 64:(h + 1) * 64], [(MT[:, h, :], xdt[:, h, :]), (MT[:, 4 + h, :], xdt[:, 4 + h, :])],
                             [B_MT, B_xdt])
                mm_group(B_P3, P3[:, 0:256], [(xbcT[:, 3, cs], Hfb[:])], [B_xbcT[3], B_Hfb])
                mm_group(B_P3, P3[:, 256:512], [(xbcT[:, 3, cs], Hbb[:])], [B_xbcT[3], B_Hbb])
                mm_group(B_P2, P2[:, 256:512], [(btm[:, c, :], xE[:, 0:4, :].rearrange("p h f -> p (h f)"))], [B_btm, B_xE])
                op(DVE, [B_Hf, B_sm], [B_Htmp], lambda e: e.tensor_tensor(
                    out=Htmp[:].rearrange("p (h f) -> p h f", h=4), in0=Hf[:].rearrange("p (h f) -> p h f", h=4),
                    in1=bc(decs[:, 0:4].unsqueeze(2), [128, 4, 64]), op=ALU.mult))
                op(DVE, [B_Htmp, B_P2], [B_Hf], lambda e: e.tensor_add(out=Hf[:], in0=Htmp[:], in1=P2[:, 256:512]))
                op(ACT, [B_Hf], [B_Hfb], lambda e: e.activation(out=Hfb[:], in_=Hf[:], func=AF.Copy))
                v3 = lambda a: a.rearrange("p (h f) -> p h f", h=4)
                op(DVE, [B_P3, B_sm], [B_yt], lambda e: e.tensor_tensor(
                    out=v3(yt[:]), in0=v3(P3[:, 0:256]), in1=bc(dI[:, 0:4].unsqueeze(2), [128, 4, 64]), op=ALU.mult))
                op(DVE, [B_P3, B_sm], [B_yt], lambda e: e.tensor_tensor(
                    out=v3(yt2[:]), in0=v3(P3[:, 256:512]), in1=bc(dI[:, 4:8].unsqueeze(2), [128, 4, 64]), op=ALU.mult))
                op(DVE, [B_yt], [B_yt], lambda e: e.tensor_add(out=yt[:], in0=yt[:], in1=yt2[:]))
                op(DVE, [B_xtm, B_c], [B_yt], lambda e, c=c: e.tensor_tensor(
                    out=v3(yt2[:]), in0=v3(xtm[:, c, :]), in1=bc(dsk[:].unsqueeze(2), [128, 4, 64]), op=ALU.mult))
                op(DVE, [B_yt], [B_yt], lambda e: e.tensor_add(out=yt[:], in0=yt[:], in1=yt2[:]))
                op(DVE, [B_yt, B_P2], [B_yt], lambda e: e.tensor_add(out=yt[:], in0=yt[:], in1=P2[:, 0:256]))
                op(DVE, [B_yt, B_ztm], [B_mixtm], lambda e, c=c: e.tensor_mul(out=mixtm[:, c, 256:512], in0=yt[:], in1=ztm[:, c, :]))
                for h in range(2):
                    mm_group(B_P7, P7[:, 128 + h * 128:256 + h * 128], [(kT[:, h, cs], qT[:, h, cs])], [B_kT, B_qT])
                for h in range(2):
                    op(DVE, [B_P7, B_k], [B_SmT], lambda e, h=h: e.tensor_tensor(
                        out=SmT[:, h, :], in0=P7[:, 128 + h * 128:256 + h * 128], in1=maskT[:, h, :], op=ALU.mult))
                for h in range(2):
                    hs = slice(h * 128, (h + 1) * 128)
                    mm_group(B_P0, P0[:, hs], [(SmT[:, h, :], vtm[:, c, hs]), (qfT[:, h, cs], rfb[:, h, :]), (qbT[:, h, cs], rbb[:, h, :])],
                             [B_SmT, B_vtm, B_qfb, B_rfb, B_rbb])
                    mm_group(B_P0, P0[:, 256 + h * 128:384 + h * 128], [(ktm[:, c, h, :], vtm[:, c, hs])], [B_ktm, B_vtm])
                for h in range(2):
                    op(DVE, [B_rf, B_P0, B_k], [B_rf], lambda e, h=h: e.scalar_tensor_tensor(
                        out=rf[:, h, :], in0=rf[:, h, :], scalar=decf[:, h:h + 1], in1=P0[:, 256 + h * 128:384 + h * 128],
                        op0=ALU.mult, op1=ALU.add))
                op(ACT, [B_rf], [B_rfb], lambda e: e.activation(out=rfb[:], in_=rf[:], func=AF.Copy))
                op(ACT, [B_P0], [B_junk], lambda e: e.activation(out=junk[:], in_=P0[:, 0:256], func=AF.Square))
                op(DVE, [B_junk], [B_ssq], lambda e: e.reduce_sum(out=ssq[:], in_=junk[:].rearrange("p (h f) -> p h f", h=2), axis=mybir.AxisListType.X))
                op(DVE, [B_ssq], [B_ssq], lambda e: e.tensor_scalar(out=rs2[:], in0=ssq[:], scalar1=1.0 / 128.0, scalar2=EPS,
                                                                    op0=ALU.mult, op1=ALU.add))
                op(ACT, [B_ssq], [B_ssq], lambda e: e.activation(out=rs2[:], in_=rs2[:], func=AF.Ln))
                op(ACT, [B_ssq], [B_ssq], lambda e: e.activation(out=rs2[:], in_=rs2[:], func=AF.Exp, scale=-0.5))
                for h in range(2):
                    hs = slice(h * 128, (h + 1) * 128)
                    op(DVE, [B_P0, B_ssq, B_c], [B_rtmp], lambda e, h=h, hs=hs: e.scalar_tensor_tensor(
                        out=rtmp[:, hs], in0=P0[:, hs], scalar=rs2[:, h:h + 1], in1=rnw[:, hs], op0=ALU.mult, op1=ALU.mult))
                op(DVE, [B_rtmp, B_gtm], [B_mixtm], lambda e, c=c: e.tensor_mul(out=mixtm[:, c, 0:256], in0=rtmp[:], in1=gtm[:, c, :]))
            if pas == 2:
                for half in range(2):
                    for c in range(4):
                        for fb in range(2):
                            f = half * 2 + fb
                            op(PE, [B_mixtm, B_k], [B_P4], lambda e, c=c, f=f, fb=fb: e.transpose(
                                P4b[:, fb * 512 + c * 128:fb * 512 + (c + 1) * 128], mixtm[:, c, f * 128:(f + 1) * 128], ident[:]))
                    op(ACT, [B_P4], [B_mixT], lambda e, half=half: e.activation(
                        out=mixT[:, 2 * half:2 * half + 2, :].rearrange("p a t -> p (a t)"), in_=P4b[:, 0:1024], func=AF.Copy))
                seg, tq = t // 4, t % 4
                def arows(sg, lo, n):
                    return agin.ap()[sg * 512:(sg + 1) * 512, lo:lo + n].rearrange("(b p) c -> p b c", p=128)
                dma(SP, B_agin, B_mixT, arows(seg, 1 + 512 * tq, 512), mixT[:, :, :])
                if tq == 0:
                    if seg > 0:
                        dma(SP, B_agin, B_mixT, arows(seg - 1, 2049, 1), mixT[:, :, 0:1], slow=True)
                    else:
                        dma(SP, B_agin, B_zpad, arows(0, 0, 1), zpad[:, :, :], slow=True)
                if tq == 3:
                    if seg < 3:
                        dma(SP, B_agin, B_mixT, arows(seg + 1, 0, 1), mixT[:, :, 511:512], slow=True)
                    else:
                        dma(SP, B_agin, B_zpad, arows(3, 2049, 1), zpad[:, :, :], slow=True)

        try:
            chk(0)
            for t in reversed(range(ntiles)):
                tile_body(t, 1)
            chk(10)
            for t in range(ntiles):
                tile_body(t, 2)
        except StopBuild:
            pass

        if debug_half1:
            dma(SP, Buf("dbgout"), B_agin, mix_dbg[:, :], agin.ap()[:, :], is_out=True)
        C.fence()
    C.es = C.es_outer

    if run_half2:
        if run_half1:
            cc_sem = C.new_sem("cc")
            for k in range(16):
                POOL.e.collective_compute("AllGather", ALU.bypass, replica_groups=[[0, 1, 2, 3], [4, 5, 6, 7]],
                                          ins=[agin.ap()[k * 128:(k + 1) * 128, :].opt()],
                                          outs=[agout.ap()[k * 512:(k + 1) * 512, :].opt()]).then_inc(cc_sem)
            tok = ("D", cc_sem, "cc", 16)
            B_agout.w = tok
            B_agout.dtok = tok
        h2 = contextlib.ExitStack()
        C.es = h2
        with h2:
            _half2(nc, C, dict(locals()))
        C.es = C.es_outer

    for tok in C.out_toks:
        SP.wait(tok)


def _half2(nc, C, V):
    PE, ACT, DVE, POOL, SP = C.PE, C.ACT, C.DVE, C.POOL, C.SP
    ACTQ = POOL
    op, dma, sb, ps = C.op, C.dma, C.sb, C.ps
    B_in, B_agout, agout, outT = V["B_in"], V["B_agout"], V["agout"], V["outT"]
    NC2 = 2050
    CT = [(0, 512), (512, 512), (1024, 512), (1536, 512), (2048, 2)]
    h1_scr = nc.dram_tensor("h1_scr", [16, 128, NC2], F32)
    h2_scr = nc.dram_tensor("h2_scr", [16, 128, 2048], F32)
    B_h1s = Buf("h1s"); B_h2s = Buf("h2s"); B_out = Buf("outb")
    wg_c = nc.dram_tensor("wg_c", [NFB, 128, 2048], BF16); wu_c = nc.dram_tensor("wu_c", [NFB, 128, 2048], BF16)
    wd_c = nc.dram_tensor("wd_c", [16, 128, NFB * 128], BF16)
    B_wgc = Buf("wgc"); B_wuc = Buf("wuc"); B_wdc = Buf("wdc")

    def mm_group(out_buf, out_ap, pairs, reads):
        n = len(pairs)
        for i, (l_, r_) in enumerate(pairs):
            op(PE, reads, [out_buf], lambda e, l_=l_, r_=r_, st=(i == 0), sp_=(i == n - 1): e.matmul(
                out_ap, l_, r_, start=st, stop=sp_), mark=(i == n - 1), wait=(i == 0))

    B_c = Buf("c2")
    def small(name, src, shape, dt=F32):
        t = sb(name, shape, dt)
        dma(SP, B_c, B_in, t[:], src)
        return t
    gidx = small("gidx", V["idx_d"][:, :], [128, 16], I32)
    snw = small("snw", V["snw_d"][:, :], [128, 16]); nfw = small("nfw", V["nfw_d"][:, :], [128, 16])
    pnw = small("pnw", V["pnw_d"][:, :], [128, 16]); fnw = small("fnw", V["fnw_d"][:, :], [128, 16])
    bpg = small("bpg", V["bpg_d"][:, :], [128, 16]); fcb = small("fcb", V["fcb_d"][:, :], [128, NFB])
    fcw = small("fcw", V["fcw_d"][:, :, :], [128, NFB, 3])
    B_k = Buf("k2")
    onesf = sb("onesf2", [128, 128], F32); onesb = sb("onesb2", [128, 128], BF16); epsc = sb("epsc2", [128, 1], F32)
    op(POOL, [], [B_k], lambda e: e.memset(onesf[:], 1.0))
    op(POOL, [], [B_k], lambda e: e.memset(onesb[:], 1.0))
    op(POOL, [], [B_k], lambda e: e.memset(epsc[:], EPS))
    pTb = sb("pTb", [128, 2, 2048], BF16); B_pT = Buf("pT")
    dma(POOL, B_pT, B_in, pTb[:], V["pT_d"][:, :, :])

    Q = [ps(f"Q{i}", [128, 512], F32) for i in range(8)]
    B_Q = [Buf(f"Q{i}", True) for i in range(8)]

    sacc = sb("sacc", [128, NC2], F32); B_sacc = Buf("sacc")
    rstd2 = sb("rstd2", [128, NC2], F32); B_rstd2 = Buf("rstd2")
    sqt = sb("sqt", [128, 512], F32); B_sqt = Buf("sqt")

    def rsqrt_cols(src_buf, src_ap, dst_buf, dst_ap, n, scale):
        mm_group(B_Q[6], Q[6][:, 0:n], [(onesf[:], src_ap)], [src_buf, B_k])
        op(ACT, [B_Q[6], B_k], [dst_buf], lambda e: e.activation(out=dst_ap, in_=Q[6][:, 0:n], func=AF.Ln, scale=scale, bias=epsc[:, 0:1]))
        op(ACT, [dst_buf], [dst_buf], lambda e: e.activation(out=dst_ap, in_=dst_ap, func=AF.Exp, scale=-0.5))

    pc = contextlib.ExitStack()
    C.es = pc
    with pc:
        mixg = sb("mixg", [128, 16, NC2], BF16); B_mixg = Buf("mixg")
        for kk in range(16):
            POOL.wait(B_agout.w)
            POOL.wait(B_c.w)
            if B_mixg.dsem is None:
                B_mixg.dsem = C.new_sem("d_mixg")
            B_mixg.dcnt += 1
            POOL.e.indirect_dma_start(out=mixg[:, kk, :], out_offset=None, in_=agout.ap()[:, :],
                                      in_offset=bass.IndirectOffsetOnAxis(ap=gidx[:, kk:kk + 1], axis=0)).then_inc(B_mixg.dsem, 16)
            tok = ("D", B_mixg.dsem, B_mixg.name, 16 * B_mixg.dcnt)
            B_mixg.w = tok; B_mixg.dtok = tok
        sqb = sb("sqb", [128, 4, 512], F32); B_sqb = Buf("sqb")
        rg = sb("rg", [128, 512], F32); B_rg = Buf("rg")
        for g_ in range(2):
            blks = [8 * g_ + 2, 8 * g_ + 3, 8 * g_ + 6, 8 * g_ + 7]
            for (c0, n) in CT:
                for bi, kk in enumerate(blks):
                    op(ACT, [B_mixg], [B_sqb], lambda e, bi=bi, kk=kk: e.activation(out=sqb[:, bi, 0:n], in_=mixg[:, kk, c0:c0 + n], func=AF.Square))
                mm_group(B_Q[6], Q[6][:, 0:n], [(onesf[:], sqb[:, bi, 0:n]) for bi in range(4)], [B_sqb, B_k])
                op(ACT, [B_Q[6], B_k], [B_rg], lambda e: e.activation(out=rg[:, 0:n], in_=Q[6][:, 0:n], func=AF.Ln, scale=1.0 / 512.0, bias=epsc[:, 0:1]))
                op(ACT, [B_rg], [B_rg], lambda e: e.activation(out=rg[:, 0:n], in_=rg[:, 0:n], func=AF.Exp, scale=-0.5))
                for kk in blks:
                    op(DVE, [B_rg, B_c], [B_mixg], lambda e, kk=kk: e.scalar_tensor_tensor(
                        out=mixg[:, kk, c0:c0 + n], in0=mixg[:, kk, c0:c0 + n], scalar=snw[:, kk:kk + 1], in1=rg[:, 0:n], op0=ALU.mult, op1=ALU.mult))
        wo = [sb(f"wo{i}", [128, 16, 128], BF16) for i in range(2)]; B_wo = [Buf(f"wo{i}") for i in range(2)]
        xrow = [sb(f"xrow{i}", [128, NC2], F32) for i in range(2)]; B_xrow = [Buf(f"xrow{i}") for i in range(2)]
        op(POOL, [], [B_sacc], lambda e: e.memset(sacc[:], 0.0))
        for m in range(16):
            s_ = m % 2
            dma(POOL, B_wo[s_], B_in, wo[s_][:].rearrange("p k n -> p (k n)"), V["wout_d"][:, m, :])
            dma(SP, B_xrow[s_], B_in, xrow[s_][:], V["xw_d"][:, m, :])
            for ti, (c0, n) in enumerate(CT):
                qb = 4 + (ti % 2)
                mm_group(B_Q[qb], Q[qb][:, 0:n], [(wo[s_][:, k, :], mixg[:, k, c0:c0 + n]) for k in range(16)], [B_wo[s_], B_mixg])
                op(DVE, [B_Q[qb]], [B_xrow[s_]], lambda e, qb=qb, c0=c0, n=n: e.tensor_add(out=xrow[s_][:, c0:c0 + n], in0=xrow[s_][:, c0:c0 + n], in1=Q[qb][:, 0:n]))
                op(POOL, [B_xrow[s_]], [B_sqt], lambda e, c0=c0, n=n: e.tensor_tensor(out=sqt[:, 0:n], in0=xrow[s_][:, c0:c0 + n], in1=xrow[s_][:, c0:c0 + n], op=ALU.mult))
                op(POOL, [B_sqt], [B_sacc], lambda e, c0=c0, n=n: e.tensor_add(out=sacc[:, c0:c0 + n], in0=sacc[:, c0:c0 + n], in1=sqt[:, 0:n]))
            dma(SP, B_h1s, B_xrow[s_], h1_scr.ap()[m, :, :], xrow[s_][:])
        for (c0, n) in CT:
            rsqrt_cols(B_sacc, sacc[:, c0:c0 + n], B_rstd2, rstd2[:, c0:c0 + n], n, 1.0 / D)
        C.fence()
    C.es = h2_es = V["h2"]

    hst = [sb(f"hst{i}", [128, 4, 514], F32) for i in range(2)]; B_hst = [Buf(f"hst{i}") for i in range(2)]
    hnq = sb("hnq", [128, 16, 514], BF16); B_hnq = Buf("hnq")
    act = sb("act", [128, NFB, 512], BF16); B_act = Buf("act")
    wgb = [sb(f"wgb{i}", [128, 16, 128], BF16) for i in range(2)]; B_wg = [Buf(f"wg{i}") for i in range(2)]
    wub = [sb(f"wub{i}", [128, 16, 128], BF16) for i in range(2)]; B_wu = [Buf(f"wu{i}") for i in range(2)]
    wdb = [sb(f"wdb{i}", [128, NFB, 128], BF16) for i in range(2)]; B_wd = [Buf(f"wd{i}") for i in range(2)]
    wpb = [sb(f"wpb{i}", [128, 2, 128], BF16) for i in range(2)]; B_wp = [Buf(f"wp{i}") for i in range(2)]
    graw = sb("graw", [128, 514], F32); B_graw = Buf("graw")
    gacc = sb("gacc", [128, 512], F32); B_gacc = Buf("gacc")
    hrow = [sb(f"hrow{i}", [128, 512], F32) for i in range(2)]; B_hrow = [Buf(f"hrow{i}") for i in range(2)]
    sacc2 = sb("sacc2", [128, 512], F32); B_sacc2 = Buf("sacc2")
    rstdq = sb("rstdq", [128, 512], F32); B_rstdq = Buf("rstdq")
    h3q = sb("h3q", [128, 16, 512], F32); B_h3q = Buf("h3q")
    sgt = sb("sgt", [128, 512], F32); B_sgt = Buf("sgt")

    for q in range(4):
        w0 = 512 * q
        for s4 in range(4):
            b_ = s4 % 2
            dma(SP, B_hst[b_], B_h1s, hst[b_][:], h1_scr.ap()[4 * s4:4 * s4 + 4, :, w0:w0 + 514].rearrange("k p c -> p k c"))
            for kk in range(4):
                k = 4 * s4 + kk
                op(DVE, [B_hst[b_], B_rstd2, B_c], [B_hnq], lambda e, b_=b_, kk=kk, k=k: e.scalar_tensor_tensor(
                    out=hnq[:, k, :], in0=hst[b_][:, kk, :], scalar=nfw[:, k:k + 1], in1=rstd2[:, w0:w0 + 514], op0=ALU.mult, op1=ALU.mult))
        for f in range(NFB):
            s_ = f % 2
            if q == 0:
                dma(POOL, B_wg[s_], B_in, wgb[s_][:].rearrange("p k n -> p (k n)"), V["wg_d"][:, f, :])
                dma(POOL, B_wu[s_], B_in, wub[s_][:].rearrange("p k n -> p (k n)"), V["wu_d"][:, f, :])
                dma(SP, B_wgc, B_wg[s_], wg_c.ap()[f, :, :], wgb[s_][:].rearrange("p k n -> p (k n)"))
                dma(SP, B_wuc, B_wu[s_], wu_c.ap()[f, :, :], wub[s_][:].rearrange("p k n -> p (k n)"))
            else:
                dma(SP, B_wg[s_], B_wgc, wgb[s_][:].rearrange("p k n -> p (k n)"), wg_c.ap()[f, :, :])
                dma(ACTQ, B_wu[s_], B_wuc, wub[s_][:].rearrange("p k n -> p (k n)"), wu_c.ap()[f, :, :])
            mm_group(B_Q[s_], Q[s_][:, 0:512], [(wgb[s_][:, k, :], hnq[:, k, 0:512]) for k in range(16)], [B_wg[s_], B_hnq])
            mm_group(B_Q[6], Q[6][:, 0:32], [(wgb[s_][:, k, :], hnq[:, k, 482:514]) for k in range(16)], [B_wg[s_], B_hnq])
            mm_group(B_Q[2 + s_], Q[2 + s_][:, 0:512], [(wub[s_][:, k, :], hnq[:, k, 1:513]) for k in range(16)], [B_wu[s_], B_hnq])
            op(ACT, [B_Q[s_]], [B_graw], lambda e, s_=s_: e.activation(out=graw[:, 0:512], in_=Q[s_][:, 0:512], func=AF.Copy))
            op(ACT, [B_Q[6]], [B_graw], lambda e: e.activation(out=graw[:, 512:514], in_=Q[6][:, 30:32], func=AF.Copy))
            op(DVE, [B_graw, B_c], [B_gacc], lambda e, f=f: e.tensor_scalar_mul(out=gacc[:], in0=graw[:, 0:512], scalar1=fcw[:, f, 0:1]))
            for j in (1, 2):
                op(DVE, [B_graw, B_c], [B_gacc], lambda e, f=f, j=j: e.scalar_tensor_tensor(
                    out=gacc[:], in0=graw[:, j:j + 512], scalar=fcw[:, f, j:j + 1], in1=gacc[:], op0=ALU.mult, op1=ALU.add))
            op(ACT, [B_gacc, B_c], [B_gacc], lambda e, f=f: e.activation(out=gacc[:], in_=gacc[:], func=AF.Gelu_apprx_tanh, bias=fcb[:, f:f + 1], scale=1.0))
            op(DVE, [B_gacc, B_Q[2 + s_]], [B_act], lambda e, f=f, s_=s_: e.tensor_tensor(out=act[:, f, :], in0=gacc[:], in1=Q[2 + s_][:, 0:512], op=ALU.mult))
        op(POOL, [], [B_sacc2], lambda e: e.memset(sacc2[:], 0.0))
        for m in range(16):
            s_ = m % 2
            if q == 0:
                dma(POOL, B_wd[s_], B_in, wdb[s_][:].rearrange("p k n -> p (k n)"), V["wd_d"][:, m, :])
                dma(SP, B_wdc, B_wd[s_], wd_c.ap()[m, :, :], wdb[s_][:].rearrange("p k n -> p (k n)"))
            else:
                dma(ACTQ, B_wd[s_], B_wdc, wdb[s_][:].rearrange("p k n -> p (k n)"), wd_c.ap()[m, :, :])
            dma(SP, B_hrow[s_], B_h1s, hrow[s_][:], h1_scr.ap()[m, :, w0 + 1:w0 + 513])
            mm_group(B_Q[4 + s_], Q[4 + s_][:, 0:512], [(wdb[s_][:, f, :], act[:, f, :]) for f in range(NFB)], [B_wd[s_], B_act])
            op(DVE, [B_Q[4 + s_]], [B_hrow[s_]], lambda e, s_=s_: e.tensor_add(out=hrow[s_][:], in0=hrow[s_][:], in1=Q[4 + s_][:, 0:512]))
            op(POOL, [B_hrow[s_]], [B_sqt], lambda e, s_=s_: e.tensor_tensor(out=sqt[:], in0=hrow[s_][:], in1=hrow[s_][:], op=ALU.mult))
            op(POOL, [B_sqt], [B_sacc2], lambda e: e.tensor_add(out=sacc2[:], in0=sacc2[:], in1=sqt[:]))
            dma(SP, B_h2s, B_hrow[s_], h2_scr.ap()[m, :, w0:w0 + 512], hrow[s_][:])
        rsqrt_cols(B_sacc2, sacc2[:], B_rstdq, rstdq[:], 512, 1.0 / D)
        for s4 in range(4):
            b_ = s4 % 2
            dma(SP, B_hst[b_], B_h2s, hst[b_][:, :, 0:512], h2_scr.ap()[4 * s4:4 * s4 + 4, :, w0:w0 + 512].rearrange("k p c -> p k c"))
            for kk in range(4):
                k = 4 * s4 + kk
                op(DVE, [B_hst[b_], B_rstdq, B_c], [B_hnq], lambda e, b_=b_, kk=kk, k=k: e.scalar_tensor_tensor(
                    out=hnq[:, k, 0:512], in0=hst[b_][:, kk, 0:512], scalar=pnw[:, k:k + 1], in1=rstdq[:], op0=ALU.mult, op1=ALU.mult))
                op(ACT, [B_hst[b_]], [B_h3q], lambda e, b_=b_, kk=kk, k=k: e.activation(out=h3q[:, k, :], in_=hst[b_][:, kk, 0:512], func=AF.Copy))
        op(POOL, [], [B_sacc2], lambda e: e.memset(sacc2[:], 0.0))
        for m in range(16):
            s_ = m % 2
            dma(POOL, B_wg[s_], B_in, wgb[s_][:].rearrange("p k n -> p (k n)"), V["wpg_d"][:, m, :])
            dma(POOL, B_wp[s_], B_in, wpb[s_][:].rearrange("p k n -> p (k n)"), V["wpp_d"][:, m, :])
            mm_group(B_Q[s_], Q[s_][:, 0:512], [(wgb[s_][:, k, :], hnq[:, k, 0:512]) for k in range(16)], [B_wg[s_], B_hnq])
            mm_group(B_Q[2 + s_], Q[2 + s_][:, 0:512], [(wpb[s_][:, kk, :], pTb[:, kk, w0:w0 + 512]) for kk in range(2)], [B_wp[s_], B_pT])
            op(ACT, [B_Q[s_], B_c], [B_sgt], lambda e, s_=s_, m=m: e.activation(out=sgt[:], in_=Q[s_][:, 0:512], func=AF.Sigmoid, bias=bpg[:, m:m + 1], scale=1.0))
            op(DVE, [B_sgt, B_Q[2 + s_]], [B_sgt], lambda e, s_=s_: e.tensor_tensor(out=sgt[:], in0=sgt[:], in1=Q[2 + s_][:, 0:512], op=ALU.mult))
            op(DVE, [B_sgt], [B_h3q], lambda e, m=m: e.tensor_add(out=h3q[:, m, :], in0=h3q[:, m, :], in1=sgt[:]))
            op(POOL, [B_h3q], [B_sqt], lambda e, m=m: e.tensor_tensor(out=sqt[:], in0=h3q[:, m, :], in1=h3q[:, m, :], op=ALU.mult))
            op(POOL, [B_sqt], [B_sacc2], lambda e: e.tensor_add(out=sacc2[:], in0=sacc2[:], in1=sqt[:]))
        rsqrt_cols(B_sacc2, sacc2[:], B_rstdq, rstdq[:], 512, 1.0 / D)
        for m in range(16):
            op(DVE, [B_rstdq, B_c], [B_h3q], lambda e, m=m: e.scalar_tensor_tensor(
                out=h3q[:, m, :], in0=h3q[:, m, :], scalar=fnw[:, m:m + 1], in1=rstdq[:], op0=ALU.mult, op1=ALU.mult))
        for m4 in range(0, 16, 4):
            dma(SP, B_out, B_h3q, outT[:, m4:m4 + 4, w0:w0 + 512], h3q[:, m4:m4 + 4, :], is_out=True)


def _pk(a):
    kp, n = a.shape
    return np.ascontiguousarray(a.reshape(kp // 128, 128, n).transpose(1, 0, 2))


def _col(v):
    return np.ascontiguousarray(v.reshape(-1, 128).T)


def prep_half1(inp, b, j):
    f32 = np.float32
    x = np.asarray(inp["x"], f32)
    xT = np.zeros((D, L + 4), f32)
    xT[:, 2:L + 2] = x[b].T
    w_in = np.asarray(inp["w_in"], f32)[0]
    OFF_Q, OFF_K, OFF_V, OFF_G, OFF_Z, OFF_XBC, OFF_DT = 0, 1024, 2048, 3072, 4096, 5120, 6656
    g = j // 2
    rh = [2 * j, 2 * j + 1]
    sh = [4 * j + i for i in range(4)]
    def hc(off, h, w=128):
        return list(range(off + h * w, off + (h + 1) * w))
    fm_cols = hc(OFF_Q, rh[0]) + hc(OFF_Q, rh[1]) + hc(OFF_K, rh[0]) + hc(OFF_K, rh[1])
    xcols = []
    for h in sh:
        xcols += hc(OFF_XBC, h, 64)
    bcols = list(range(OFF_XBC + 1024 + g * 128, OFF_XBC + 1024 + (g + 1) * 128))
    ccols = list(range(OFF_XBC + 1024 + 256 + g * 128, OFF_XBC + 1024 + 256 + (g + 1) * 128))
    fm_cols += xcols + bcols + ccols
    zcols = []
    for h in sh:
        zcols += hc(OFF_Z, h, 64)
    dtcols = [OFF_DT + h for h in sh] + [OFF_DT + 16 + h for h in sh]
    tm_cols = hc(OFF_V, rh[0]) + hc(OFF_V, rh[1]) + hc(OFF_G, rh[0]) + hc(OFF_G, rh[1]) + dtcols + zcols
    convc = np.array(xcols + bcols + ccols) - OFF_XBC
    cw = np.asarray(inp["ssd_conv_w"], f32)[0][:, convc]
    cbv = np.asarray(inp["ssd_conv_b"], f32)[0][convc]
    m = {
        "xTp": _pk(xT),
        "pos": np.ascontiguousarray(np.asarray(inp["positions"]).astype(np.int32)[b][None, :]),
        "wfm": _pk(w_in[:, fm_cols]),
        "wtm": _pk(w_in[:, tm_cols]),
        "nmw": _col(np.asarray(inp["norm_mix_w"], f32)[0]),
        "cw": np.ascontiguousarray(cw.T.reshape(4, 128, 5).transpose(1, 0, 2)),
        "cb": np.ascontiguousarray(cbv.reshape(4, 128).T),
        "dtb": np.concatenate([np.asarray(inp["ssd_dt_bias"], f32)[0, 0, sh], np.asarray(inp["ssd_dt_bias"], f32)[0, 1, sh]])[None, :],
        "alog": np.concatenate([np.asarray(inp["ssd_a_log"], f32)[0, 0, sh], np.asarray(inp["ssd_a_log"], f32)[0, 1, sh]])[None, :],
        "dsk": np.asarray(inp["ssd_d"], f32)[0, sh][None, :],
        "rnw": np.asarray(inp["ret_norm_w"], f32)[0, rh[0] * 128:(rh[1] + 1) * 128][None, :],
        "hh": np.array([rh], f32),
    }
    return {k: np.ascontiguousarray(v) for k, v in m.items()}


def _blk(a, nb):
    K, N = a.shape
    kk = K // 128
    return np.ascontiguousarray(a.reshape(kk, 128, nb, N // nb).transpose(1, 2, 0, 3).reshape(128, nb, kk * (N // nb)))


def gathered_perm():
    perm = []
    for i in range(4):
        perm += list(range(256 * i, 256 * i + 256))
        perm += list(range(1024 + 256 * i, 1024 + 256 * i + 256))
    return np.array(perm)


def prep_half2(inp, b, j):
    f32 = np.float32
    x = np.asarray(inp["x"], f32)
    xT = np.zeros((D, L + 4), f32)
    xT[:, 2:L + 2] = x[b].T
    perm = gathered_perm()
    snw_full = np.zeros(2048, f32)
    snw_full[1024:] = np.asarray(inp["ssd_norm_w"], f32)[0]
    idx = np.zeros((128, 16), np.int32)
    for i in range(4):
        for bl in range(4):
            idx[:, i * 4 + bl] = ((j * 4 + bl) * 4 + i) * 128 + np.arange(128)
    m = {
        "xw": _pk(xT[:, 2048 * j + 1:2048 * j + 1 + 2050]),
        "pT": _pk(np.asarray(inp["p"], f32)[0, b, 2048 * j:2048 * j + 2048, :].T),
        "gidx": idx,
        "wout": _blk(np.asarray(inp["w_out"], f32)[0][perm, :], 16),
        "snw": _col(snw_full[perm]),
        "nfw": _col(np.asarray(inp["norm_ffn_w"], f32)[0]),
        "wg": _blk(np.asarray(inp["ffn_w_gate"], f32)[0], NFB),
        "wu": _blk(np.asarray(inp["ffn_w_up"], f32)[0], NFB),
        "fcw": np.ascontiguousarray(np.asarray(inp["ffn_conv_w"], f32)[0].T.reshape(NFB, 128, 3).transpose(1, 0, 2)),
        "fcb": _col(np.asarray(inp["ffn_conv_b"], f32)[0]),
        "wd": _blk(np.asarray(inp["ffn_w_down"], f32)[0], 16),
        "pnw": _col(np.asarray(inp["ple_norm_w"], f32)[0]),
        "wpg": _blk(np.asarray(inp["ple_w_gate"], f32)[0], 16),
        "bpg": _col(np.asarray(inp["ple_b_gate"], f32)[0]),
        "wpp": _blk(np.asarray(inp["ple_w_proj"], f32)[0], 16),
        "fnw": _col(np.asarray(inp["final_norm_w"], f32)),
    }
    return {k: np.ascontiguousarray(v) for k, v in m.items()}


_NC_CACHE = {}


def kernel(**inputs):
    if "nc" not in _NC_CACHE:
        _NC_CACHE["nc"] = build_nc()
    nc = _NC_CACHE["nc"]
    in_maps = []
    for c in range(8):
        b, j = c // 4, c % 4
        m = prep_half1(inputs, b, j)
        m.update(prep_half2(inputs, b, j))
        in_maps.append(m)
    res = run_bass_kernel_spmd(nc, in_maps, core_ids=list(range(8)))
    out = np.zeros((2, L, D), np.float32)
    for c in range(8):
        b, j = c // 4, c % 4
        oT = np.asarray(res.results[c]["outT"])
        out[b, 2048 * j:2048 * j + 2048, :] = oT.transpose(2, 1, 0).reshape(2048, D)
    return out
```

```python
import contextlib
import math
import numpy as np
import ml_dtypes
import concourse.bass as bass
import concourse.mybir as mybir
from concourse.bass_utils import run_bass_kernel_spmd

F32 = mybir.dt.float32
BF16 = mybir.dt.bfloat16
I32 = mybir.dt.int32
AF = mybir.ActivationFunctionType
ALU = mybir.AluOpType

D = 2048
L = 8192
NT = 16
TW = 516
EPS = 1e-6
DFF = 5632
NFB = DFF // 128
SEMCH = 20000
NSEM = 100


import os
STOP = float(os.environ.get('KSTOP', '99'))


class StopBuild(Exception):
    pass


def chk(level):
    if STOP <= level:
        raise StopBuild()


ALL_BUFS = []


class Buf:
    def __init__(self, name, psum=False):
        ALL_BUFS.append(self)
        self.name = name
        self.psum = psum
        self.dtok = None
        self.w = None
        self.r = []
        self.dsem = None
        self.dcnt = 0


class Eng:
    def __init__(self, ctx, name, e):
        self.ctx = ctx
        self.name = name
        self.e = e
        self.sems = []
        self.n = 0
        self.seen = {}
        self.last = None

    def _sem(self, idx):
        while len(self.sems) <= idx:
            self.sems.append(self.ctx.new_sem(f"{self.name}_p{len(self.sems)}"))
        return self.sems[idx]

    def mark(self, ins):
        k, v = divmod(self.n, SEMCH)
        ins.then_inc(self._sem(k), 1)
        self.n += 1
        self.last = ("E", self, k, v + 1)
        return self.last

    def wait(self, tok):
        if tok is None:
            return
        if tok[0] == "E":
            _, prod, k, v = tok
            if prod is self and (not self.ctx.same_sync or self.name == "pe"):
                return
            key = ("E", prod.name)
            if self.seen.get(key, (-1, 0)) >= (k, v):
                return
            self.e.wait_ge(prod.sems[k], v)
            self.seen[key] = (k, v)
        else:
            _, sem, name, v = tok
            key = ("D", name)
            if self.seen.get(key, 0) >= v:
                return
            self.e.wait_ge(sem, v)
            self.seen[key] = v


class Ctx:
    def __init__(self, nc, es):
        self.nc = nc
        self.es = es
        self.same_sync = not os.environ.get('KNOSAME')
        self.nsem = 0
        self.PE = Eng(self, "pe", nc.tensor)
        self.ACT = Eng(self, "act", nc.scalar)
        self.DVE = Eng(self, "dve", nc.vector)
        self.POOL = Eng(self, "pool", nc.gpsimd)
        self.SP = Eng(self, "sp", nc.sync)
        self.out_toks = []
        self.sem_pool = [es.enter_context(nc.semaphore(f"sem{i}")) for i in range(NSEM)]

    def new_sem(self, name):
        self.nsem += 1
        return self.sem_pool[self.nsem - 1]

    def sb(self, name, shape, dt):
        return self.es.enter_context(self.nc.sbuf_tensor("s_" + name, list(shape), dt))

    def ps(self, name, shape, dt):
        return self.es.enter_context(self.nc.psum_tensor("p_" + name, list(shape), dt))

    def op(self, E, reads, writes, fn, mark=True, wait=True):
        if wait:
            for b in reads:
                E.wait(b.w)
                if b.psum and not os.environ.get('KNOPS'):
                    for t in b.r:
                        E.wait(t)
            for b in writes:
                E.wait(b.w)
                for t in b.r:
                    E.wait(t)
        ins = fn(E.e)
        if mark:
            tok = E.mark(ins)
            for b in reads:
                b.r.append(tok)
                if len(b.r) > 24:
                    b.r = b.r[-24:]
            for b in writes:
                b.w = tok
                b.r = []
        return ins

    def fence(self, engines=None):
        allE = (self.PE, self.ACT, self.DVE, self.POOL, self.SP)
        lasts = [E.last for E in allE]
        for E in (engines or allE):
            for t in lasts:
                if t is not None and t[1] is not E:
                    E.wait(t)
            for b in ALL_BUFS:
                E.wait(b.dtok)
                for tk in getattr(b, "dtoks", {}).values():
                    E.wait(tk)

    def dma(self, Q, dst, src, out_ap, in_ap, is_out=False, slow=False):
        Q.wait(src.w)
        Q.wait(dst.w)
        for t in dst.r:
            Q.wait(t)
        kind = "sw" if Q is self.POOL else "hw"
        if not hasattr(dst, "dsems"):
            dst.dsems = {}
        if kind not in dst.dsems:
            dst.dsems[kind] = [self.new_sem("d_" + dst.name + kind), 0]
        ent = dst.dsems[kind]
        ent[1] += 1
        dst.dsem = ent[0]
        dst.dcnt = ent[1]
        if slow:
            Q.e.dma_start(out=out_ap, in_=in_ap, allow_slow_non_contiguous=True).then_inc(dst.dsem, 16)
        else:
            Q.e.dma_start(out=out_ap, in_=in_ap).then_inc(dst.dsem, 16)
        tok = ("D", dst.dsem, dst.name + kind, 16 * dst.dcnt)
        dst.w = tok
        dst.dtok = tok
        if not hasattr(dst, "dtoks"):
            dst.dtoks = {}
        dst.dtoks[kind] = tok
        dst.r = []
        src.r.append(tok)
        if len(src.r) > 24:
            src.r = src.r[-24:]
        if is_out:
            self.out_toks.append(tok)
        return tok


def bc(ap, shape):
    return ap.to_broadcast(list(shape))


def build_nc(debug_half1=False, ntiles=NT, run_half2=True, run_half1=True):
    del ALL_BUFS[:]
    nc = bass.Bass("TRN2", target_bir_lowering=False)
    es = contextlib.ExitStack()
    with es:
        C = Ctx(nc, es)
        _build(nc, C, debug_half1, ntiles, run_half2, run_half1)
    return nc


def _build(nc, C, debug_half1, ntiles, run_half2, run_half1=True):
    PE, ACT, DVE, POOL, SP = C.PE, C.ACT, C.DVE, C.POOL, C.SP
    op, dma, sb, ps = C.op, C.dma, C.sb, C.ps

    def din(name, shape, dt=F32):
        return nc.dram_tensor(name, list(shape), dt, kind="ExternalInput").ap()

    xTp = din("xTp", [128, 16, L + 4])
    pos_d = din("pos", [1, L], I32)
    wfm_d = din("wfm", [128, 16, 1024])
    wtm_d = din("wtm", [128, 16, 776])
    nmw_d = din("nmw", [128, 16])
    cw_d = din("cw", [128, 4, 5])
    cb_d = din("cb", [128, 4])
    dtb_d = din("dtb", [1, 8])
    alog_d = din("alog", [1, 8])
    dsk_d = din("dsk", [1, 4])
    rnw_d = din("rnw", [1, 256])
    hh_d = din("hh", [1, 2])
    NC2 = 2050
    if run_half2:
        xw_d = din("xw", [128, 16, NC2])
        pT_d = din("pT", [128, 2, 2048])
        idx_d = din("gidx", [128, 16], I32)
        wout_d = din("wout", [128, 16, 2048])
        snw_d = din("snw", [128, 16])
        nfw_d = din("nfw", [128, 16])
        wg_d = din("wg", [128, NFB, 2048])
        wu_d = din("wu", [128, NFB, 2048])
        fcw_d = din("fcw", [128, NFB, 3])
        fcb_d = din("fcb", [128, NFB])
        wd_d = din("wd", [128, 16, NFB * 128])
        pnw_d = din("pnw", [128, 16])
        wpg_d = din("wpg", [128, 16, 2048])
        bpg_d = din("bpg", [128, 16])
        wpp_d = din("wpp", [128, 16, 256])
        fnw_d = din("fnw", [128, 16])
        outT = nc.dram_tensor("outT", [128, 16, 2048], F32, kind="ExternalOutput").ap()
    if debug_half1:
        mix_dbg = nc.dram_tensor("mix_dbg", [2048, 2050], BF16, kind="ExternalOutput").ap()

    agin = nc.dram_tensor("agin", [2048, NC2], BF16)
    if run_half1:
        agout = nc.dram_tensor("agout", [8192, NC2], BF16)
    else:
        agout = nc.dram_tensor("agout", [8192, NC2], BF16, kind="ExternalInput")
    rb_scr = nc.dram_tensor("rb_scr", [64, 128, 256], BF16)
    hb_scr = nc.dram_tensor("hb_scr", [64, 128, 256], BF16)
    B_agin = Buf("agin")
    B_agout = Buf("agout")
    B_in = Buf("ext_in")
    B_rbs = [Buf("rbs")] * 64
    B_hbs = [Buf("hbs")] * 64

    h1 = contextlib.ExitStack()
    C.es_outer = C.es
    C.es = h1
    with h1:
      if run_half1:
        wfm = sb("wfm_s", [128, 16, 1024], BF16); B_wfm = Buf("wfm")
        wtm = sb("wtm_s", [128, 16, 776], BF16); B_wtm = Buf("wtm")
        for k in range(0, 16, 4):
            dma(POOL, B_wfm, B_in, wfm[:, k:k + 4, :], wfm_d[:, k:k + 4, :])
            dma(POOL, B_wtm, B_in, wtm[:, k:k + 4, :], wtm_d[:, k:k + 4, :])
        B_c = Buf("consts")
        nmw = sb("nmw", [128, 16], F32); cw = sb("cw", [128, 4, 5], F32); cb = sb("cb", [128, 4], F32)
        dtb = sb("dtb", [128, 8], F32); alog = sb("alog", [128, 8], F32); dsk = sb("dsk", [128, 4], F32)
        rnw = sb("rnw", [128, 256], F32); hh = sb("hh", [128, 2], F32)
        dma(SP, B_c, B_in, nmw[:], nmw_d[:, :])
        dma(SP, B_c, B_in, cw[:], cw_d[:, :, :])
        dma(SP, B_c, B_in, cb[:], cb_d[:, :])
        dma(SP, B_c, B_in, dtb[:], dtb_d[0:1, :].partition_broadcast(128))
        dma(SP, B_c, B_in, alog[:], alog_d[0:1, :].partition_broadcast(128))
        dma(SP, B_c, B_in, dsk[:], dsk_d[0:1, :].partition_broadcast(128))
        dma(SP, B_c, B_in, rnw[:], rnw_d[0:1, :].partition_broadcast(128))
        dma(SP, B_c, B_in, hh[:], hh_d[0:1, :].partition_broadcast(128))

        B_k = Buf("kconst")
        dmat = sb("dmat", [128, 128], F32)
        TI = sb("TI", [128, 128], F32); TS = sb("TS", [128, 128], F32)
        TIs = sb("TIs", [128, 128], F32); TSs = sb("TSs", [128, 128], F32)
        onesf = sb("onesf", [128, 128], F32); onesb = sb("onesb", [128, 128], BF16)
        identf = sb("identf", [128, 128], F32); ident = sb("ident", [128, 128], BF16)
        permf = sb("permf", [128, 128], F32); perm = sb("perm", [128, 128], BF16)
        pcol = sb("pcol", [128, 1], F32); irow = sb("irow", [128, 128], F32)
        ifr = sb("ifr", [128, 1], F32); sgn = sb("sgn", [128, 1], F32)
        tmpc = sb("tmpc", [128, 128], F32); tmpc2 = sb("tmpc2", [128, 128], F32)
        g = POOL
        op(g, [], [B_k], lambda e: e.iota(dmat[:], pattern=[[1, 128]], base=0, channel_multiplier=-1,
                                          allow_small_or_imprecise_dtypes=True))
        op(g, [], [B_k], lambda e: e.tensor_single_scalar(out=TI[:], in_=dmat[:], scalar=0.0, op=ALU.is_ge))
        op(g, [], [B_k], lambda e: e.tensor_single_scalar(out=TS[:], in_=dmat[:], scalar=0.0, op=ALU.is_le))
        op(g, [], [B_k], lambda e: e.tensor_single_scalar(out=TIs[:], in_=dmat[:], scalar=0.0, op=ALU.is_gt))
        op(g, [], [B_k], lambda e: e.tensor_single_scalar(out=TSs[:], in_=dmat[:], scalar=0.0, op=ALU.is_lt))
        op(g, [], [B_k], lambda e: e.tensor_single_scalar(out=identf[:], in_=dmat[:], scalar=0.0, op=ALU.is_equal))
        op(g, [], [B_k], lambda e: e.tensor_copy(out=ident[:], in_=identf[:]))
        op(g, [], [B_k], lambda e: e.memset(onesf[:], 1.0))
        op(g, [], [B_k], lambda e: e.memset(onesb[:], 1.0))
        op(g, [], [B_k], lambda e: e.tensor_scalar(out=tmpc[:], in0=dmat[:], scalar1=64.0, scalar2=0.0,
                                                   op0=ALU.add, op1=ALU.is_equal))
        op(g, [], [B_k], lambda e: e.tensor_scalar(out=tmpc2[:], in0=dmat[:], scalar1=-64.0, scalar2=0.0,
                                                   op0=ALU.add, op1=ALU.is_equal))
        op(g, [], [B_k], lambda e: e.tensor_add(out=permf[:], in0=tmpc[:], in1=tmpc2[:]))
        op(g, [], [B_k], lambda e: e.tensor_copy(out=perm[:], in_=permf[:]))
        op(g, [], [B_k], lambda e: e.iota(pcol[:], pattern=[[0, 1]], base=0, channel_multiplier=1,
                                          allow_small_or_imprecise_dtypes=True))
        op(g, [], [B_k], lambda e: e.iota(irow[:], pattern=[[1, 128]], base=0, channel_multiplier=0,
                                          allow_small_or_imprecise_dtypes=True))
        pm = sb("pm", [128, 1], F32)
        op(DVE, [B_k], [B_k], lambda e: e.tensor_scalar(out=pm[:], in0=pcol[:], scalar1=64.0, scalar2=-64.0, op0=ALU.is_ge, op1=ALU.mult))
        op(DVE, [B_k], [B_k], lambda e: e.tensor_add(out=pm[:], in0=pm[:], in1=pcol[:]))
        op(DVE, [B_k], [B_k], lambda e: e.tensor_scalar(out=sgn[:], in0=pcol[:], scalar1=64.0, scalar2=2.0,
                                                   op0=ALU.is_ge, op1=ALU.mult))
        op(DVE, [B_k], [B_k], lambda e: e.tensor_scalar_add(out=sgn[:], in0=sgn[:], scalar1=-1.0))
        op(ACT, [B_k], [B_k], lambda e: e.activation(out=ifr[:], in_=pm[:], func=AF.Exp,
                                                     scale=-math.log(10000.0) / 64.0))
        aneg = sb("aneg", [128, 8], F32)
        op(ACT, [B_c], [B_k], lambda e: e.activation(out=aneg[:], in_=alog[:], func=AF.Exp))
        op(ACT, [B_k], [B_k], lambda e: e.mul(out=aneg[:], in_=aneg[:], mul=-1.0))
        lf = sb("lf", [128, 2], F32); lb = sb("lb", [128, 2], F32)
        LN2 = math.log(2.0)
        for (dst, off) in ((lf, 5.0), (lb, 5.5)):
            op(ACT, [B_c, B_k], [B_k], lambda e, dst=dst, off=off: e.activation(
                out=dst[:], in_=hh[:], func=AF.Exp, scale=-LN2, bias=-LN2 * off))
            op(ACT, [B_k], [B_k], lambda e, dst=dst: e.activation(
                out=dst[:], in_=dst[:], func=AF.Ln, scale=-1.0, bias=1.0))
        maskT = sb("maskT", [128, 2, 128], F32)
        kfc = sb("kfc", [128, 2], F32); kbc = sb("kbc", [128, 2], F32)
        qfr = sb("qfr", [128, 2, 128], F32); qbr = sb("qbr", [128, 2, 128], F32)
        decf = sb("decf", [128, 2], F32); decb = sb("decb", [128, 2], F32)
        posd = sb("posd", [128, 128], F32); negd = sb("negd", [128, 128], F32)
        op(DVE, [B_k], [B_k], lambda e: e.tensor_scalar_max(out=posd[:], in0=dmat[:], scalar1=0.0))
        op(DVE, [B_k], [B_k], lambda e: e.tensor_sub(out=negd[:], in0=posd[:], in1=dmat[:]))
        jr = sb("jr", [128, 1], F32)
        op(DVE, [B_k], [B_k], lambda e: e.tensor_scalar(out=jr[:], in0=pcol[:], scalar1=-1.0, scalar2=127.0,
                                                        op0=ALU.mult, op1=ALU.add))
        ip1 = sb("ip1", [128, 128], F32); cmi = sb("cmi", [128, 128], F32)
        op(DVE, [B_k], [B_k], lambda e: e.tensor_scalar_add(out=ip1[:], in0=irow[:], scalar1=1.0))
        op(DVE, [B_k], [B_k], lambda e: e.tensor_scalar(out=cmi[:], in0=irow[:], scalar1=-1.0, scalar2=128.0,
                                                        op0=ALU.mult, op1=ALU.add))
        for h in range(2):
            op(DVE, [B_k], [B_k], lambda e, h=h: e.tensor_scalar_mul(out=tmpc[:], in0=posd[:], scalar1=lf[:, h:h + 1]))
            op(DVE, [B_k], [B_k], lambda e, h=h: e.scalar_tensor_tensor(
                out=tmpc[:], in0=negd[:], scalar=lb[:, h:h + 1], in1=tmpc[:], op0=ALU.mult, op1=ALU.add))
            op(ACT, [B_k], [B_k], lambda e, h=h: e.activation(out=maskT[:, h, :], in_=tmpc[:], func=AF.Exp))
            op(ACT, [B_k], [B_k], lambda e, h=h: e.activation(out=kfc[:, h:h + 1], in_=jr[:], func=AF.Exp,
                                                               scale=lf[:, h:h + 1]))
            op(ACT, [B_k], [B_k], lambda e, h=h: e.activation(out=kbc[:, h:h + 1], in_=pcol[:], func=AF.Exp,
                                                               scale=lb[:, h:h + 1]))
            op(ACT, [B_k], [B_k], lambda e, h=h: e.activation(out=qfr[:, h, :], in_=ip1[:], func=AF.Exp,
                                                               scale=lf[:, h:h + 1]))
            op(ACT, [B_k], [B_k], lambda e, h=h: e.activation(out=qbr[:, h, :], in_=cmi[:], func=AF.Exp,
                                                               scale=lb[:, h:h + 1]))
        op(ACT, [B_k], [B_k], lambda e: e.activation(out=decf[:], in_=lf[:], func=AF.Exp, scale=128.0))
        op(ACT, [B_k], [B_k], lambda e: e.activation(out=decb[:], in_=lb[:], func=AF.Exp, scale=128.0))

        epsc = sb("epsc", [128, 1], F32)
        op(POOL, [], [B_k], lambda e: e.memset(epsc[:], EPS))
        xs = sb("xs", [128, 16, TW], F32); B_xs4 = [Buf(f"xs{g_}") for g_ in range(4)]
        sq = sb("sq", [128, 16, TW], BF16); B_sqk = [Buf(f"sq{k_}") for k_ in range(16)]
        hn = sq; B_hnl = B_sqk
        rstd = sb("rstd", [128, TW], F32); B_rstd = Buf("rstd")
        posi = sb("posi", [128, 512], I32); B_posi = Buf("posi")
        ang = sb("ang", [128, 512], F32); ang2 = sb("ang2", [128, 512], F32)
        cosT = sb("cosT", [128, 512], F32); sinT = sb("sinT", [128, 512], F32); B_cs = Buf("cossin")
        rawb = sb("rawb", [128, 512], BF16); B_rawb = Buf("rawb")
        rt1 = sb("rt1", [128, 512], F32); B_rt1 = Buf("rt1")
        qT = sb("qT", [128, 2, 512], BF16); kT = sb("kT", [128, 2, 512], BF16)
        qfT = sb("qfT", [128, 2, 512], BF16); qbT = sb("qbT", [128, 2, 512], BF16)
        B_qT = Buf("qT"); B_kT = Buf("kT"); B_qfb = Buf("qfb")
        rawc = sb("rawc", [128, TW], F32); B_rawc = Buf("rawc")
        cacc = sb("cacc", [128, 512], F32); B_cacc = Buf("cacc")
        xbcT = sb("xbcT", [128, 4, 512], BF16); B_xbcT = [Buf(f"xbcT{i}") for i in range(4)]
        vtm = sb("vtm", [128, 4, 256], BF16); gtm = sb("gtm", [128, 4, 256], BF16); ztm = sb("ztm", [128, 4, 256], BF16)
        B_vtm = Buf("vtm"); B_gtm = Buf("gtm"); B_ztm = Buf("ztm")
        dtr = sb("dtr", [128, 4, 8], F32); dtv = sb("dtv", [128, 4, 8], F32); adt = sb("adt", [128, 4, 8], F32)
        B_dt = Buf("dt")
        ktm = sb("ktm", [128, 4, 2, 128], BF16); B_ktm = Buf("ktm")
        xtm = sb("xtm", [128, 4, 256], BF16); B_xtm = Buf("xtm")
        btm = sb("btm", [128, 4, 128], BF16); B_btm = Buf("btm")
        nfc = sb("nfc", [128, 8], F32); ex24 = sb("ex24", [128, 24], F32)
        dI = ex24[:, 0:8]; dE = ex24[:, 8:16]; decs = ex24[:, 16:24]
        wE = sb("wE", [128, 8], F32); B_sm = Buf("ssd_small")
        Lm = sb("Lm", [128, 8, 128], F32); B_Lm = Buf("Lm")
        cbm = sb("cbm", [128, 2, 128], F32); B_cbm = Buf("cbm")
        MT = sb("MT", [128, 8, 128], BF16); B_MT = Buf("MT")
        xdt = sb("xdt", [128, 8, 64], BF16); xE = sb("xE", [128, 8, 64], BF16); B_xdt = Buf("xdt"); B_xE = Buf("xE")
        Hf = sb("Hf", [128, 256], F32); Hfb = sb("Hfb", [128, 256], BF16); B_Hf = Buf("Hf"); B_Hfb = Buf("Hfb")
        Hb = sb("Hb", [128, 256], F32); Hbb = sb("Hbb", [128, 256], BF16); B_Hb = Buf("Hb"); B_Hbb = Buf("Hbb")
        Htmp = sb("Htmp", [128, 256], F32); B_Htmp = Buf("Htmp")
        rf = sb("rf", [128, 2, 128], F32); rfb = sb("rfb", [128, 2, 128], BF16); B_rf = Buf("rf"); B_rfb = Buf("rfb")
        rbk = sb("rbk", [128, 2, 128], F32); rbb = sb("rbb", [128, 2, 128], BF16); B_rb = Buf("rb"); B_rbb = Buf("rbb")
        SmT = sb("SmT", [128, 2, 128], BF16); B_SmT = Buf("SmT")
        ssq = sb("ssq", [128, 2], F32); rs2 = sb("rs2", [128, 2], F32); B_ssq = Buf("ssq")
        junk = sb("junk", [128, 256], F32); B_junk = Buf("junk")
        rtmp = sb("rtmp", [128, 256], F32); B_rtmp = Buf("rtmp")
        yt = sb("yt", [128, 256], F32); yt2 = sb("yt2", [128, 256], F32); B_yt = Buf("yt")
        mixtm = sb("mixtm", [128, 4, 512], BF16); B_mixtm = Buf("mixtm")
        mixT = sb("mixT", [128, 4, 512], BF16); B_mixT = Buf("mixT")

        P0 = ps("P0", [128, 512], F32); P1 = ps("P1", [128, 512], F32); P2 = ps("P2", [128, 512], F32)
        P3 = ps("P3", [128, 512], F32); P4 = ps("P4", [128, 512], F32); P56 = ps("P56", [128, 1024], F32)
        P7 = ps("P7", [128, 512], F32)
        B_P0 = Buf("P0", True); B_P1 = Buf("P1", True); B_P2 = Buf("P2", True); B_P3 = Buf("P3", True); B_P4 = Buf("P4", True)
        B_P56 = Buf("P56", True); B_P7 = Buf("P7", True)
        P4b = P4[:].bitcast(BF16)

        zpad = sb("zpad", [128, 4, 1], BF16); B_zpad = Buf("zpad")
        op(POOL, [], [B_zpad], lambda e: e.memset(zpad[:], 0.0))
        op(POOL, [], [B_Hf], lambda e: e.memset(Hf[:], 0.0))
        op(POOL, [], [B_Hfb], lambda e: e.memset(Hfb[:], 0.0))
        op(POOL, [], [B_Hb], lambda e: e.memset(Hb[:], 0.0))
        op(POOL, [], [B_Hbb], lambda e: e.memset(Hbb[:], 0.0))
        op(POOL, [], [B_rf], lambda e: e.memset(rf[:], 0.0))
        op(POOL, [], [B_rfb], lambda e: e.memset(rfb[:], 0.0))
        op(POOL, [], [B_rb], lambda e: e.memset(rbk[:], 0.0))
        op(POOL, [], [B_rbb], lambda e: e.memset(rbb[:], 0.0))

        def mm_group(out_buf, out_ap, pairs, reads, first=True, last=True):
            n = len(pairs)
            for i, (l_, r_) in enumerate(pairs):
                st = first and i == 0
                sp_ = last and i == n - 1
                op(PE, reads, [out_buf], lambda e, l_=l_, r_=r_, st=st, sp_=sp_: e.matmul(
                    out_ap, l_, r_, start=st, stop=sp_), mark=(i == n - 1), wait=(i == 0))

        def tile_body(t, pas):
            c0 = 512 * t
            for k4 in range(0, 16, 4):
                dma(SP, B_xs4[k4 // 4], B_in, xs[:, k4:k4 + 4, :], xTp[:, k4:k4 + 4, c0:c0 + TW])
            chk(1)
            for k in range(16):
                op(ACT, [B_xs4[k // 4]], [B_sqk[k]], lambda e, k=k: e.activation(out=sq[:, k, :], in_=xs[:, k, :], func=AF.Square))
            chk(2)
            mm_group(B_P4, P4[:, 0:512], [(onesb[:], sq[:, k, 0:512]) for k in range(16)], B_sqk + [B_k])
            mm_group(B_P1, P1[:, 0:32], [(onesb[:], sq[:, k, TW - 32:TW]) for k in range(16)], B_sqk + [B_k])
            chk(2.1)
            epsb = EPS
            op(ACT, [B_P4], [B_rstd], lambda e: e.activation(out=(ang[:, 0:512] if os.environ.get('KALT') else rstd[:, 0:512]), in_=P4[:, 0:512], func=(AF.Copy if os.environ.get('KALT2') else AF.Ln), scale=1.0 / D, bias=(0.0 if os.environ.get('KALT2') else epsc[:, 0:1])))
            if not os.environ.get('KSKIP'):
                op(ACT, [B_P1], [B_rstd], lambda e: e.activation(out=rstd[:, TW - 32:TW], in_=P1[:, 0:32], func=AF.Ln, scale=1.0 / D, bias=epsc[:, 0:1]))
            chk(2.2)
            op(ACT, [B_rstd], [B_rstd], lambda e: e.activation(out=rstd[:], in_=rstd[:], func=AF.Exp, scale=-0.5))
            chk(2.3)
            for k in range(16):
                g_ = k // 4
                if g_ % 2 == 0:
                    op(DVE, [B_xs4[g_], B_rstd, B_c], [B_sqk[k]], lambda e, k=k: e.scalar_tensor_tensor(
                        out=hn[:, k, :], in0=xs[:, k, :], scalar=nmw[:, k:k + 1], in1=rstd[:], op0=ALU.mult, op1=ALU.mult))
                else:
                    op(POOL, [B_rstd], [B_xs4[g_]], lambda e, k=k: e.tensor_tensor(out=xs[:, k, :], in0=xs[:, k, :], in1=rstd[:], op=ALU.mult))
                    op(POOL, [B_xs4[g_], B_c], [B_sqk[k]], lambda e, k=k: e.tensor_scalar_mul(out=hn[:, k, :], in0=xs[:, k, :], scalar1=nmw[:, k:k + 1]))
            chk(3)
            dma(SP, B_posi, B_in, posi[:], pos_d[0:1, c0:c0 + 512].partition_broadcast(128))
            TWO_PI = 2 * math.pi
            op(DVE, [B_posi], [B_cs], lambda e: e.tensor_copy(out=ang[:], in_=posi[:]))
            op(DVE, [B_cs, B_k], [B_cs], lambda e: e.tensor_scalar_mul(out=ang[:], in0=ang[:], scalar1=ifr[:, 0:1]))
            op(DVE, [B_cs], [B_cs], lambda e: e.tensor_scalar_mul(out=ang2[:], in0=ang[:], scalar1=1.0 / TWO_PI))
            op(DVE, [B_cs], [B_posi], lambda e: e.tensor_copy(out=posi[:], in_=ang2[:]))
            op(DVE, [B_posi], [B_cs], lambda e: e.tensor_copy(out=ang2[:], in_=posi[:]))
            op(DVE, [B_cs], [B_cs], lambda e: e.scalar_tensor_tensor(out=ang[:], in0=ang2[:], scalar=-TWO_PI, in1=ang[:], op0=ALU.mult, op1=ALU.add))
            op(DVE, [B_cs], [B_cs], lambda e: e.tensor_scalar(out=ang2[:], in0=ang[:], scalar1=math.pi, scalar2=-TWO_PI, op0=ALU.is_gt, op1=ALU.mult))
            op(DVE, [B_cs], [B_cs], lambda e: e.tensor_add(out=ang[:], in0=ang[:], in1=ang2[:]))
            op(DVE, [B_cs], [B_cs], lambda e: e.tensor_scalar(out=ang2[:], in0=ang[:], scalar1=-math.pi, scalar2=TWO_PI, op0=ALU.is_lt, op1=ALU.mult))
            op(DVE, [B_cs], [B_cs], lambda e: e.tensor_add(out=ang[:], in0=ang[:], in1=ang2[:]))
            op(DVE, [B_cs], [B_cs], lambda e: e.tensor_scalar_add(out=ang2[:], in0=ang[:], scalar1=math.pi / 2))
            op(DVE, [B_cs], [B_cs], lambda e: e.tensor_scalar(out=cosT[:], in0=ang2[:], scalar1=math.pi, scalar2=-TWO_PI, op0=ALU.is_gt, op1=ALU.mult))
            op(DVE, [B_cs], [B_cs], lambda e: e.tensor_add(out=ang2[:], in0=ang2[:], in1=cosT[:]))
            op(ACT, [B_cs], [B_cs], lambda e: e.activation(out=sinT[:], in_=ang[:], func=AF.Sin))
            op(ACT, [B_cs], [B_cs], lambda e: e.activation(out=cosT[:], in_=ang2[:], func=AF.Sin))
            op(DVE, [B_cs, B_k], [B_cs], lambda e: e.tensor_scalar_mul(out=sinT[:], in0=sinT[:], scalar1=sgn[:, 0:1]))

            chk(4)
            def fm_block(bi, conv):
                if conv:
                    mm_group(B_P0, P0[:, 0:512], [(wfm[:, k, bi * 128:(bi + 1) * 128], hn[:, k, 0:512]) for k in range(16)],
                             B_hnl + [B_wfm])
                    mm_group(B_P1, P1[:, 32:64], [(wfm[:, k, bi * 128:(bi + 1) * 128], hn[:, k, TW - 32:TW]) for k in range(16)],
                             B_hnl + [B_wfm])
                else:
                    mm_group(B_P0, P0[:, 0:512], [(wfm[:, k, bi * 128:(bi + 1) * 128], hn[:, k, 2:514]) for k in range(16)],
                             B_hnl + [B_wfm])

            def rotary(bi, dst, dbuf, hidx, scale):
                second = (hidx == 1)
                fm_block(bi, False)
                chk(4.45 if second else 4.1)
                op(ACT, [B_P0], [B_rawb], lambda e: e.activation(out=rawb[:], in_=P0[:, 0:512], func=AF.Copy, scale=scale))
                op(DVE, [B_P0, B_cs], [B_rt1], lambda e: e.scalar_tensor_tensor(
                    out=rt1[:], in0=P0[:, 0:512], scalar=scale, in1=cosT[:], op0=ALU.mult, op1=ALU.mult))
                chk(4.46 if second else 4.2)
                mm_group(B_P7, P7[:, 0:512], [(perm[:], rawb[:])], [B_rawb, B_k])
                chk(4.47 if second else 4.25)
                op(DVE, [B_P7, B_cs], [B_cacc], lambda e: e.tensor_tensor(out=cacc[:], in0=P7[:, 0:512], in1=sinT[:], op=ALU.mult))
                chk(4.48 if second else 4.3)
                op(DVE, [B_rt1, B_cacc], [dbuf], lambda e: e.tensor_add(out=dst[:, hidx, :], in0=rt1[:], in1=cacc[:]))
                chk(4.49 if second else 4.4)

            if pas == 2:
                for h in range(2):
                    rotary(h, qT, B_qT, h, 1.0)
                    for c in range(4):
                        op(POOL, [B_qT, B_k], [B_qfb], lambda e, h=h, c=c: e.tensor_tensor(
                            out=qfT[:, h, c * 128:(c + 1) * 128], in0=qT[:, h, c * 128:(c + 1) * 128], in1=qfr[:, h, :], op=ALU.mult))
                        op(POOL, [B_qT, B_k], [B_qfb], lambda e, h=h, c=c: e.tensor_tensor(
                            out=qbT[:, h, c * 128:(c + 1) * 128], in0=qT[:, h, c * 128:(c + 1) * 128], in1=qbr[:, h, :], op=ALU.mult))
            for h in range(2):
                rotary(2 + h, kT, B_kT, h, 128.0 ** -0.5)
            chk(4.5)
            conv_blocks = [0, 1, 2] if pas == 1 else [0, 1, 2, 3]
            for ci in conv_blocks:
                fm_block(4 + ci, True)
                chk(4.6)
                op(ACT, [B_P0], [B_rawc], lambda e: e.activation(out=rawc[:, 0:512], in_=P0[:, 0:512], func=AF.Copy))
                op(ACT, [B_P1], [B_rawc], lambda e: e.activation(out=rawc[:, 512:516], in_=P1[:, 60:64], func=AF.Copy))
                chk(4.7)
                op(DVE, [B_rawc, B_c], [B_cacc], lambda e, ci=ci: e.tensor_scalar_mul(
                    out=cacc[:], in0=rawc[:, 0:512], scalar1=cw[:, ci, 0:1]))
                for j in range(1, 5):
                    op(DVE, [B_rawc, B_c], [B_cacc], lambda e, ci=ci, j=j: e.scalar_tensor_tensor(
                        out=cacc[:], in0=rawc[:, j:j + 512], scalar=cw[:, ci, j:j + 1], in1=cacc[:], op0=ALU.mult, op1=ALU.add))
                chk(4.8)
                op(ACT, [B_cacc, B_c], [B_xbcT[ci]], lambda e, ci=ci: e.activation(
                    out=xbcT[:, ci, :], in_=cacc[:], func=AF.Silu, bias=cb[:, ci:ci + 1], scale=1.0))

            chk(5)
            for c in range(4):
                lo = 2 + 128 * c
                if pas == 1:
                    mm_group(B_P2, P2[:, 0:256], [(hn[:, k, lo:lo + 128], wtm[:, k, 0:256]) for k in range(16)], B_hnl + [B_wtm])
                    mm_group(B_P3, P3[:, 0:8], [(hn[:, k, lo:lo + 128], wtm[:, k, 512:520]) for k in range(16)], B_hnl + [B_wtm])
                else:
                    mm_group(B_P2, P2[:, 0:512], [(hn[:, k, lo:lo + 128], wtm[:, k, 0:512]) for k in range(16)], B_hnl + [B_wtm])
                    mm_group(B_P3, P3[:, 0:264], [(hn[:, k, lo:lo + 128], wtm[:, k, 512:776]) for k in range(16)], B_hnl + [B_wtm])
                op(ACT, [B_P2], [B_vtm], lambda e, c=c: e.activation(out=vtm[:, c, :], in_=P2[:, 0:256], func=AF.Copy))
                op(DVE, [B_P3], [B_dt], lambda e, c=c: e.tensor_copy(out=dtr[:, c, :], in_=P3[:, 0:8]))
                if pas == 2:
                    op(ACT, [B_P2], [B_gtm], lambda e, c=c: e.activation(out=gtm[:, c, :], in_=P2[:, 256:512], func=AF.Silu))
                    op(ACT, [B_P3], [B_ztm], lambda e, c=c: e.activation(out=ztm[:, c, :], in_=P3[:, 8:264], func=AF.Silu))
            op(DVE, [B_dt, B_c], [B_dt], lambda e: e.tensor_tensor(out=dtr[:], in0=dtr[:], in1=bc(dtb[:].unsqueeze(1), [128, 4, 8]), op=ALU.add))
            op(ACT, [B_dt], [B_dt], lambda e: e.activation(out=dtr[:], in_=dtr[:], func=AF.Exp))
            op(ACT, [B_dt], [B_dt], lambda e: e.activation(out=dtv[:], in_=dtr[:], func=AF.Ln, bias=1.0, scale=1.0))
            op(DVE, [B_dt, B_k], [B_dt], lambda e: e.tensor_tensor(out=adt[:], in0=dtv[:], in1=bc(aneg[:].unsqueeze(1), [128, 4, 8]), op=ALU.mult))

            chk(6)
            kw = kbc if pas == 1 else kfc
            for c in range(4):
                for h in range(2):
                    op(PE, [B_kT, B_k], [B_P4], lambda e, c=c, h=h: e.transpose(
                        P4b[:, (c * 2 + h) * 128:(c * 2 + h + 1) * 128], kT[:, h, c * 128:(c + 1) * 128], ident[:]))
            for c in range(4):
                for h in range(2):
                    op(DVE, [B_P4, B_k], [B_ktm], lambda e, c=c, h=h: e.tensor_scalar_mul(
                        out=ktm[:, c, h, :], in0=P4b[:, (c * 2 + h) * 128:(c * 2 + h + 1) * 128], scalar1=kw[:, h:h + 1]))
            for c in range(4):
                for bl in range(2):
                    op(PE, [B_xbcT[bl], B_k], [B_P4], lambda e, c=c, bl=bl: e.transpose(
                        P4b[:, (c * 2 + bl) * 128:(c * 2 + bl + 1) * 128], xbcT[:, bl, c * 128:(c + 1) * 128], ident[:]))
            op(ACT, [B_P4], [B_xtm], lambda e: e.activation(out=xtm[:].rearrange("p c f -> p (c f)"), in_=P4b[:, 0:1024], func=AF.Copy))
            for c in range(4):
                op(PE, [B_xbcT[2], B_k], [B_P4], lambda e, c=c: e.transpose(
                    P4b[:, c * 128:(c + 1) * 128], xbcT[:, 2, c * 128:(c + 1) * 128], ident[:]))
            op(ACT, [B_P4], [B_btm], lambda e: e.activation(out=btm[:].rearrange("p c f -> p (c f)"), in_=P4b[:, 0:512], func=AF.Copy))

            chk(7)
            chunks = [3, 2, 1, 0] if pas == 1 else [0, 1, 2, 3]
            for c in chunks:
                gc = 4 * t + c
                cs = slice(c * 128, (c + 1) * 128)
                mm_group(B_P1, P1[:, 16:20], [(TI[:], adt[:, c, 0:4])], [B_dt, B_k])
                mm_group(B_P1, P1[:, 20:24], [(TS[:], adt[:, c, 4:8])], [B_dt, B_k])
                mm_group(B_P1, P1[:, 24:28], [(TSs[:], adt[:, c, 0:4])], [B_dt, B_k])
                mm_group(B_P1, P1[:, 28:32], [(TIs[:], adt[:, c, 4:8])], [B_dt, B_k])
                mm_group(B_P1, P1[:, 32:40], [(onesf[:], adt[:, c, :])], [B_dt, B_k])
                op(ACT, [B_P1], [B_sm], lambda e: e.activation(out=nfc[:], in_=P1[:, 16:24], func=AF.Copy, scale=-1.0))
                op(ACT, [B_P1], [B_sm], lambda e: e.activation(out=ex24[:], in_=P1[:, 16:40], func=AF.Exp))
                op(DVE, [B_sm, B_dt], [B_sm], lambda e, c=c: e.tensor_mul(out=wE[:], in0=dE, in1=dtv[:, c, :]))
                dsel = slice(4, 8) if pas == 1 else slice(0, 4)
                op(DVE, [B_xtm, B_sm], [B_xE], lambda e, c=c: e.tensor_tensor(
                    out=xE[:, 0:4, :], in0=xtm[:, c, :].rearrange("p (h f) -> p h f", h=4),
                    in1=bc(wE[:, dsel].unsqueeze(2), [128, 4, 64]), op=ALU.mult))
                if pas == 1:
                    dma(SP, B_hbs[gc], B_Hbb, hb_scr.ap()[gc, :, :], Hbb[:])
                    dma(SP, B_rbs[gc], B_rbb, rb_scr.ap()[gc, :, :], rbb[:].rearrange("p h f -> p (h f)"))
                    mm_group(B_P2, P2[:, 0:256], [(btm[:, c, :], xE[:, 0:4, :].rearrange("p h f -> p (h f)"))], [B_btm, B_xE])
                    op(DVE, [B_Hb, B_sm], [B_Htmp], lambda e: e.tensor_tensor(
                        out=Htmp[:].rearrange("p (h f) -> p h f", h=4), in0=Hb[:].rearrange("p (h f) -> p h f", h=4),
                        in1=bc(decs[:, 4:8].unsqueeze(2), [128, 4, 64]), op=ALU.mult))
                    op(DVE, [B_Htmp, B_P2], [B_Hb], lambda e: e.tensor_add(out=Hb[:], in0=Htmp[:], in1=P2[:, 0:256]))
                    op(ACT, [B_Hb], [B_Hbb], lambda e: e.activation(out=Hbb[:], in_=Hb[:], func=AF.Copy))
                    for h in range(2):
                        mm_group(B_P0, P0[:, h * 128:(h + 1) * 128], [(ktm[:, c, h, :], vtm[:, c, h * 128:(h + 1) * 128])], [B_ktm, B_vtm])
                    for h in range(2):
                        op(DVE, [B_rb, B_P0, B_k], [B_rb], lambda e, h=h: e.scalar_tensor_tensor(
                            out=rbk[:, h, :], in0=rbk[:, h, :], scalar=decb[:, h:h + 1], in1=P0[:, h * 128:(h + 1) * 128],
                            op0=ALU.mult, op1=ALU.add))
                    op(ACT, [B_rb], [B_rbb], lambda e: e.activation(out=rbb[:], in_=rbk[:], func=AF.Copy))
                    continue
                dma(SP, B_Hbb, B_hbs[gc], Hbb[:], hb_scr.ap()[gc, :, :])
                dma(SP, B_rbb, B_rbs[gc], rbb[:].rearrange("p h f -> p (h f)"), rb_scr.ap()[gc, :, :])
                for hd in range(8):
                    tri = TI if hd < 4 else TS
                    mm_group(B_P56, P56[:, hd * 128:(hd + 1) * 128], [(bc(adt[:, c, hd:hd + 1], [128, 128]), tri[:])], [B_dt, B_k])
                for hd in range(8):
                    op(DVE, [B_P56, B_sm], [B_Lm], lambda e, hd=hd: e.tensor_scalar(
                        out=Lm[:, hd, :], in0=P56[:, hd * 128:(hd + 1) * 128], scalar1=nfc[:, hd:hd + 1], scalar2=0.0,
                        op0=ALU.add, op1=ALU.min))
                op(ACT, [B_Lm], [B_Lm], lambda e: e.activation(out=Lm[:], in_=Lm[:], func=AF.Exp))
                mm_group(B_P7, P7[:, 0:128], [(xbcT[:, 2, cs], xbcT[:, 3, cs])], [B_xbcT[2], B_xbcT[3]])
                op(DVE, [B_P7, B_k], [B_cbm], lambda e: e.tensor_tensor(out=cbm[:, 0, :], in0=P7[:, 0:128], in1=TI[:], op=ALU.mult))
                op(DVE, [B_P7, B_k], [B_cbm], lambda e: e.tensor_tensor(out=cbm[:, 1, :], in0=P7[:, 0:128], in1=TSs[:], op=ALU.mult))
                for d_ in range(2):
                    op(DVE, [B_Lm, B_cbm], [B_MT], lambda e, d_=d_: e.scalar_tensor_tensor(
                        out=MT[:, 4 * d_:4 * d_ + 4, :], in0=Lm[:, 4 * d_:4 * d_ + 4, :], scalar=1.0,
                        in1=bc(cbm[:, d_:d_ + 1, :], [128, 4, 128]), op0=ALU.mult, op1=ALU.mult))
                    op(POOL, [B_xtm, B_dt], [B_xdt], lambda e, d_=d_, c=c: e.tensor_tensor(
                        out=xdt[:, 4 * d_:4 * d_ + 4, :], in0=xtm[:, c, :].rearrange("p (h f) -> p h f", h=4),
                        in1=bc(dtv[:, c, 4 * d_:4 * d_ + 4].unsqueeze(2), [128, 4, 64]), op=ALU.mult))
                for h in range(4):
                    mm_group(B_P2, P2[:, h * 64:(h + 1) * 64], [(MT[:, h, :], xdt[:, h, :]), (MT[:, 4 + h, :], xdt[:, 4 + h, :])],
                             [B_MT, B_xdt])
                mm_group(B_P3, P3[:, 0:256], [(xbcT[:, 3, cs], Hfb[:])], [B_xbcT[3], B_Hfb])
                mm_group(B_P3, P3[:, 256:512], [(xbcT[:, 3, cs], Hbb[:])], [B_xbcT[3], B_Hbb])
                mm_group(B_P2, P2[:, 256:512], [(btm[:, c, :], xE[:, 0:4, :].rearrange("p h f -> p (h f)"))], [B_btm, B_xE])
                op(DVE, [B_Hf, B_sm], [B_Htmp], lambda e: e.tensor_tensor(
                    out=Htmp[:].rearrange("p (h f) -> p h f", h=4), in0=Hf[:].rearrange("p (h f) -> p h f", h=4),
                    in1=bc(decs[:, 0:4].unsqueeze(2), [128, 4, 64]), op=ALU.mult))
                op(DVE, [B_Htmp, B_P2], [B_Hf], lambda e: e.tensor_add(out=Hf[:], in0=Htmp[:], in1=P2[:, 256:512]))
                op(ACT, [B_Hf], [B_Hfb], lambda e: e.activation(out=Hfb[:], in_=Hf[:], func=AF.Copy))
                v3 = lambda a: a.rearrange("p (h f) -> p h f", h=4)
                op(DVE, [B_P3, B_sm], [B_yt], lambda e: e.tensor_tensor(
                    out=v3(yt[:]), in0=v3(P3[:, 0:256]), in1=bc(dI[:, 0:4].unsqueeze(2), [128, 4, 64]), op=ALU.mult))
                op(DVE, [B_P3, B_sm], [B_yt], lambda e: e.tensor_tensor(
                    out=v3(yt2[:]), in0=v3(P3[:, 256:512]), in1=bc(dI[:, 4:8].unsqueeze(2), [128, 4, 64]), op=ALU.mult))
                op(DVE, [B_yt], [B_yt], lambda e: e.tensor_add(out=yt[:], in0=yt[:], in1=yt2[:]))
                op(DVE, [B_xtm, B_c], [B_yt], lambda e, c=c: e.tensor_tensor(
                    out=v3(yt2[:]), in0=v3(xtm[:, c, :]), in1=bc(dsk[:].unsqueeze(2), [128, 4, 64]), op=ALU.mult))
                op(DVE, [B_yt], [B_yt], lambda e: e.tensor_add(out=yt[:], in0=yt[:], in1=yt2[:]))
                op(DVE, [B_yt, B_P2], [B_yt], lambda e: e.tensor_add(out=yt[:], in0=yt[:], in1=P2[:, 0:256]))
                op(DVE, [B_yt, B_ztm], [B_mixtm], lambda e, c=c: e.tensor_mul(out=mixtm[:, c, 256:512], in0=yt[:], in1=ztm[:, c, :]))
                for h in range(2):
                    mm_group(B_P7, P7[:, 128 + h * 128:256 + h * 128], [(kT[:, h, cs], qT[:, h, cs])], [B_kT, B_qT])
                for h in range(2):
                    op(DVE, [B_P7, B_k], [B_SmT], lambda e, h=h: e.tensor_tensor(
                        out=SmT[:, h, :], in0=P7[:, 128 + h * 128:256 + h * 128], in1=maskT[:, h, :], op=ALU.mult))
                for h in range(2):
                    hs = slice(h * 128, (h + 1) * 128)
                    mm_group(B_P0, P0[:, hs], [(SmT[:, h, :], vtm[:, c, hs]), (qfT[:, h, cs], rfb[:, h, :]), (qbT[:, h, cs], rbb[:, h, :])],
                             [B_SmT, B_vtm, B_qfb, B_rfb, B_rbb])
                    mm_group(B_P0, P0[:, 256 + h * 128:384 + h * 128], [(ktm[:, c, h, :], vtm[:, c, hs])], [B_ktm, B_vtm])
                for h in range(2):
                    op(DVE, [B_rf, B_P0, B_k], [B_rf], lambda e, h=h: e.scalar_tensor_tensor(
                        out=rf[:, h, :], in0=rf[:, h, :], scalar=decf[:, h:h + 1], in1=P0[:, 256 + h * 128:384 + h * 128],
                        op0=ALU.mult, op1=ALU.add))
                op(ACT, [B_rf], [B_rfb], lambda e: e.activation(out=rfb[:], in_=rf[:], func=AF.Copy))
                op(ACT, [B_P0], [B_junk], lambda e: e.activation(out=junk[:], in_=P0[:, 0:256], func=AF.Square))
                op(DVE, [B_junk], [B_ssq], lambda e: e.reduce_sum(out=ssq[:], in_=junk[:].rearrange("p (h f) -> p h f", h=2), axis=mybir.AxisListType.X))
                op(DVE, [B_ssq], [B_ssq], lambda e: e.tensor_scalar(out=rs2[:], in0=ssq[:], scalar1=1.0 / 128.0, scalar2=EPS,
                                                                    op0=ALU.mult, op1=ALU.add))
                op(ACT, [B_ssq], [B_ssq], lambda e: e.activation(out=rs2[:], in_=rs2[:], func=AF.Ln))
                op(ACT, [B_ssq], [B_ssq], lambda e: e.activation(out=rs2[:], in_=rs2[:], func=AF.Exp, scale=-0.5))
                for h in range(2):
                    hs = slice(h * 128, (h + 1) * 128)
                    op(DVE, [B_P0, B_ssq, B_c], [B_rtmp], lambda e, h=h, hs=hs: e.scalar_tensor_tensor(
                        out=rtmp[:, hs], in0=P0[:, hs], scalar=rs2[:, h:h + 1], in1=rnw[:, hs], op0=ALU.mult, op1=ALU.mult))
                op(DVE, [B_rtmp, B_gtm], [B_mixtm], lambda e, c=c: e.tensor_mul(out=mixtm[:, c, 0:256], in0=rtmp[:], in1=gtm[:, c, :]))
            if pas == 2:
                for half in range(2):
                    for c in range(4):
                        for fb in range(2):
                            f = half * 2 + fb
                            op(PE, [B_mixtm, B_k], [B_P4], lambda e, c=c, f=f, fb=fb: e.transpose(
                                P4b[:, fb * 512 + c * 128:fb * 512 + (c + 1) * 128], mixtm[:, c, f * 128:(f + 1) * 128], ident[:]))
                    op(ACT, [B_P4], [B_mixT], lambda e, half=half: e.activation(
                        out=mixT[:, 2 * half:2 * half + 2, :].rearrange("p a t -> p (a t)"), in_=P4b[:, 0:1024], func=AF.Copy))
                seg, tq = t // 4, t % 4
                def arows(sg, lo, n):
                    return agin.ap()[sg * 512:(sg + 1) * 512, lo:lo + n].rearrange("(b p) c -> p b c", p=128)
                dma(SP, B_agin, B_mixT, arows(seg, 1 + 512 * tq, 512), mixT[:, :, :])
                if tq == 0:
                    if seg > 0:
                        dma(SP, B_agin, B_mixT, arows(seg - 1, 2049, 1), mixT[:, :, 0:1], slow=True)
                    else:
                        dma(SP, B_agin, B_zpad, arows(0, 0, 1), zpad[:, :, :], slow=True)
                if tq == 3:
                    if seg < 3:
                        dma(SP, B_agin, B_mixT, arows(seg + 1, 0, 1), mixT[:, :, 511:512], slow=True)
                    else:
                        dma(SP, B_agin, B_zpad, arows(3, 2049, 1), zpad[:, :, :], slow=True)

        try:
            chk(0)
            for t in reversed(range(ntiles)):
                tile_body(t, 1)
            chk(10)
            for t in range(ntiles):
                tile_body(t, 2)
        except StopBuild:
            pass

        if debug_half1:
            if ntiles >= 16:
                dma(SP, Buf("dbgout"), B_agin, mix_dbg[:, :], agin.ap()[:, :], is_out=True)
            else:
                dma(SP, Buf("dbgout"), B_agin, mix_dbg[0:512, 0:1 + 512 * ntiles], agin.ap()[0:512, 0:1 + 512 * ntiles], is_out=True)
        C.fence()
    C.es = C.es_outer

    if run_half2:
        if run_half1:
            cc_sem = C.new_sem("cc")
            for k in range(16):
                POOL.e.collective_compute("AllGather", ALU.bypass, replica_groups=[[0, 1, 2, 3], [4, 5, 6, 7]],
                                          ins=[agin.ap()[k * 128:(k + 1) * 128, :].opt()],
                                          outs=[agout.ap()[k * 512:(k + 1) * 512, :].opt()]).then_inc(cc_sem)
            tok = ("D", cc_sem, "cc", 16)
            B_agout.w = tok
            B_agout.dtok = tok
        h2 = contextlib.ExitStack()
        C.es = h2
        with h2:
            _half2(nc, C, dict(locals()))
        C.es = C.es_outer

    for tok in C.out_toks:
        SP.wait(tok)


def _half2(nc, C, V):
    PE, ACT, DVE, POOL, SP = C.PE, C.ACT, C.DVE, C.POOL, C.SP
    ACTQ = POOL
    op, dma, sb, ps = C.op, C.dma, C.sb, C.ps
    B_in, B_agout, agout, outT = V["B_in"], V["B_agout"], V["agout"], V["outT"]
    NC2 = 2050
    CT = [(0, 512), (512, 512), (1024, 512), (1536, 512), (2048, 2)]
    h1_scr = nc.dram_tensor("h1_scr", [16, 128, NC2], F32)
    h2_scr = nc.dram_tensor("h2_scr", [16, 128, 2048], F32)
    B_h1s = Buf("h1s"); B_h2s = Buf("h2s"); B_out = Buf("outb")
    wg_c = nc.dram_tensor("wg_c", [NFB, 128, 2048], BF16); wu_c = nc.dram_tensor("wu_c", [NFB, 128, 2048], BF16)
    wd_c = nc.dram_tensor("wd_c", [16, 128, NFB * 128], BF16)
    B_wgc = Buf("wgc"); B_wuc = Buf("wuc"); B_wdc = Buf("wdc")

    def mm_group(out_buf, out_ap, pairs, reads):
        n = len(pairs)
        for i, (l_, r_) in enumerate(pairs):
            op(PE, reads, [out_buf], lambda e, l_=l_, r_=r_, st=(i == 0), sp_=(i == n - 1): e.matmul(
                out_ap, l_, r_, start=st, stop=sp_), mark=(i == n - 1), wait=(i == 0))

    B_c = Buf("c2")
    def small(name, src, shape, dt=F32):
        t = sb(name, shape, dt)
        dma(SP, B_c, B_in, t[:], src)
        return t
    gidx = small("gidx", V["idx_d"][:, :], [128, 16], I32)
    snw = small("snw", V["snw_d"][:, :], [128, 16]); nfw = small("nfw", V["nfw_d"][:, :], [128, 16])
    pnw = small("pnw", V["pnw_d"][:, :], [128, 16]); fnw = small("fnw", V["fnw_d"][:, :], [128, 16])
    bpg = small("bpg", V["bpg_d"][:, :], [128, 16]); fcb = small("fcb", V["fcb_d"][:, :], [128, NFB])
    fcw = small("fcw", V["fcw_d"][:, :, :], [128, NFB, 3])
    B_k = Buf("k2")
    onesf = sb("onesf2", [128, 128], F32); onesb = sb("onesb2", [128, 128], BF16); epsc = sb("epsc2", [128, 1], F32)
    op(POOL, [], [B_k], lambda e: e.memset(onesf[:], 1.0))
    op(POOL, [], [B_k], lambda e: e.memset(onesb[:], 1.0))
    op(POOL, [], [B_k], lambda e: e.memset(epsc[:], EPS))
    pTb = sb("pTb", [128, 2, 2048], BF16); B_pT = Buf("pT")
    dma(POOL, B_pT, B_in, pTb[:], V["pT_d"][:, :, :])

    Q = [ps(f"Q{i}", [128, 512], F32) for i in range(8)]
    B_Q = [Buf(f"Q{i}", True) for i in range(8)]

    sacc = sb("sacc", [128, NC2], F32); B_sacc = Buf("sacc")
    rstd2 = sb("rstd2", [128, NC2], F32); B_rstd2 = Buf("rstd2")
    sqt = sb("sqt", [128, 512], F32); B_sqt = Buf("sqt")

    def rsqrt_cols(src_buf, src_ap, dst_buf, dst_ap, n, scale):
        mm_group(B_Q[6], Q[6][:, 0:n], [(onesf[:], src_ap)], [src_buf, B_k])
        op(ACT, [B_Q[6], B_k], [dst_buf], lambda e: e.activation(out=dst_ap, in_=Q[6][:, 0:n], func=AF.Ln, scale=scale, bias=epsc[:, 0:1]))
        op(ACT, [dst_buf], [dst_buf], lambda e: e.activation(out=dst_ap, in_=dst_ap, func=AF.Exp, scale=-0.5))

    pc = contextlib.ExitStack()
    C.es = pc
    with pc:
        mixg = sb("mixg", [128, 16, NC2], BF16); B_mixg = Buf("mixg")
        for kk in range(16):
            POOL.wait(B_agout.w)
            POOL.wait(B_c.w)
            if B_mixg.dsem is None:
                B_mixg.dsem = C.new_sem("d_mixg")
            B_mixg.dcnt += 1
            POOL.e.indirect_dma_start(out=mixg[:, kk, :], out_offset=None, in_=agout.ap()[:, :],
                                      in_offset=bass.IndirectOffsetOnAxis(ap=gidx[:, kk:kk + 1], axis=0)).then_inc(B_mixg.dsem, 16)
            tok = ("D", B_mixg.dsem, B_mixg.name, 16 * B_mixg.dcnt)
            B_mixg.w = tok; B_mixg.dtok = tok
        sqb = sb("sqb", [128, 4, 512], F32); B_sqb = Buf("sqb")
        rg = sb("rg", [128, 512], F32); B_rg = Buf("rg")
        for g_ in range(2):
            blks = [8 * g_ + 2, 8 * g_ + 3, 8 * g_ + 6, 8 * g_ + 7]
            for (c0, n) in CT:
                for bi, kk in enumerate(blks):
                    op(ACT, [B_mixg], [B_sqb], lambda e, bi=bi, kk=kk: e.activation(out=sqb[:, bi, 0:n], in_=mixg[:, kk, c0:c0 + n], func=AF.Square))
                mm_group(B_Q[6], Q[6][:, 0:n], [(onesf[:], sqb[:, bi, 0:n]) for bi in range(4)], [B_sqb, B_k])
                op(ACT, [B_Q[6], B_k], [B_rg], lambda e: e.activation(out=rg[:, 0:n], in_=Q[6][:, 0:n], func=AF.Ln, scale=1.0 / 512.0, bias=epsc[:, 0:1]))
                op(ACT, [B_rg], [B_rg], lambda e: e.activation(out=rg[:, 0:n], in_=rg[:, 0:n], func=AF.Exp, scale=-0.5))
                for kk in blks:
                    op(DVE, [B_rg, B_c], [B_mixg], lambda e, kk=kk: e.scalar_tensor_tensor(
                        out=mixg[:, kk, c0:c0 + n], in0=mixg[:, kk, c0:c0 + n], scalar=snw[:, kk:kk + 1], in1=rg[:, 0:n], op0=ALU.mult, op1=ALU.mult))
        wo = [sb(f"wo{i}", [128, 16, 128], BF16) for i in range(2)]; B_wo = [Buf(f"wo{i}") for i in range(2)]
        xrow = [sb(f"xrow{i}", [128, NC2], F32) for i in range(2)]; B_xrow = [Buf(f"xrow{i}") for i in range(2)]
        op(POOL, [], [B_sacc], lambda e: e.memset(sacc[:], 0.0))
        for m in range(16):
            s_ = m % 2
            dma(POOL, B_wo[s_], B_in, wo[s_][:].rearrange("p k n -> p (k n)"), V["wout_d"][:, m, :])
            dma(SP, B_xrow[s_], B_in, xrow[s_][:], V["xw_d"][:, m, :])
            for ti, (c0, n) in enumerate(CT):
                qb = 4 + (ti % 2)
                mm_group(B_Q[qb], Q[qb][:, 0:n], [(wo[s_][:, k, :], mixg[:, k, c0:c0 + n]) for k in range(16)], [B_wo[s_], B_mixg])
                op(DVE, [B_Q[qb]], [B_xrow[s_]], lambda e, qb=qb, c0=c0, n=n: e.tensor_add(out=xrow[s_][:, c0:c0 + n], in0=xrow[s_][:, c0:c0 + n], in1=Q[qb][:, 0:n]))
                op(POOL, [B_xrow[s_]], [B_sqt], lambda e, c0=c0, n=n: e.tensor_tensor(out=sqt[:, 0:n], in0=xrow[s_][:, c0:c0 + n], in1=xrow[s_][:, c0:c0 + n], op=ALU.mult))
                op(POOL, [B_sqt], [B_sacc], lambda e, c0=c0, n=n: e.tensor_add(out=sacc[:, c0:c0 + n], in0=sacc[:, c0:c0 + n], in1=sqt[:, 0:n]))
            dma(SP, B_h1s, B_xrow[s_], h1_scr.ap()[m, :, :], xrow[s_][:])
        for (c0, n) in CT:
            rsqrt_cols(B_sacc, sacc[:, c0:c0 + n], B_rstd2, rstd2[:, c0:c0 + n], n, 1.0 / D)
        C.fence()
    C.es = h2_es = V["h2"]

    hst = [sb(f"hst{i}", [128, 4, 514], F32) for i in range(2)]; B_hst = [Buf(f"hst{i}") for i in range(2)]
    hnq = sb("hnq", [128, 16, 514], BF16); B_hnq = Buf("hnq")
    act = sb("act", [128, NFB, 512], BF16); B_act = Buf("act")
    wgb = [sb(f"wgb{i}", [128, 16, 128], BF16) for i in range(2)]; B_wg = [Buf(f"wg{i}") for i in range(2)]
    wub = [sb(f"wub{i}", [128, 16, 128], BF16) for i in range(2)]; B_wu = [Buf(f"wu{i}") for i in range(2)]
    wdb = [sb(f"wdb{i}", [128, NFB, 128], BF16) for i in range(2)]; B_wd = [Buf(f"wd{i}") for i in range(2)]
    wpb = [sb(f"wpb{i}", [128, 2, 128], BF16) for i in range(2)]; B_wp = [Buf(f"wp{i}") for i in range(2)]
    graw = sb("graw", [128, 514], F32); B_graw = Buf("graw")
    gacc = sb("gacc", [128, 512], F32); B_gacc = Buf("gacc")
    hrow = [sb(f"hrow{i}", [128, 512], F32) for i in range(2)]; B_hrow = [Buf(f"hrow{i}") for i in range(2)]
    sacc2 = sb("sacc2", [128, 512], F32); B_sacc2 = Buf("sacc2")
    rstdq = sb("rstdq", [128, 512], F32); B_rstdq = Buf("rstdq")
    h3q = sb("h3q", [128, 16, 512], F32); B_h3q = Buf("h3q")
    sgt = sb("sgt", [128, 512], F32); B_sgt = Buf("sgt")

    for q in range(4):
        w0 = 512 * q
        for s4 in range(4):
            b_ = s4 % 2
            dma(SP, B_hst[b_], B_h1s, hst[b_][:], h1_scr.ap()[4 * s4:4 * s4 + 4, :, w0:w0 + 514].rearrange("k p c -> p k c"))
            for kk in range(4):
                k = 4 * s4 + kk
                op(DVE, [B_hst[b_], B_rstd2, B_c], [B_hnq], lambda e, b_=b_, kk=kk, k=k: e.scalar_tensor_tensor(
                    out=hnq[:, k, :], in0=hst[b_][:, kk, :], scalar=nfw[:, k:k + 1], in1=rstd2[:, w0:w0 + 514], op0=ALU.mult, op1=ALU.mult))
        for f in range(NFB):
            s_ = f % 2
            if q == 0:
                dma(POOL, B_wg[s_], B_in, wgb[s_][:].rearrange("p k n -> p (k n)"), V["wg_d"][:, f, :])
                dma(POOL, B_wu[s_], B_in, wub[s_][:].rearrange("p k n -> p (k n)"), V["wu_d"][:, f, :])
                dma(SP, B_wgc, B_wg[s_], wg_c.ap()[f, :, :], wgb[s_][:].rearrange("p k n -> p (k n)"))
                dma(SP, B_wuc, B_wu[s_], wu_c.ap()[f, :, :], wub[s_][:].rearrange("p k n -> p (k n)"))
            else:
                dma(SP, B_wg[s_], B_wgc, wgb[s_][:].rearrange("p k n -> p (k n)"), wg_c.ap()[f, :, :])
                dma(ACTQ, B_wu[s_], B_wuc, wub[s_][:].rearrange("p k n -> p (k n)"), wu_c.ap()[f, :, :])
            mm_group(B_Q[s_], Q[s_][:, 0:512], [(wgb[s_][:, k, :], hnq[:, k, 0:512]) for k in range(16)], [B_wg[s_], B_hnq])
            mm_group(B_Q[6], Q[6][:, 0:32], [(wgb[s_][:, k, :], hnq[:, k, 482:514]) for k in range(16)], [B_wg[s_], B_hnq])
            mm_group(B_Q[2 + s_], Q[2 + s_][:, 0:512], [(wub[s_][:, k, :], hnq[:, k, 1:513]) for k in range(16)], [B_wu[s_], B_hnq])
            op(ACT, [B_Q[s_]], [B_graw], lambda e, s_=s_: e.activation(out=graw[:, 0:512], in_=Q[s_][:, 0:512], func=AF.Copy))
            op(ACT, [B_Q[6]], [B_graw], lambda e: e.activation(out=graw[:, 512:514], in_=Q[6][:, 30:32], func=AF.Copy))
            op(DVE, [B_graw, B_c], [B_gacc], lambda e, f=f: e.tensor_scalar_mul(out=gacc[:], in0=graw[:, 0:512], scalar1=fcw[:, f, 0:1]))
            for j in (1, 2):
                op(DVE, [B_graw, B_c], [B_gacc], lambda e, f=f, j=j: e.scalar_tensor_tensor(
                    out=gacc[:], in0=graw[:, j:j + 512], scalar=fcw[:, f, j:j + 1], in1=gacc[:], op0=ALU.mult, op1=ALU.add))
            op(ACT, [B_gacc, B_c], [B_gacc], lambda e, f=f: e.activation(out=gacc[:], in_=gacc[:], func=AF.Gelu_apprx_tanh, bias=fcb[:, f:f + 1], scale=1.0))
            op(DVE, [B_gacc, B_Q[2 + s_]], [B_act], lambda e, f=f, s_=s_: e.tensor_tensor(out=act[:, f, :], in0=gacc[:], in1=Q[2 + s_][:, 0:512], op=ALU.mult))
        op(POOL, [], [B_sacc2], lambda e: e.memset(sacc2[:], 0.0))
        for m in range(16):
            s_ = m % 2
            if q == 0:
                dma(POOL, B_wd[s_], B_in, wdb[s_][:].rearrange("p k n -> p (k n)"), V["wd_d"][:, m, :])
                dma(SP, B_wdc, B_wd[s_], wd_c.ap()[m, :, :], wdb[s_][:].rearrange("p k n -> p (k n)"))
            else:
                dma(ACTQ, B_wd[s_], B_wdc, wdb[s_][:].rearrange("p k n -> p (k n)"), wd_c.ap()[m, :, :])
            dma(SP, B_hrow[s_], B_h1s, hrow[s_][:], h1_scr.ap()[m, :, w0 + 1:w0 + 513])
            mm_group(B_Q[4 + s_], Q[4 + s_][:, 0:512], [(wdb[s_][:, f, :], act[:, f, :]) for f in range(NFB)], [B_wd[s_], B_act])
            op(DVE, [B_Q[4 + s_]], [B_hrow[s_]], lambda e, s_=s_: e.tensor_add(out=hrow[s_][:], in0=hrow[s_][:], in1=Q[4 + s_][:, 0:512]))
            op(POOL, [B_hrow[s_]], [B_sqt], lambda e, s_=s_: e.tensor_tensor(out=sqt[:], in0=hrow[s_][:], in1=hrow[s_][:], op=ALU.mult))
            op(POOL, [B_sqt], [B_sacc2], lambda e: e.tensor_add(out=sacc2[:], in0=sacc2[:], in1=sqt[:]))
            dma(SP, B_h2s, B_hrow[s_], h2_scr.ap()[m, :, w0:w0 + 512], hrow[s_][:])
        rsqrt_cols(B_sacc2, sacc2[:], B_rstdq, rstdq[:], 512, 1.0 / D)
        for s4 in range(4):
            b_ = s4 % 2
            dma(SP, B_hst[b_], B_h2s, hst[b_][:, :, 0:512], h2_scr.ap()[4 * s4:4 * s4 + 4, :, w0:w0 + 512].rearrange("k p c -> p k c"))
            for kk in range(4):
                k = 4 * s4 + kk
                op(DVE, [B_hst[b_], B_rstdq, B_c], [B_hnq], lambda e, b_=b_, kk=kk, k=k: e.scalar_tensor_tensor(
                    out=hnq[:, k, 0:512], in0=hst[b_][:, kk, 0:512], scalar=pnw[:, k:k + 1], in1=rstdq[:], op0=ALU.mult, op1=ALU.mult))
                op(ACT, [B_hst[b_]], [B_h3q], lambda e, b_=b_, kk=kk, k=k: e.activation(out=h3q[:, k, :], in_=hst[b_][:, kk, 0:512], func=AF.Copy))
        op(POOL, [], [B_sacc2], lambda e: e.memset(sacc2[:], 0.0))
        for m in range(16):
            s_ = m % 2
            dma(POOL, B_wg[s_], B_in, wgb[s_][:].rearrange("p k n -> p (k n)"), V["wpg_d"][:, m, :])
            dma(POOL, B_wp[s_], B_in, wpb[s_][:].rearrange("p k n -> p (k n)"), V["wpp_d"][:, m, :])
            mm_group(B_Q[s_], Q[s_][:, 0:512], [(wgb[s_][:, k, :], hnq[:, k, 0:512]) for k in range(16)], [B_wg[s_], B_hnq])
            mm_group(B_Q[2 + s_], Q[2 + s_][:, 0:512], [(wpb[s_][:, kk, :], pTb[:, kk, w0:w0 + 512]) for kk in range(2)], [B_wp[s_], B_pT])
            op(ACT, [B_Q[s_], B_c], [B_sgt], lambda e, s_=s_, m=m: e.activation(out=sgt[:], in_=Q[s_][:, 0:512], func=AF.Sigmoid, bias=bpg[:, m:m + 1], scale=1.0))
            op(DVE, [B_sgt, B_Q[2 + s_]], [B_sgt], lambda e, s_=s_: e.tensor_tensor(out=sgt[:], in0=sgt[:], in1=Q[2 + s_][:, 0:512], op=ALU.mult))
            op(DVE, [B_sgt], [B_h3q], lambda e, m=m: e.tensor_add(out=h3q[:, m, :], in0=h3q[:, m, :], in1=sgt[:]))
            op(POOL, [B_h3q], [B_sqt], lambda e, m=m: e.tensor_tensor(out=sqt[:], in0=h3q[:, m, :], in1=h3q[:, m, :], op=ALU.mult))
            op(POOL, [B_sqt], [B_sacc2], lambda e: e.tensor_add(out=sacc2[:], in0=sacc2[:], in1=sqt[:]))
        rsqrt_cols(B_sacc2, sacc2[:], B_rstdq, rstdq[:], 512, 1.0 / D)
        for m in range(16):
            op(DVE, [B_rstdq, B_c], [B_h3q], lambda e, m=m: e.scalar_tensor_tensor(
                out=h3q[:, m, :], in0=h3q[:, m, :], scalar=fnw[:, m:m + 1], in1=rstdq[:], op0=ALU.mult, op1=ALU.mult))
        for m4 in range(0, 16, 4):
            dma(SP, B_out, B_h3q, outT[:, m4:m4 + 4, w0:w0 + 512], h3q[:, m4:m4 + 4, :], is_out=True)


def _pk(a):
    kp, n = a.shape
    return np.ascontiguousarray(a.reshape(kp // 128, 128, n).transpose(1, 0, 2))


def _col(v):
    return np.ascontiguousarray(v.reshape(-1, 128).T)


def prep_half1(inp, b, j):
    f32 = np.float32
    x = np.asarray(inp["x"], f32)
    xT = np.zeros((D, L + 4), f32)
    xT[:, 2:L + 2] = x[b].T
    w_in = np.asarray(inp["w_in"], f32)[0]
    OFF_Q, OFF_K, OFF_V, OFF_G, OFF_Z, OFF_XBC, OFF_DT = 0, 1024, 2048, 3072, 4096, 5120, 6656
    g = j // 2
    rh = [2 * j, 2 * j + 1]
    sh = [4 * j + i for i in range(4)]
    def hc(off, h, w=128):
        return list(range(off + h * w, off + (h + 1) * w))
    fm_cols = hc(OFF_Q, rh[0]) + hc(OFF_Q, rh[1]) + hc(OFF_K, rh[0]) + hc(OFF_K, rh[1])
    xcols = []
    for h in sh:
        xcols += hc(OFF_XBC, h, 64)
    bcols = list(range(OFF_XBC + 1024 + g * 128, OFF_XBC + 1024 + (g + 1) * 128))
    ccols = list(range(OFF_XBC + 1024 + 256 + g * 128, OFF_XBC + 1024 + 256 + (g + 1) * 128))
    fm_cols += xcols + bcols + ccols
    zcols = []
    for h in sh:
        zcols += hc(OFF_Z, h, 64)
    dtcols = [OFF_DT + h for h in sh] + [OFF_DT + 16 + h for h in sh]
    tm_cols = hc(OFF_V, rh[0]) + hc(OFF_V, rh[1]) + hc(OFF_G, rh[0]) + hc(OFF_G, rh[1]) + dtcols + zcols
    convc = np.array(xcols + bcols + ccols) - OFF_XBC
    cw = np.asarray(inp["ssd_conv_w"], f32)[0][:, convc]
    cbv = np.asarray(inp["ssd_conv_b"], f32)[0][convc]
    m = {
        "xTp": _pk(xT),
        "pos": np.ascontiguousarray(np.asarray(inp["positions"]).astype(np.int32)[b][None, :]),
        "wfm": _pk(w_in[:, fm_cols]),
        "wtm": _pk(w_in[:, tm_cols]),
        "nmw": _col(np.asarray(inp["norm_mix_w"], f32)[0]),
        "cw": np.ascontiguousarray(cw.T.reshape(4, 128, 5).transpose(1, 0, 2)),
        "cb": np.ascontiguousarray(cbv.reshape(4, 128).T),
        "dtb": np.concatenate([np.asarray(inp["ssd_dt_bias"], f32)[0, 0, sh], np.asarray(inp["ssd_dt_bias"], f32)[0, 1, sh]])[None, :],
        "alog": np.concatenate([np.asarray(inp["ssd_a_log"], f32)[0, 0, sh], np.asarray(inp["ssd_a_log"], f32)[0, 1, sh]])[None, :],
        "dsk": np.asarray(inp["ssd_d"], f32)[0, sh][None, :],
        "rnw": np.asarray(inp["ret_norm_w"], f32)[0, rh[0] * 128:(rh[1] + 1) * 128][None, :],
        "hh": np.array([rh], f32),
    }
    return {k: np.ascontiguousarray(v) for k, v in m.items()}


def _blk(a, nb):
    K, N = a.shape
    kk = K // 128
    return np.ascontiguousarray(a.reshape(kk, 128, nb, N // nb).transpose(1, 2, 0, 3).reshape(128, nb, kk * (N // nb)))


def gathered_perm():
    perm = []
    for i in range(4):
        perm += list(range(256 * i, 256 * i + 256))
        perm += list(range(1024 + 256 * i, 1024 + 256 * i + 256))
    return np.array(perm)


def prep_half2(inp, b, j):
    f32 = np.float32
    x = np.asarray(inp["x"], f32)
    xT = np.zeros((D, L + 4), f32)
    xT[:, 2:L + 2] = x[b].T
    perm = gathered_perm()
    snw_full = np.zeros(2048, f32)
    snw_full[1024:] = np.asarray(inp["ssd_norm_w"], f32)[0]
    idx = np.zeros((128, 16), np.int32)
    for i in range(4):
        for bl in range(4):
            idx[:, i * 4 + bl] = ((j * 4 + bl) * 4 + i) * 128 + np.arange(128)
    m = {
        "xw": _pk(xT[:, 2048 * j + 1:2048 * j + 1 + 2050]),
        "pT": _pk(np.asarray(inp["p"], f32)[0, b, 2048 * j:2048 * j + 2048, :].T),
        "gidx": idx,
        "wout": _blk(np.asarray(inp["w_out"], f32)[0][perm, :], 16),
        "snw": _col(snw_full[perm]),
        "nfw": _col(np.asarray(inp["norm_ffn_w"], f32)[0]),
        "wg": _blk(np.asarray(inp["ffn_w_gate"], f32)[0], NFB),
        "wu": _blk(np.asarray(inp["ffn_w_up"], f32)[0], NFB),
        "fcw": np.ascontiguousarray(np.asarray(inp["ffn_conv_w"], f32)[0].T.reshape(NFB, 128, 3).transpose(1, 0, 2)),
        "fcb": _col(np.asarray(inp["ffn_conv_b"], f32)[0]),
        "wd": _blk(np.asarray(inp["ffn_w_down"], f32)[0], 16),
        "pnw": _col(np.asarray(inp["ple_norm_w"], f32)[0]),
        "wpg": _blk(np.asarray(inp["ple_w_gate"], f32)[0], 16),
        "bpg": _col(np.asarray(inp["ple_b_gate"], f32)[0]),
        "wpp": _blk(np.asarray(inp["ple_w_proj"], f32)[0], 16),
        "fnw": _col(np.asarray(inp["final_norm_w"], f32)),
    }
    return {k: np.ascontiguousarray(v) for k, v in m.items()}


_NC_CACHE = {}


def kernel(**inputs):
    if "nc" not in _NC_CACHE:
        _NC_CACHE["nc"] = build_nc()
    nc = _NC_CACHE["nc"]
    in_maps = []
    for c in range(8):
        b, j = c // 4, c % 4
        m = prep_half1(inputs, b, j)
        m.update(prep_half2(inputs, b, j))
        in_maps.append(m)
    res = run_bass_kernel_spmd(nc, in_maps, core_ids=list(range(8)))
    out = np.zeros((2, L, D), np.float32)
    for c in range(8):
        b, j = c // 4, c % 4
        oT = np.asarray(res.results[c]["outT"])
        out[b, 2048 * j:2048 * j + 2048, :] = oT.transpose(2, 1, 0).reshape(2048, D)
    return out
```
